# Optimizing a Trainium2 kernel written in Bass

```python
import jax, jax.numpy as jnp
from jax import lax
import numpy as np

D_MODEL = 1024
BATCH = 4
SEQ = 4096
DEPTH = 1
DEC_BATCH = 32
DEC_SEQ = 8
PAST_LEN = 16384
PAGE_SIZE = 128

D_PLE = 256
NSA_HEADS = 8
NSA_GROUPS = 2
HD = 64
HPG = NSA_HEADS // NSA_GROUPS
W_A = NSA_HEADS * HD
KV_W = NSA_GROUPS * HD
L_CMP = 32
D_CMP = 16
CMP_HID = 128
L_SEL = 64
N_SEL = 16
WIN = 512
Q_BLOCK = 128
FORCE_SCORE = 1e4
NEG_INF = -1e30
R_HEADS = 8
R_HD = 64
W_R = R_HEADS * R_HD
R_W = 64
R_A = 64
C_R = 3 * W_R + R_W + R_A
GN_EPS = 64e-5
ROPE_THETA = 10000.0
NORM_EPS = 1e-6
SPLIT_SIZES = (W_A, 6 * KV_W, 3 * NSA_HEADS, W_A, C_R, W_R, 2 * D_MODEL)
C_IN = sum(SPLIT_SIZES)
SPLIT_OFFSETS = tuple(int(v) for v in np.cumsum(SPLIT_SIZES)[:-1])

kernel_name = 'nsa_rwkv7_hybrid_step'


def rms_norm(x, g):
    xf = x.astype(jnp.float32)
    y = xf * lax.rsqrt(jnp.mean(xf * xf, axis=-1, keepdims=True) + NORM_EPS)
    return (y * g.astype(jnp.float32)).astype(x.dtype)


def rope(x, pos):
    half = HD // 2
    inv = ROPE_THETA ** (-jnp.arange(half, dtype=jnp.float32) / half)
    ang = pos.astype(jnp.float32)[:, None] * inv[None, :]
    shape = (1, pos.shape[0]) + (1,) * (x.ndim - 3) + (half,)
    cos = jnp.cos(ang).reshape(shape)
    sin = jnp.sin(ang).reshape(shape)
    xf = x.astype(jnp.float32)
    x1, x2 = xf[..., :half], xf[..., half:]
    return jnp.concatenate([x1 * cos - x2 * sin, x2 * cos + x1 * sin], axis=-1).astype(x.dtype)


def masked_softmax(s, mask):
    s = jnp.where(mask, s.astype(jnp.float32), NEG_INF)
    return jax.nn.softmax(s, axis=-1) * mask


def compress_rows(raw, pe, w1, b1, w2, b2, n_blocks):
    b, t = raw.shape[:2]
    n_chunk = -(-t // D_CMP)
    raw = jnp.pad(raw, ((0, 0), (0, n_chunk * D_CMP - t), (0, 0), (0, 0)))
    chunks = raw.reshape(b, n_chunk, D_CMP, NSA_GROUPS, HD)
    lo = jnp.einsum('bcjgd,jdf->bcgf', chunks, w1[:D_CMP])
    hi = jnp.einsum('bcjgd,jdf->bcgf', chunks, w1[D_CMP:])
    pos_term = jnp.einsum('jd,jdf->f', pe, w1)
    hid = (lo[:, :-1] + hi[:, 1:])[:, :n_blocks] + pos_term + b1
    return jnp.einsum('bcgf,fd->bcgd', jax.nn.silu(hid), w2) + b2


def nsa_context(cmp_kv, sel_kv, cmp_pe, cmp_w1, cmp_b1, cmp_w2, cmp_b2):
    b, t = cmp_kv.shape[:2]
    n_cmp = (t - L_CMP) // D_CMP + 1
    kc = compress_rows(cmp_kv[:, :, 0], cmp_pe[0], cmp_w1[0], cmp_b1[0], cmp_w2[0], cmp_b2[0], n_cmp)
    vc = compress_rows(cmp_kv[:, :, 1], cmp_pe[1], cmp_w1[1], cmp_b1[1], cmp_w2[1], cmp_b2[1], n_cmp)
    n_sb = -(-t // L_SEL)
    sel = jnp.pad(sel_kv, ((0, 0), (0, n_sb * L_SEL - t), (0, 0), (0, 0), (0, 0)))
    sel = sel.reshape(b, n_sb, L_SEL, 2, NSA_GROUPS, HD).transpose(3, 0, 4, 1, 2, 5)
    c_start = np.arange(n_cmp) * D_CMP
    s_start = np.arange(n_sb) * L_SEL
    overlap = (c_start[:, None] <= s_start[None, :] + L_SEL - 1) & (c_start[:, None] + L_CMP - 1 >= s_start[None, :])
    c_end = jnp.asarray(c_start + L_CMP - 1, dtype=jnp.int32)
    return kc, vc, c_end, sel[0], sel[1], jnp.asarray(overlap, dtype=jnp.float32)


def nsa_query_block(q, gates, qpos, kc, vc, c_end, ks, vs, overlap, kw, vw, kw_pos):
    b, nq = q.shape[:2]
    qg = q.reshape(b, nq, NSA_GROUPS, HPG, HD) * (HD ** -0.5)
    s_c = jnp.einsum('bqghd,bcgd->bqghc', qg, kc)
    p_c = masked_softmax(s_c, (c_end[None, :] <= qpos[:, None])[None, :, None, None, :])
    o_c = jnp.einsum('bqghc,bcgd->bqghd', p_c, vc)
    n_sb = ks.shape[2]
    n_sel = min(N_SEL, n_sb)
    imp = jnp.einsum('bqgc,cn->bqgn', p_c.sum(axis=3), overlap)
    blk = jnp.arange(n_sb)
    cur = (qpos // L_SEL)[:, None]
    forced = (blk[None, :] == 0) | (blk[None, :] == cur) | (blk[None, :] == cur - 1)
    future = blk[None, :] * L_SEL > qpos[:, None]
    imp = jnp.where(forced[None, :, None, :], FORCE_SCORE, imp)
    imp = jnp.where(future[None, :, None, :], -1.0, imp)
    _, idx = lax.top_k(imp, n_sel)
    bi = jnp.arange(b)[:, None, None, None]
    gi = jnp.arange(NSA_GROUPS)[None, None, :, None]
    k_sel = ks[bi, gi, idx]
    v_sel = vs[bi, gi, idx]
    sel_pos = idx[..., None] * L_SEL + jnp.arange(L_SEL)
    m_s = (sel_pos <= qpos[None, :, None, None, None]).reshape(b, nq, NSA_GROUPS, 1, n_sel * L_SEL)
    s_s = jnp.einsum('bqghd,bqgnld->bqghnl', qg, k_sel).reshape(b, nq, NSA_GROUPS, HPG, n_sel * L_SEL)
    p_s = masked_softmax(s_s, m_s).reshape(b, nq, NSA_GROUPS, HPG, n_sel, L_SEL)
    o_s = jnp.einsum('bqghnl,bqgnld->bqghd', p_s, v_sel)
    s_w = jnp.einsum('bqghd,bkgd->bqghk', qg, kw)
    dist = qpos[:, None] - kw_pos[None, :]
    m_w = (dist >= 0) & (dist < WIN) & (kw_pos[None, :] >= 0)
    p_w = masked_softmax(s_w, m_w[None, :, None, None, :])
    o_w = jnp.einsum('bqghk,bkgd->bqghd', p_w, vw)
    g = gates.reshape(b, nq, NSA_GROUPS, HPG, 3)
    o = g[..., 0:1] * o_c + g[..., 1:2] * o_s + g[..., 2:3] * o_w
    return o.reshape(b, nq, W_A).astype(q.dtype)


def rwkv_time_mix(rz, shift0, state0, mu, w0, w_up, a0, a_up, k_k, k_a, r_k, gn_g, gn_b):
    b, t = rz.shape[:2]
    z = rz.astype(jnp.float32)
    prev = jnp.concatenate([shift0.astype(jnp.float32)[:, None], z[:, :-1]], axis=1)
    zs = z + (prev - z) * mu
    r, k, v, wd, ad = jnp.split(zs, (W_R, 2 * W_R, 3 * W_R, 3 * W_R + R_W), axis=-1)
    w = -jax.nn.softplus(-(w0 + jnp.tanh(wd) @ w_up)) - 0.5
    decay = jnp.exp(-jnp.exp(w))
    a = jax.nn.sigmoid(a0 + ad @ a_up)
    heads = lambda u: u.reshape(b, t, R_HEADS, R_HD)
    kk = heads(k * k_k)
    kk = kk / jnp.maximum(jnp.linalg.norm(kk, axis=-1, keepdims=True), 1e-12)
    k = k * (1.0 + (a - 1.0) * k_a)
    r, k, v, decay, a = map(heads, (r, k, v, decay, a))

    def step(s, inp):
        r_t, d_t, k_t, v_t, kk_t, a_t = inp
        sa = jnp.einsum('bhvk,bhk->bhv', s, -kk_t)
        s = s * d_t[:, :, None, :] + sa[..., None] * (kk_t * a_t)[:, :, None, :] + v_t[..., None] * k_t[:, :, None, :]
        return s, jnp.einsum('bhvk,bhk->bhv', s, r_t)

    xs = tuple(jnp.moveaxis(u, 1, 0) for u in (r, decay, k, v, kk, a))
    state, y = lax.scan(step, state0.astype(jnp.float32), xs)
    y = jnp.moveaxis(y, 0, 1)
    mean = jnp.mean(y, axis=-1, keepdims=True)
    var = jnp.mean(jnp.square(y - mean), axis=-1, keepdims=True)
    y = ((y - mean) * lax.rsqrt(var + GN_EPS)).reshape(b, t, W_R) * gn_g + gn_b
    y = y + (jnp.sum(r * k * r_k, axis=-1, keepdims=True) * v).reshape(b, t, W_R)
    return y.astype(rz.dtype), state, z[:, -1]


def mixer_inputs(x, pos, norm_g, w_in):
    b, t = x.shape[:2]
    h = rms_norm(x, norm_g)
    z = jnp.einsum('btd,dc->btc', h, w_in)
    q, kv, nsa_g, a_gate, rz, r_gate, merge = jnp.split(z, SPLIT_OFFSETS, axis=-1)
    q = rope(q.reshape(b, t, NSA_HEADS, HD), pos)
    kv = kv.reshape(b, t, 3, 2, NSA_GROUPS, HD)
    kv = jnp.stack([rope(kv[:, :, :, 0], pos), kv[:, :, :, 1]], axis=3)
    gates = jax.nn.sigmoid(nsa_g.astype(jnp.float32)).reshape(b, t, NSA_HEADS, 3)
    return q, kv, gates, a_gate, rz, r_gate, merge


def mixer_outputs(x, o_a, o_b, a_gate, r_gate, merge, p_in, w_pa, w_pb, w_out, ple_norm_g, w_ple_gate, w_ple_proj):
    y_a = jnp.einsum('btc,cd->btd', o_a * jax.nn.silu(a_gate), w_pa)
    y_b = jnp.einsum('btc,cd->btd', o_b * jax.nn.silu(r_gate), w_pb)
    g_a, g_b = jnp.split(jax.nn.sigmoid(merge), 2, axis=-1)
    x = x + jnp.einsum('btd,de->bte', g_a * y_a + g_b * y_b, w_out).astype(x.dtype)
    gate = jax.nn.sigmoid(jnp.einsum('btd,de->bte', rms_norm(x, ple_norm_g), w_ple_gate))
    return x + (gate * jnp.einsum('btp,pd->btd', p_in, w_ple_proj)).astype(x.dtype)


def setup_inputs(seed: int = 0) -> dict:
    key = jax.random.key(seed)
    ks = jax.random.split(key, 40)
    n_pages = PAST_LEN // PAGE_SIZE
    n_used = DEC_BATCH * n_pages
    n_pool = n_used + max(1, n_used // 4)
    win_buf = min(WIN, PAST_LEN)
    nrm = lambda k, shape, s: s * jax.random.normal(k, shape, jnp.float32)
    uni = lambda k, shape, lo, hi: jax.random.uniform(k, shape, jnp.float32, lo, hi)
    page_table = jax.random.permutation(ks[9], n_pool)[:n_used].reshape(DEC_BATCH, n_pages).astype(jnp.int32)
    return {
        'x_prompt': nrm(ks[0], (BATCH, SEQ, D_MODEL), 1.0),
        'x_sample': nrm(ks[1], (DEC_BATCH, DEC_SEQ, D_MODEL), 1.0),
        'p_prompt': nrm(ks[2], (DEPTH, BATCH, SEQ, D_PLE), 1.0),
        'p_sample': nrm(ks[3], (DEPTH, DEC_BATCH, DEC_SEQ, D_PLE), 1.0),
        'cache_cmp_kv': nrm(ks[4], (DEPTH, n_pool, PAGE_SIZE, 2, NSA_GROUPS, HD), 1.0),
        'cache_sel_kv': nrm(ks[5], (DEPTH, n_pool, PAGE_SIZE, 2, NSA_GROUPS, HD), 1.0),
        'cache_win_kv': nrm(ks[6], (DEPTH, DEC_BATCH, win_buf, 2, NSA_GROUPS, HD), 1.0),
        'state_wkv': nrm(ks[7], (DEPTH, DEC_BATCH, R_HEADS, R_HD, R_HD), 0.1),
        'state_shift': nrm(ks[8], (DEPTH, DEC_BATCH, C_R), 1.0),
        'page_table': page_table,
        'norm_g': 1.0 + nrm(ks[10], (DEPTH, D_MODEL), 0.02),
        'w_in': nrm(ks[11], (DEPTH, D_MODEL, C_IN), D_MODEL ** -0.5),
        'cmp_pe': nrm(ks[12], (DEPTH, 2, L_CMP, HD), 0.02),
        'cmp_w1': nrm(ks[13], (DEPTH, 2, L_CMP, HD, CMP_HID), (L_CMP * HD) ** -0.5),
        'cmp_b1': nrm(ks[14], (DEPTH, 2, CMP_HID), 0.01),
        'cmp_w2': nrm(ks[15], (DEPTH, 2, CMP_HID, HD), CMP_HID ** -0.5),
        'cmp_b2': nrm(ks[16], (DEPTH, 2, HD), 0.01),
        'w_pa': nrm(ks[17], (DEPTH, W_A, D_MODEL), W_A ** -0.5),
        'rwkv_mu': uni(ks[18], (DEPTH, C_R), 0.0, 1.0),
        'rwkv_w0': uni(ks[19], (DEPTH, W_R), -6.0, -1.0),
        'rwkv_w_up': nrm(ks[20], (DEPTH, R_W, W_R), 0.1 * R_W ** -0.5),
        'rwkv_a0': nrm(ks[21], (DEPTH, W_R), 0.1),
        'rwkv_a_up': nrm(ks[22], (DEPTH, R_A, W_R), 0.1 * R_A ** -0.5),
        'rwkv_k_k': 0.85 + nrm(ks[23], (DEPTH, W_R), 0.05),
        'rwkv_k_a': 1.0 + nrm(ks[24], (DEPTH, W_R), 0.05),
        'rwkv_r_k': nrm(ks[25], (DEPTH, R_HEADS, R_HD), 0.1),
        'rwkv_gn_g': 1.0 + nrm(ks[26], (DEPTH, W_R), 0.02),
        'rwkv_gn_b': nrm(ks[27], (DEPTH, W_R), 0.01),
        'w_pb': nrm(ks[28], (DEPTH, W_R, D_MODEL), W_R ** -0.5),
        'w_out': nrm(ks[29], (DEPTH, D_MODEL, D_MODEL), D_MODEL ** -0.5),
        'ple_norm_g': 1.0 + nrm(ks[30], (DEPTH, D_MODEL), 0.02),
        'w_ple_gate': nrm(ks[31], (DEPTH, D_MODEL, D_MODEL), D_MODEL ** -0.5),
        'w_ple_proj': nrm(ks[32], (DEPTH, D_PLE, D_MODEL), D_PLE ** -0.5),
        'final_norm_g': 1.0 + nrm(ks[33], (D_MODEL,), 0.02),
    }


def reference(x_prompt, x_sample, p_prompt, p_sample, cache_cmp_kv, cache_sel_kv, cache_win_kv, state_wkv, state_shift,
              page_table, norm_g, w_in, cmp_pe, cmp_w1, cmp_b1, cmp_w2, cmp_b2, w_pa, rwkv_mu, rwkv_w0, rwkv_w_up,
              rwkv_a0, rwkv_a_up, rwkv_k_k, rwkv_k_a, rwkv_r_k, rwkv_gn_g, rwkv_gn_b, w_pb, w_out, ple_norm_g,
              w_ple_gate, w_ple_proj, final_norm_g):
    b, s_len = x_prompt.shape[:2]
    db, ds = x_sample.shape[:2]
    n_pages = page_table.shape[1]
    past = n_pages * PAGE_SIZE
    win_buf = cache_win_kv.shape[2]
    win_p = min(WIN, s_len)
    pos_p = jnp.arange(s_len, dtype=jnp.int32)
    pos_s = past + jnp.arange(ds, dtype=jnp.int32)
    kw_pos_p = jnp.concatenate([jnp.full((WIN,), -1, jnp.int32), pos_p])
    kw_pos_s = past - win_buf + jnp.arange(win_buf + ds, dtype=jnp.int32)
    xp, xs = x_prompt, x_sample
    cmp_p, cmp_s, sel_p, sel_s, win_p_l, win_s_l = [], [], [], [], [], []
    wkv_p, wkv_s, sh_p, sh_s = [], [], [], []
    for i in range(DEPTH):
        cmp_w = (cmp_pe[i], cmp_w1[i], cmp_b1[i], cmp_w2[i], cmp_b2[i])
        rw = (rwkv_mu[i], rwkv_w0[i], rwkv_w_up[i], rwkv_a0[i], rwkv_a_up[i], rwkv_k_k[i], rwkv_k_a[i],
              rwkv_r_k[i], rwkv_gn_g[i], rwkv_gn_b[i])
        out_w = (w_pa[i], w_pb[i], w_out[i], ple_norm_g[i], w_ple_gate[i], w_ple_proj[i])

        q, kv, gates, a_gate, rz, r_gate, merge = mixer_inputs(xp, pos_p, norm_g[i], w_in[i])
        kc, vc, c_end, k_s, v_s, ov = nsa_context(kv[:, :, 0], kv[:, :, 1], *cmp_w)
        kw = jnp.pad(kv[:, :, 2], ((0, 0), (WIN, 0), (0, 0), (0, 0), (0, 0)))

        def block(j):
            s0 = j * Q_BLOCK
            qb = lax.dynamic_slice_in_dim(q, s0, Q_BLOCK, axis=1)
            gb = lax.dynamic_slice_in_dim(gates, s0, Q_BLOCK, axis=1)
            pb = lax.dynamic_slice_in_dim(pos_p, s0, Q_BLOCK)
            wb = lax.dynamic_slice_in_dim(kw, s0, Q_BLOCK + WIN, axis=1)
            wp = lax.dynamic_slice_in_dim(kw_pos_p, s0, Q_BLOCK + WIN)
            return nsa_query_block(qb, gb, pb, kc, vc, c_end, k_s, v_s, ov, wb[:, :, 0], wb[:, :, 1], wp)

        o_a = lax.map(block, jnp.arange(s_len // Q_BLOCK))
        o_a = jnp.moveaxis(o_a, 0, 1).reshape(b, s_len, W_A)
        o_b, st_p, shift_p = rwkv_time_mix(rz, jnp.zeros((b, C_R), rz.dtype),
                                           jnp.zeros((b, R_HEADS, R_HD, R_HD), jnp.float32), *rw)
        xp = mixer_outputs(xp, o_a, o_b, a_gate, r_gate, merge, p_prompt[i], *out_w)
        cmp_p.append(kv[:, :, 0].astype(cache_cmp_kv.dtype))
        sel_p.append(kv[:, :, 1].astype(cache_sel_kv.dtype))
        win_p_l.append(kv[:, s_len - win_p:, 2].astype(cache_win_kv.dtype))
        wkv_p.append(st_p.astype(state_wkv.dtype))
        sh_p.append(shift_p.astype(state_shift.dtype))

        q, kv, gates, a_gate, rz, r_gate, merge = mixer_inputs(xs, pos_s, norm_g[i], w_in[i])
        past_cmp = cache_cmp_kv[i][page_table].reshape(db, past, 2, NSA_GROUPS, HD)
        past_sel = cache_sel_kv[i][page_table].reshape(db, past, 2, NSA_GROUPS, HD)
        full_cmp = jnp.concatenate([past_cmp, kv[:, :, 0].astype(past_cmp.dtype)], axis=1)
        full_sel = jnp.concatenate([past_sel, kv[:, :, 1].astype(past_sel.dtype)], axis=1)
        kc, vc, c_end, k_s, v_s, ov = nsa_context(full_cmp, full_sel, *cmp_w)
        win = jnp.concatenate([cache_win_kv[i], kv[:, :, 2].astype(cache_win_kv.dtype)], axis=1)
        o_a = nsa_query_block(q, gates, pos_s, kc, vc, c_end, k_s, v_s, ov, win[:, :, 0], win[:, :, 1], kw_pos_s)
        o_b, st_s, shift_s = rwkv_time_mix(rz, state_shift[i], state_wkv[i], *rw)
        xs = mixer_outputs(xs, o_a, o_b, a_gate, r_gate, merge, p_sample[i], *out_w)
        cmp_s.append(kv[:, :, 0].astype(cache_cmp_kv.dtype))
        sel_s.append(kv[:, :, 1].astype(cache_sel_kv.dtype))
        win_s_l.append(win[:, win.shape[1] - win_buf:])
        wkv_s.append(st_s.astype(state_wkv.dtype))
        sh_s.append(shift_s.astype(state_shift.dtype))

    y_prompt = rms_norm(xp, final_norm_g)
    y_sample = rms_norm(xs, final_norm_g)
    return (y_prompt, y_sample, jnp.stack(cmp_p), jnp.stack(cmp_s), jnp.stack(sel_p), jnp.stack(sel_s),
            jnp.stack(win_p_l), jnp.stack(win_s_l), jnp.stack(wkv_p), jnp.stack(wkv_s), jnp.stack(sh_p), jnp.stack(sh_s))
```

```python
import numpy as np
import concourse.bass as bass
import concourse.mybir as mybir
from concourse.bass_utils import run_bass_kernel_spmd

F32 = mybir.dt.float32
BF16 = mybir.dt.bfloat16
I32 = mybir.dt.int32
AF = mybir.ActivationFunctionType
ALU = mybir.AluOpType
AX = mybir.AxisListType

NCORES = 8
D = 1024
NT_CTX = 16
NT_MAIN = 16
NT_S = 4
NT = NT_CTX + NT_MAIN + NT_S
C_R = 1664
OFF_Q, OFF_KV, OFF_NG, OFF_AG, OFF_RZ, OFF_RG, OFF_MG = 0, 512, 1280, 1304, 1816, 3480, 3992
SYNC_SAME = ('dve', 'act', 'pool')


class Buf:
    def __init__(self, name):
        self.name = name
        self.lw = None
        self.rd = []
        self.sem = None
        self.semcnt = 0


class Sched:
    def __init__(self, nc):
        self.nc = nc
        self.eng = {'pe': nc.tensor, 'dve': nc.vector, 'act': nc.scalar, 'pool': nc.gpsimd, 'sp': nc.sync}
        self.esem = {k: nc.alloc_semaphore(name='es_' + k) for k in self.eng}
        self.ecnt = {k: 0 for k in self.eng}
        self.waited = {k: {} for k in self.eng}
        self.bufs = {}
        self.store_bufs = []
        self.nsem = 0

    def buf(self, name):
        b = self.bufs.get(name)
        if b is None:
            b = self.bufs[name] = Buf(name)
        return b

    def _wait(self, e, ev):
        sem, cnt, src = ev
        if src == e and e not in SYNC_SAME:
            return
        key = id(sem)
        if self.waited[e].get(key, 0) >= cnt:
            return
        self.eng[e].wait_ge(sem, cnt)
        self.waited[e][key] = cnt

    def _deps(self, e, reads, writes):
        for b in reads:
            b = self.buf(b)
            if b.lw is not None:
                self._wait(e, b.lw)
        for b in writes:
            b = self.buf(b)
            if b.lw is not None:
                self._wait(e, b.lw)
            for ev in b.rd:
                self._wait(e, ev)

    def op(self, e, reads, writes, fn):
        self._deps(e, reads, writes)
        ins = fn()
        self.ecnt[e] += 1
        ins.then_inc(self.esem[e], 1)
        ev = (self.esem[e], self.ecnt[e], e)
        for b in writes:
            b = self.buf(b)
            b.lw = ev
            b.rd = []
        for b in reads:
            if b not in writes:
                self.buf(b).rd.append(ev)
        return ins

    def dma(self, q, reads, writes, out, in_, **kw):
        self._deps(q, reads, writes)
        ob = self.buf(writes[0]) if writes else self.buf(reads[0])
        if ob.sem is None:
            ob.sem = self.nc.alloc_semaphore(name='ds_%d' % self.nsem)
            self.nsem += 1
        ins = self.eng[q].dma_start(out=out, in_=in_, **kw)
        ob.semcnt += 16
        ins.then_inc(ob.sem, 16)
        ev = (ob.sem, ob.semcnt, None)
        for b in writes:
            b = self.buf(b)
            b.lw = ev
            b.rd = []
        for b in reads:
            self.buf(b).rd.append(ev)
        if not writes:
            self.store_bufs.append(ob)
        return ins

    def idma(self, reads, writes, out, in_, idx_ap):
        q = 'pool'
        self._deps(q, reads, writes)
        ob = self.buf(writes[0])
        if ob.sem is None:
            ob.sem = self.nc.alloc_semaphore(name='ds_%d' % self.nsem)
            self.nsem += 1
        ins = self.nc.gpsimd.indirect_dma_start(out=out, out_offset=None, in_=in_,
                                                in_offset=bass.IndirectOffsetOnAxis(ap=idx_ap, axis=0))
        ob.semcnt += 16
        ins.then_inc(ob.sem, 16)
        ev = (ob.sem, ob.semcnt, None)
        for b in writes:
            b = self.buf(b)
            b.lw = ev
            b.rd = []
        for b in reads:
            self.buf(b).rd.append(ev)
        return ins

    def barrier(self):
        dsems = [(b.sem, b.semcnt) for b in self.bufs.values() if b.sem is not None]
        for e in self.eng:
            for o in self.eng:
                if o != e and self.ecnt[o] > self.waited[e].get(id(self.esem[o]), 0):
                    self.eng[e].wait_ge(self.esem[o], self.ecnt[o])
                    self.waited[e][id(self.esem[o])] = self.ecnt[o]
            for sem, cnt in dsems:
                if cnt > self.waited[e].get(id(sem), 0):
                    self.eng[e].wait_ge(sem, cnt)
                    self.waited[e][id(sem)] = cnt

    def finish(self, e='sp'):
        seen = set()
        for ob in self.store_bufs:
            if id(ob) in seen:
                continue
            seen.add(id(ob))
            self.eng[e].wait_ge(ob.sem, ob.semcnt)


import os
_KT = os.environ.get('KTILES')
_KCUT = int(os.environ.get('KCUT', '99'))
_KDBG = bool(os.environ.get('KDBG'))
DBG = {}


def build_program(NPOOL):
    nc = bass.Bass("TRN2", target_bir_lowering=False)
    S = Sched(nc)

    def din(name, shape, dt=F32):
        return nc.dram_tensor(name, list(shape), dt, kind="ExternalInput").ap()

    def dout(name, shape, dt=F32):
        return nc.dram_tensor(name, list(shape), dt, kind="ExternalOutput").ap()

    x_loc = din("x_loc", [NT * 128, D])
    rope_cs = din("rope_cs", [NT * 128, 64])
    ident_d = din("ident", [128, 128])
    norm_g = din("norm_g", [1, D])
    w_in = din("w_in", [D, 6040])
    rwkv_mu = din("rwkv_mu", [1, C_R])
    sshift = din("sshift", [NT_S, C_R])
    cwin = din("cwin", [NT_S, 512, 256])

    cmat = din("cmat", [128, 5 * 128 + 3 + 128])
    e0row_d = din("e0row", [1, 128])
    rw_vec = din("rw_vec", [7, 512])
    rw_up = din("rw_up", [2, 64, 512])
    swkv = din("swkv", [NT_S, 8, 64, 64])
    cmp_w1 = din("cmp_w1", [2, 32, 64, 128])
    cmp_pe = din("cmp_pe", [2, 32, 64])
    cmp_w2 = din("cmp_w2", [2, 128, 64])
    cmp_b1 = din("cmp_b1", [2, 128])
    cmp_b2 = din("cmp_b2", [2, 64])
    cmaskP = din("cmaskP", [128, NT_MAIN, 256])
    tkm1 = din("tkm1", [128, NT_MAIN, 64])
    tkm2 = din("tkm2", [128, NT_MAIN, 64])
    ovP = din("ovP", [128, 2, 64])
    ctxv_d = din("ctxv", [128, 1])
    tkmS = din("tkmS", [128, 2, 257])
    ovS = din("ovS", [128, 8, 257])
    iotap = din("iotap", [128, 1])
    ptab = din("ptab", [NT_S, 128], I32)
    ccmp = din("ccmp", [NPOOL * 128, 256])
    csel = din("csel", [NPOOL * 128, 256])
    w_pa = din("w_pa", [512, D])
    w_pb = din("w_pb", [512, D])
    w_out = din("w_out", [D, D])
    w_pg = din("w_pg", [D, D])
    w_pp = din("w_pp", [256, D])
    ple_g = din("ple_g", [1, D])
    fin_g = din("fin_g", [1, D])
    p_loc = din("p_loc", [(NT_MAIN + NT_S) * 128, 256])
    o_y = dout("o_y", [2048, D])
    o_y_s = dout("o_y_s", [NT_S * 8, D])

    o_wkv = dout("o_wkv", [8, 64, 64])
    o_wkv_s = dout("o_wkv_s", [NT_S, 8, 64, 64])
    kvscr = nc.dram_tensor("kvscr", [NT * 128, 768], F32, kind="Internal").ap()
    oascr = nc.dram_tensor("oascr", [(NT_MAIN + NT_S) * 128, 512], F32, kind="ExternalOutput" if _KDBG else "Internal").ap()
    obscr = nc.dram_tensor("obscr", [(NT_MAIN + NT_S) * 128, 512], F32, kind="ExternalOutput" if _KDBG else "Internal").ap()
    o_cmp = dout("o_cmp", [2048, 256])
    o_sel = dout("o_sel", [2048, 256])
    o_win = dout("o_win", [512, 256])
    o_shift = dout("o_shift", [1, C_R])
    o_cmp_s = dout("o_cmp_s", [NT_S * 8, 256])
    o_sel_s = dout("o_sel_s", [NT_S * 8, 256])
    o_win_s = dout("o_win_s", [NT_S, 512, 256])
    o_shift_s = dout("o_shift_s", [NT_S, C_R])

    psb = lambda name, shape, dt=F32: nc.alloc_sbuf_tensor(name, list(shape), dt)
    arena = {'ptr': None, 'lo': None, 'phase': 0}
    DTB = {F32: 4, BF16: 2, I32: 4}

    def sb(name, shape, dt=F32):
        if arena['lo'] is None:
            arena['lo'] = arena['ptr'] = (nc.sbuf_base + 63) // 64 * 64
        nbytes = int(np.prod(shape[1:])) * DTB[dt]
        off = arena['ptr']
        arena['ptr'] = (off + nbytes + 31) // 32 * 32
        assert arena['ptr'] <= nc.sbuf_top, ("SBUF overflow", name, arena['ptr'], nc.sbuf_top)
        return nc.alloc_sbuf_tensor_at("ph%d_%s" % (arena['phase'], name), list(shape), dt, offset=off)

    def arena_reset():
        if os.environ.get('KVERB'):
            print("phase", arena['phase'], "used", arena['ptr'] - arena['lo'], "free", nc.sbuf_top - arena['ptr'], flush=True)
        arena['ptr'] = arena['lo']
        arena['phase'] += 1
    ident = psb("identt", [128, 128])
    gbc = psb("gbc", [128, D])
    identb = psb("identb", [128, 128], BF16)
    xt = [psb("xt%d" % i, [128, D]) for i in range(2)]
    cs = [psb("cs%d" % i, [128, 64]) for i in range(2)]
    hf = psb("hf", [128, D])
    ss = psb("ss", [128, 4])
    hT = [psb("hT%d" % i, [128, 8, 129], BF16) for i in range(2)]
    cm = psb("cm", [128, 5 * 128 + 3 + 128])
    junk = psb("junk", [128, D], BF16)
    PS = [nc.alloc_psum_tensor("P%d" % i, [128, 512], F32) for i in range(8)]
    mubc = sb("mubc", [128, C_R])
    Wkv = sb("Wkv", [128, 8, 768], BF16)
    W1 = sb("W1", [128, 8, C_R], BF16)
    W2 = sb("W2", [128, 8, C_R], BF16)
    wstage = [sb("wstage0", [128, C_R])] * 2
    TriU, TriUs, Mlow, Mup, Mupi = [cm[:, j * 128:(j + 1) * 128] for j in range(5)]
    e0row = sb("e0rowb", [1, 128], BF16)
    e0f = sb("e0f", [1, 128])
    rwv = sb("rwv", [128, 7, 512])
    w0bc, a0bc, kkbc, kabc, rkbc, gngbc, gnbbc = [rwv[:, j, :] for j in range(7)]
    wup = sb("wup", [64, 512])
    aup = sb("aup", [64, 512])
    ssf = wstage[0][0:1, :]
    zl = ssf
    ssm = sb("ssm", [1, C_R], BF16)
    twd = sb("twd", [64, 128])
    adT = sb("adT", [64, 128])
    RW = {n: sb("rw_" + n, [128, 512]) for n in
          ['t0', 'sig', 'a', 'P', 'Pinv', 'Pm1', 'kk', 'k2', 'b', 'kkt', 'kt', 'bt', 'rt', 'V', 't1']}
    rsm = sb("rsm", [128, 8, 8])
    TT = {n: sb("tt_" + n, [64, 8, 128]) for n in ['kkt', 'kt', 'bt', 'rt']}
    MX = {n: sb("mx_" + n, [128, 8, 128]) for n in ['X0', 'X1', 'XT0', 'XT1', 'Z', 'A1T']}
    Hs = sb("Hs", [64, 8, 64])
    Sio = sb("Sio", [64, 8, 64])
    PCt = sb("PCt", [64, 8])
    kvf = [sb("kvf0", [128, 768])] * 2
    rtmp = hf[:, 0:768].rearrange("p (a b c d) -> p a b c d", a=4, b=3, c=2)

    S.dma('sp', [], ['ident'], ident[:], ident_d[:, :])
    S.op('dve', ['ident'], ['identb'], lambda: nc.vector.tensor_copy(out=identb[:], in_=ident[:]))
    S.dma('sp', [], ['gbc'], gbc[:], norm_g.partition_broadcast(128))
    S.dma('sp', [], ['mubc'], mubc[:], rwkv_mu.partition_broadcast(128))
    for k in range(8):
        S.dma('pool', [], ['Wkv'], Wkv[:, k, :], w_in[k * 128:(k + 1) * 128, OFF_KV:OFF_KV + 768])
    for k in range(8):
        ws = wstage[k % 2]
        wk = 'wstage0'
        S.dma('sp', [], [wk], ws[:], w_in[k * 128:(k + 1) * 128, OFF_RZ:OFF_RZ + C_R])
        S.op('dve', [wk, 'mubc'], ['W2'],
             lambda: nc.vector.tensor_tensor(out=W2[:, k, :], in0=ws[:], in1=mubc[:], op=ALU.mult))
        S.op('dve', [wk, 'W2'], ['W1'],
             lambda: nc.vector.tensor_tensor(out=W1[:, k, :], in0=ws[:], in1=W2[:, k, :], op=ALU.subtract))
    S.dma('sp', [], ['cm'], cm[:], cmat[:, :])
    S.dma('sp', [], ['e0f'], e0f[:], e0row_d[:, :])
    S.op('dve', ['e0f'], ['e0row'], lambda: nc.vector.tensor_copy(out=e0row[:], in_=e0f[:]))
    for j in range(7):
        S.dma('sp', [], ['rwv'], rwv[:, j, :], rw_vec[j:j + 1, :].partition_broadcast(128))
    S.dma('sp', [], ['wup'], wup[:], rw_up[0])
    S.dma('sp', [], ['aup'], aup[:], rw_up[1])
    S.op('dve', [], ['hT0'], lambda: nc.vector.memset(hT[0][:], 0.0))
    S.op('dve', [], ['hT1'], lambda: nc.vector.memset(hT[1][:], 0.0))

    def load_x(lt):
        i = lt % 2
        S.dma('sp', [], ['xt%d' % i], xt[i][:], x_loc[lt * 128:(lt + 1) * 128, :])
        S.dma('sp', [], ['cs%d' % i], cs[i][:], rope_cs[lt * 128:(lt + 1) * 128, :])

    def norm_transpose(lt, first_of_seq):
        i = lt % 2
        xk, hk = 'xt%d' % i, 'hT%d' % i
        x_ = xt[i]
        S.op('act', [xk], ['junk', 'ss'],
             lambda: nc.scalar.activation(out=junk[:], in_=x_[:], func=AF.Square, accum_out=ss[:, 0:1]))
        S.op('dve', ['ss'], ['ss'],
             lambda: nc.vector.tensor_scalar(out=ss[:, 1:2], in0=ss[:, 0:1], scalar1=1.0 / D, scalar2=1e-6,
                                             op0=ALU.mult, op1=ALU.add))
        S.op('act', ['ss'], ['ss'], lambda: nc.scalar.activation(out=ss[:, 2:3], in_=ss[:, 1:2], func=AF.Sqrt))
        S.op('dve', ['ss'], ['ss'], lambda: nc.vector.reciprocal(out=ss[:, 3:4], in_=ss[:, 2:3]))
        S.op('dve', [xk, 'ss', 'gbc'], ['hf'],
             lambda: nc.vector.scalar_tensor_tensor(out=hf[:], in0=x_[:], scalar=ss[:, 3:4], in1=gbc[:],
                                                    op0=ALU.mult, op1=ALU.mult))
        for k in range(8):
            pk = k // 4
            S.op('pe', ['hf', 'ident'], ['P%d' % pk],
                 lambda: nc.tensor.transpose(out=PS[pk][:, (k % 4) * 128:(k % 4 + 1) * 128],
                                             in_=hf[:, k * 128:(k + 1) * 128], identity=ident[:]))
        if first_of_seq:
            S.op('pool', [], [hk], lambda: nc.gpsimd.memset(hT[i][:, :, 0:1], 0.0))
        else:
            S.op('pool', ['hT%d' % (1 - i)], [hk],
                 lambda: nc.gpsimd.tensor_copy(out=hT[i][:, :, 0:1], in_=hT[1 - i][:, :, 128:129]))
        for pk in range(2):
            S.op('act', ['P%d' % pk], [hk],
                 lambda: nc.scalar.copy(out=hT[i][:, pk * 4:(pk + 1) * 4, 1:129],
                                        in_=PS[pk][:].rearrange("p (k t) -> p k t", k=4)))

    def project(lt, W, wkey, c0, ncols, pidx, shifted=False):
        i = lt % 2
        hk = 'hT%d' % i
        lo = 0 if shifted else 1
        for k in range(8):
            S.op('pe', [hk, wkey], ['P%d' % pidx],
                 lambda: nc.tensor.matmul(PS[pidx][:, 0:ncols], lhsT=hT[i][:, k, lo:lo + 128],
                                          rhs=W[:, k, c0:c0 + ncols], start=(k == 0), stop=(k == 7)))

    def kv_part(lt):
        i = lt % 2
        kk = 'kvf0'
        kv_ = kvf[i]
        for cg in range(2):
            project(lt, Wkv, 'Wkv', cg * 384, 384, 2 + cg)
            S.op('act', ['P%d' % (2 + cg)], [kk],
                 lambda: nc.scalar.copy(out=kv_[:, cg * 384:(cg + 1) * 384], in_=PS[2 + cg][:, 0:384]))
        kview = kv_[:].rearrange("p (br kv g d) -> p br kv g d", br=3, kv=2, g=2, d=64)
        x1 = kview[:, :, 0, :, 0:32]
        x2 = kview[:, :, 0, :, 32:64]
        ck = 'cs%d' % i
        cosb = cs[i][:, 0:32].unsqueeze(1).unsqueeze(1).to_broadcast([128, 3, 2, 32])
        sinb = cs[i][:, 32:64].unsqueeze(1).unsqueeze(1).to_broadcast([128, 3, 2, 32])
        S.op('dve', [kk, ck], ['hf'], lambda: nc.vector.tensor_tensor(out=rtmp[:, 0], in0=x1, in1=cosb, op=ALU.mult))
        S.op('dve', [kk, ck], ['hf'], lambda: nc.vector.tensor_tensor(out=rtmp[:, 1], in0=x2, in1=sinb, op=ALU.mult))
        S.op('dve', [kk, ck], ['hf'], lambda: nc.vector.tensor_tensor(out=rtmp[:, 2], in0=x2, in1=cosb, op=ALU.mult))
        S.op('dve', [kk, ck], ['hf'], lambda: nc.vector.tensor_tensor(out=rtmp[:, 3], in0=x1, in1=sinb, op=ALU.mult))
        S.op('dve', ['hf'], [kk], lambda: nc.vector.tensor_tensor(out=x1, in0=rtmp[:, 0], in1=rtmp[:, 1], op=ALU.subtract))
        S.op('dve', ['hf'], [kk], lambda: nc.vector.tensor_tensor(out=x2, in0=rtmp[:, 2], in1=rtmp[:, 3], op=ALU.add))
        S.dma('sp', [kk], [], kvscr[lt * 128:(lt + 1) * 128, :], kv_[:, :])
        if NT_CTX <= lt < NT_CTX + NT_MAIN:
            r0 = (lt - NT_CTX) * 128
            S.dma('sp', [kk], [], o_cmp[r0:r0 + 128, :], kv_[:, 0:256])
            S.dma('sp', [kk], [], o_sel[r0:r0 + 128, :], kv_[:, 256:512])
            if lt >= NT_CTX + NT_MAIN - 4:
                w0 = (lt - (NT_CTX + NT_MAIN - 4)) * 128
                S.dma('sp', [kk], [], o_win[w0:w0 + 128, :], kv_[:, 512:768])
        elif lt >= NT_CTX + NT_MAIN:
            s = lt - NT_CTX - NT_MAIN
            S.dma('sp', [kk], [], o_cmp_s[s * 8:(s + 1) * 8, :], kv_[0:8, 0:256])
            S.dma('sp', [kk], [], o_sel_s[s * 8:(s + 1) * 8, :], kv_[0:8, 256:512])
            S.dma('sp', [kk], [], o_win_s[s, 504:512, :], kv_[0:8, 512:768])

    def z_last(lt, col, out_ap):
        i = lt % 2
        hk = 'hT%d' % i
        for cg in range(4):
            n = 0
            for W, wk in ((W1, 'W1'), (W2, 'W2')):
                for k in range(8):
                    S.op('pe', [hk, wk], ['P4'],
                         lambda: nc.tensor.matmul(PS[4][0:1, 0:416], lhsT=hT[i][:, k, col:col + 1],
                                                  rhs=W[:, k, cg * 416:(cg + 1) * 416],
                                                  start=(n == 0), stop=(n == 15)))
                    n += 1
            S.op('act', ['P4'], ['wstage0'], lambda: nc.scalar.copy(out=zl[0:1, cg * 416:(cg + 1) * 416], in_=PS[4][0:1, 0:416]))
        S.dma('sp', ['wstage0'], [], out_ap, zl)

    psn = [0]

    def newps():
        psn[0] = 3 + (psn[0] - 3 + 1) % 5 if psn[0] >= 3 else 3
        return psn[0]

    def mm(pidx, out, lhsT, rhs, rkeys, start=True, stop=True):
        S.op('pe', rkeys, ['P%d' % pidx],
             lambda: nc.tensor.matmul(out, lhsT=lhsT, rhs=rhs, start=start, stop=stop))

    def vtt(out, in0, in1, op, r, w):
        S.op('dve', r, w, lambda: nc.vector.tensor_tensor(out=out, in0=in0, in1=in1, op=op))

    def vstt(out, in0, scalar, in1, op0, op1, r, w):
        S.op('dve', r, w, lambda: nc.vector.scalar_tensor_tensor(out=out, in0=in0, scalar=scalar, in1=in1,
                                                                 op0=op0, op1=op1))

    def act(out, in_, func, r, w, scale=1.0):
        S.op('act', r, w, lambda: nc.scalar.activation(out=out, in_=in_, func=func, scale=scale))

    def h3(ap):
        return ap.rearrange("p (h d) -> p h d", h=8)

    def bank4(pidx):
        return PS[pidx][:].rearrange("p (h t) -> p h t", h=4)

    def rwkv_tile(lt):
        i = lt % 2
        hk = 'hT%d' % i
        is_s = lt >= NT_CTX + NT_MAIN
        s_i = lt - NT_CTX - NT_MAIN
        R = dict(RW)
        R['RHS'], R['nU'], R['Y'] = RW['kk'], RW['b'], RW['P']
        kal = {'RHS': 'kk', 'nU': 'b', 'Y': 'P'}
        k_ = lambda n: 'rw_' + kal.get(n, n)
        if lt == 0:
            S.op('dve', [], ['Hs'], lambda: nc.vector.memset(Hs[:], 0.0))
        if is_s:
            S.dma('sp', [], ['wstage0'], ssf, sshift[s_i:s_i + 1, :])
            vtt(ssm[:], ssf, mubc[0:1, :], ALU.mult, ['wstage0', 'mubc'], ['ssm'])
            S.dma('sp', [], ['Sio'], Sio[:], swkv[s_i].rearrange("h v k -> v h k"))
            p = newps()
            for h in range(8):
                S.op('pe', ['Sio', 'ident'], ['P%d' % p],
                     lambda: nc.tensor.transpose(out=PS[p][0:64, h * 64:(h + 1) * 64], in_=Sio[:, h, :],
                                                 identity=ident[0:64, 0:64]))
            S.op('act', ['P%d' % p], ['Hs'],
                 lambda: nc.scalar.copy(out=Hs[:].rearrange("p h v -> p (h v)"), in_=PS[p][0:64, :]))
        zp = []
        for g3 in range(3):
            p = g3
            n = 0
            tot = 16 + (1 if is_s else 0)
            for (W, wk, lo) in ((W1, 'W1', 1), (W2, 'W2', 0)):
                for k in range(8):
                    mm(p, PS[p][:, :], hT[i][:, k, lo:lo + 128], W[:, k, g3 * 512:(g3 + 1) * 512], [hk, wk],
                       start=(n == 0), stop=(n == tot - 1))
                    n += 1
            if is_s:
                mm(p, PS[p][:, :], e0row[0:1, :], ssm[0:1, g3 * 512:(g3 + 1) * 512], ['e0row', 'ssm'],
                   start=False, stop=True)
            zp.append(p)
        pr, pk, pv = zp
        rP, kP, vP = 'P%d' % pr, 'P%d' % pk, 'P%d' % pv
        if _KCUT < 1:
            return
        p = newps()
        for part in range(2):
            c0 = 1536 + 64 * part
            n = 0
            tot = 16 + (1 if is_s else 0)
            for (W, wk, lo) in ((W1, 'W1', 1), (W2, 'W2', 0)):
                for k in range(8):
                    mm(p, PS[p][0:64, part * 128:(part + 1) * 128], W[:, k, c0:c0 + 64], hT[i][:, k, lo:lo + 128],
                       [hk, wk], start=(n == 0), stop=(n == tot - 1))
                    n += 1
            if is_s:
                mm(p, PS[p][0:64, part * 128:(part + 1) * 128], ssm[0:1, c0:c0 + 64], e0row[0:1, :],
                   ['e0row', 'ssm'], start=False, stop=True)
        act(twd[:], PS[p][0:64, 0:128], AF.Tanh, ['P%d' % p], ['twd'])
        act(adT[:], PS[p][0:64, 128:256], AF.Copy, ['P%d' % p], ['adT'])
        if _KCUT < 2:
            return
        pu = newps()
        mm(pu, PS[pu][:, :], twd[:], wup[:], ['twd', 'wup'])
        pa = newps()
        mm(pa, PS[pa][:, :], adT[:], aup[:], ['adT', 'aup'])
        vtt(R['t0'][:], PS[pu][:, :], w0bc, ALU.add, ['P%d' % pu, 'rwv'], [k_('t0')])
        act(R['sig'][:], R['t0'][:], AF.Sigmoid, [k_('t0')], [k_('sig')])
        vtt(R['t1'][:], PS[pa][:, :], a0bc, ALU.add, ['P%d' % pa, 'rwv'], [k_('t1')])
        act(R['a'][:], R['t1'][:], AF.Sigmoid, [k_('t1')], [k_('a')])
        if _KCUT < 3:
            return
        pc = newps()
        mm(pc, PS[pc][:, :], TriU, R['sig'][:], ['cm', k_('sig')])
        pce = newps()
        mm(pce, PS[pce][:, :], TriUs, R['sig'][:], ['cm', k_('sig')])
        act(R['P'][:], PS[pc][:, :], AF.Exp, ['P%d' % pc], [k_('P')])
        act(R['Pinv'][:], PS[pc][:, :], AF.Exp, ['P%d' % pc], [k_('Pinv')], scale=-1.0)
        act(R['Pm1'][:], PS[pce][:, :], AF.Exp, ['P%d' % pce], [k_('Pm1')])
        ppc = newps()
        lcol = 5 * 128 + (1 if is_s else 0)
        for h in range(8):
            mm(ppc, PS[ppc][0:64, h:h + 1], R['sig'][:, h * 64:(h + 1) * 64], cm[:, lcol:lcol + 1], ['cm', k_('sig')])
        act(PCt[:], PS[ppc][0:64, 0:8], AF.Exp, ['P%d' % ppc], ['PCt'])
        if _KCUT < 4:
            return
        vtt(R['t0'][:], PS[pk][:, :], kkbc, ALU.mult, [kP, 'rwv'], [k_('t0')])
        vtt(R['t1'][:], R['t0'][:], R['t0'][:], ALU.mult, [k_('t0')], [k_('t1')])
        S.op('dve', [k_('t1')], ['rsm'],
             lambda: nc.vector.tensor_reduce(out=rsm[:, 0, :], in_=h3(R['t1'][:]), axis=AX.X, op=ALU.add))
        act(rsm[:, 1, :], rsm[:, 0, :], AF.Sqrt, ['rsm'], ['rsm'])
        S.op('dve', ['rsm'], ['rsm'],
             lambda: nc.vector.tensor_scalar(out=rsm[:, 2, :], in0=rsm[:, 1, :], scalar1=1e-12, scalar2=None, op0=ALU.max))
        S.op('dve', ['rsm'], ['rsm'], lambda: nc.vector.reciprocal(out=rsm[:, 3, :], in_=rsm[:, 2, :]))
        vtt(h3(R['kk'][:]), h3(R['t0'][:]), rsm[:, 3, :].unsqueeze(2).to_broadcast([128, 8, 64]), ALU.mult,
            [k_('t0'), 'rsm'], [k_('kk')])
        vstt(R['t1'][:], R['a'][:], -1.0, kabc, ALU.add, ALU.mult, [k_('a'), 'rwv'], [k_('t1')])
        vstt(R['k2'][:], R['t1'][:], 1.0, PS[pk][:, :], ALU.add, ALU.mult, [k_('t1'), kP], [k_('k2')])
        if is_s:
            rm = cm[:, 5 * 128 + 2:5 * 128 + 3]
            for nm in ('kk', 'k2'):
                S.op('dve', [k_(nm), 'cm'], [k_(nm)],
                     lambda: nc.vector.tensor_scalar(out=R[nm][:], in0=R[nm][:], scalar1=rm, scalar2=None, op0=ALU.mult))
        vtt(R['b'][:], R['kk'][:], R['a'][:], ALU.mult, [k_('kk'), k_('a')], [k_('b')])
        vtt(R['kkt'][:], R['kk'][:], R['Pm1'][:], ALU.mult, [k_('kk'), k_('Pm1')], [k_('kkt')])
        vtt(R['kt'][:], R['k2'][:], R['Pinv'][:], ALU.mult, [k_('k2'), k_('Pinv')], [k_('kt')])
        vtt(R['bt'][:], R['b'][:], R['Pinv'][:], ALU.mult, [k_('b'), k_('Pinv')], [k_('bt')])
        vtt(R['rt'][:], PS[pr][:, :], R['P'][:], ALU.mult, [rP, k_('P')], [k_('rt')])
        act(R['V'][:], PS[pv][:, :], AF.Copy, [vP], [k_('V')])
        if is_s:
            S.op('dve', [k_('V'), 'cm'], [k_('V')],
                 lambda: nc.vector.tensor_scalar(out=R['V'][:], in0=R['V'][:], scalar1=rm, scalar2=None, op0=ALU.mult))
        vtt(R['t0'][:], PS[pr][:, :], R['k2'][:], ALU.mult, [rP, k_('k2')], [k_('t0')])
        vtt(R['t0'][:], R['t0'][:], rkbc, ALU.mult, [k_('t0'), 'rwv'], [k_('t0')])
        S.op('dve', [k_('t0')], ['rsm'],
             lambda: nc.vector.tensor_reduce(out=rsm[:, 4, :], in_=h3(R['t0'][:]), axis=AX.X, op=ALU.add))
        if _KCUT < 5:
            return
        for n in ['kkt', 'kt', 'bt', 'rt']:
            for half in range(2):
                p = newps()
                for hh in range(4):
                    h = half * 4 + hh
                    S.op('pe', [k_(n), 'ident'], ['P%d' % p],
                         lambda: nc.tensor.transpose(out=PS[p][0:64, hh * 128:(hh + 1) * 128],
                                                     in_=R[n][:, h * 64:(h + 1) * 64], identity=ident[:]))
                S.op('act', ['P%d' % p], ['tt_' + n],
                     lambda: nc.scalar.copy(out=TT[n][:, half * 4:(half + 1) * 4, :].rearrange("p h t -> p (h t)"),
                                            in_=PS[p][0:64, :]))
        kktT, ktT, btT, rtT = TT['kkt'], TT['kt'], TT['bt'], TT['rt']
        if _KCUT < 6:
            return
        def pair_mat(dst, lhs, lkey, rhs_, rkey, mask, neg, extra=None):
            for half in range(2):
                p = newps()
                for hh in range(4):
                    h = half * 4 + hh
                    mm(p, PS[p][:, hh * 128:(hh + 1) * 128], lhs[:, h, :], rhs_[:, h, :], [lkey, rkey])
                mb = mask.unsqueeze(1).to_broadcast([128, 4, 128])
                vstt(MX[dst][:, half * 4:(half + 1) * 4, :], bank4(p), -1.0 if neg else 1.0, mb, ALU.mult, ALU.mult,
                     ['P%d' % p, 'cm'], ['mx_' + {'A3T': 'XT0', 'A4T': 'XT1'}.get(dst, dst)])
        pair_mat('X0', kktT, 'tt_kkt', btT, 'tt_bt', Mlow, True)
        pair_mat('XT0', btT, 'tt_bt', kktT, 'tt_kkt', Mup, True)
        pair_mat('A1T', ktT, 'tt_kt', kktT, 'tt_kkt', Mup, False)
        for half in range(2):
            vtt(MX['Z'][:, half * 4:(half + 1) * 4, :], MX['XT0'][:, half * 4:(half + 1) * 4, :],
                ident[:].unsqueeze(1).to_broadcast([128, 4, 128]), ALU.add, ['mx_XT0', 'ident'], ['mx_Z'])
        if _KCUT < 7:
            return
        cur = 0
        for j in range(1, 7):
            Xo, XTo = MX['X%d' % cur], MX['XT%d' % cur]
            Xn, XTn = MX['X%d' % (1 - cur)], MX['XT%d' % (1 - cur)]
            ko, kto, kn, ktn = 'mx_X%d' % cur, 'mx_XT%d' % cur, 'mx_X%d' % (1 - cur), 'mx_XT%d' % (1 - cur)
            for half in range(2):
                p = newps()
                for hh in range(4):
                    h = half * 4 + hh
                    mm(p, PS[p][:, hh * 128:(hh + 1) * 128], XTo[:, h, :], Xo[:, h, :], [ko, kto])
                S.op('act', ['P%d' % p], [kn],
                     lambda: nc.scalar.copy(out=Xn[:, half * 4:(half + 1) * 4, :], in_=bank4(p)))
            if j < 6:
                for half in range(2):
                    p = newps()
                    for hh in range(4):
                        h = half * 4 + hh
                        mm(p, PS[p][:, hh * 128:(hh + 1) * 128], Xo[:, h, :], XTo[:, h, :], [ko, kto])
                    S.op('act', ['P%d' % p], [ktn],
                         lambda: nc.scalar.copy(out=XTn[:, half * 4:(half + 1) * 4, :], in_=bank4(p)))
            for half in range(2):
                p = newps()
                for hh in range(4):
                    h = half * 4 + hh
                    mm(p, PS[p][:, hh * 128:(hh + 1) * 128], Xn[:, h, :], MX['Z'][:, h, :], [kn, 'mx_Z'])
                vtt(MX['Z'][:, half * 4:(half + 1) * 4, :], MX['Z'][:, half * 4:(half + 1) * 4, :], bank4(p), ALU.add,
                    ['mx_Z', 'P%d' % p], ['mx_Z'])
            cur = 1 - cur
        if _KCUT < 8:
            return
        MX['A3T'], MX['A4T'] = MX['XT0'], MX['XT1']
        pair_mat('A3T', ktT, 'tt_kt', rtT, 'tt_rt', Mupi, False)
        pair_mat('A4T', btT, 'tt_bt', rtT, 'tt_rt', Mupi, False)
        V3 = h3(R['V'][:])
        p = newps()
        for h in range(8):
            mm(p, PS[p][:, h * 64:(h + 1) * 64], kktT[:, h, :], Hs[:, h, :], ['tt_kkt', 'Hs'], start=True, stop=False)
            mm(p, PS[p][:, h * 64:(h + 1) * 64], MX['A1T'][:, h, :], V3[:, h, :], ['mx_A1T', k_('V')], start=False, stop=True)
        act(R['RHS'][:], PS[p][:, :], AF.Copy, ['P%d' % p], [k_('RHS')])
        RHS3 = h3(R['RHS'][:])
        p = newps()
        for h in range(8):
            mm(p, PS[p][:, h * 64:(h + 1) * 64], MX['Z'][:, h, :], RHS3[:, h, :], ['mx_Z', k_('RHS')])
        act(R['nU'][:], PS[p][:, :], AF.Copy, ['P%d' % p], [k_('nU')], scale=-1.0)
        nU3 = h3(R['nU'][:])
        p = newps()
        for h in range(8):
            mm(p, PS[p][:, h * 64:(h + 1) * 64], rtT[:, h, :], Hs[:, h, :], ['tt_rt', 'Hs'], start=True, stop=False)
            mm(p, PS[p][:, h * 64:(h + 1) * 64], MX['A3T'][:, h, :], V3[:, h, :], ['mx_XT0', k_('V')], start=False, stop=False)
            mm(p, PS[p][:, h * 64:(h + 1) * 64], MX['A4T'][:, h, :], nU3[:, h, :], ['mx_XT1', k_('nU')], start=False, stop=True)
        act(R['Y'][:], PS[p][:, :], AF.Copy, ['P%d' % p], [k_('Y')])
        if _KCUT < 9:
            return
        p = newps()
        kt3, bt3 = h3(R['kt'][:]), h3(R['bt'][:])
        for h in range(8):
            mm(p, PS[p][0:64, h * 64:(h + 1) * 64], kt3[:, h, :], V3[:, h, :], [k_('kt'), k_('V')], start=True, stop=False)
            mm(p, PS[p][0:64, h * 64:(h + 1) * 64], bt3[:, h, :], nU3[:, h, :], [k_('bt'), k_('nU')], start=False, stop=True)
        Hf = Hs[:].rearrange("p h v -> p (h v)")
        vtt(Hf, Hf, PS[p][0:64, :], ALU.add, ['Hs', 'P%d' % p], ['Hs'])
        vtt(Hs[:], Hs[:], PCt[:].unsqueeze(2).to_broadcast([64, 8, 64]), ALU.mult, ['Hs', 'PCt'], ['Hs'])
        if _KCUT < 10:
            return
        if is_s or lt == NT_CTX + NT_MAIN - 1:
            p = newps()
            for h in range(8):
                S.op('pe', ['Hs', 'ident'], ['P%d' % p],
                     lambda: nc.tensor.transpose(out=PS[p][0:64, h * 64:(h + 1) * 64], in_=Hs[:, h, :],
                                                 identity=ident[0:64, 0:64]))
            S.op('act', ['P%d' % p], ['Sio'],
                 lambda: nc.scalar.copy(out=Sio[:].rearrange("p h v -> p (h v)"), in_=PS[p][0:64, :]))
            dst = o_wkv_s[s_i] if is_s else o_wkv
            S.dma('sp', ['Sio'], [], dst.rearrange("h v k -> v h k"), Sio[:])
        if _KCUT < 11:
            return
        if lt >= NT_CTX:
            Y3 = h3(R['Y'][:])
            S.op('dve', [k_('Y')], ['rsm'],
                 lambda: nc.vector.tensor_reduce(out=rsm[:, 5, :], in_=Y3, axis=AX.X, op=ALU.add))
            S.op('dve', ['rsm'], ['rsm'],
                 lambda: nc.vector.tensor_scalar(out=rsm[:, 5, :], in0=rsm[:, 5, :], scalar1=-1.0 / 64, scalar2=None, op0=ALU.mult))
            vtt(Y3, Y3, rsm[:, 5, :].unsqueeze(2).to_broadcast([128, 8, 64]), ALU.add, [k_('Y'), 'rsm'], [k_('Y')])
            vtt(R['t0'][:], R['Y'][:], R['Y'][:], ALU.mult, [k_('Y')], [k_('t0')])
            S.op('dve', [k_('t0')], ['rsm'],
                 lambda: nc.vector.tensor_reduce(out=rsm[:, 6, :], in_=h3(R['t0'][:]), axis=AX.X, op=ALU.add))
            S.op('dve', ['rsm'], ['rsm'],
                 lambda: nc.vector.tensor_scalar(out=rsm[:, 6, :], in0=rsm[:, 6, :], scalar1=1.0 / 64, scalar2=64e-5,
                                                 op0=ALU.mult, op1=ALU.add))
            act(rsm[:, 7, :], rsm[:, 6, :], AF.Sqrt, ['rsm'], ['rsm'])
            S.op('dve', ['rsm'], ['rsm'], lambda: nc.vector.reciprocal(out=rsm[:, 6, :], in_=rsm[:, 7, :]))
            vtt(Y3, Y3, rsm[:, 6, :].unsqueeze(2).to_broadcast([128, 8, 64]), ALU.mult, [k_('Y'), 'rsm'], [k_('Y')])
            vtt(R['Y'][:], R['Y'][:], gngbc, ALU.mult, [k_('Y'), 'rwv'], [k_('Y')])
            vtt(R['Y'][:], R['Y'][:], gnbbc, ALU.add, [k_('Y'), 'rwv'], [k_('Y')])
            vtt(h3(R['t0'][:]), V3, rsm[:, 4, :].unsqueeze(2).to_broadcast([128, 8, 64]), ALU.mult,
                [k_('V'), 'rsm'], [k_('t0')])
            vtt(R['Y'][:], R['Y'][:], R['t0'][:], ALU.add, [k_('Y'), k_('t0')], [k_('Y')])
            r0 = (lt - NT_CTX) * 128
            S.dma('sp', [k_('Y')], [], obscr[r0:r0 + 128, :], R['Y'][:])

    for s in range(NT_S):
        S.dma('pool', [], ['cwin_copy'], o_win_s[s, 0:504, :], cwin[s, 8:512, :])
    S.store_bufs.append(S.buf('cwin_copy'))
    tiles = list(range(NT)) if not _KT else [int(v) for v in _KT.split(',')]
    load_x(tiles[0])
    for ti, lt in enumerate(tiles):
        if ti + 1 < len(tiles):
            load_x(tiles[ti + 1])
        first = (lt == 0) or (lt >= NT_CTX + NT_MAIN)
        norm_transpose(lt, first)
        kv_part(lt)
        rwkv_tile(lt)
        if lt == NT_CTX + NT_MAIN - 1:
            z_last(lt, 128, o_shift[0:1, :])
        if lt >= NT_CTX + NT_MAIN:
            s = lt - NT_CTX - NT_MAIN
            z_last(lt, 8, o_shift_s[s:s + 1, :])

    psb_n = [0]

    def newpsB():
        psb_n[0] = (psb_n[0] + 1) % 6
        return psb_n[0]

    def tsc(out, in0, s1, s2, op0, op1, r, w, accum=None):
        if op1 is None:
            S.op('dve', r, w, lambda: nc.vector.tensor_scalar(out=out, in0=in0, scalar1=s1, scalar2=s2, op0=op0))
        else:
            S.op('dve', r, w, lambda: nc.vector.tensor_scalar(out=out, in0=in0, scalar1=s1, scalar2=s2, op0=op0,
                                                              op1=op1, accum_out=accum))

    def nsa_common_alloc(NC, NB, NKT, NWT, NCHMAX, alias_wb1=False, NBUF=2, NKVT=2):
        T = {'NBUF': NBUF}
        if not alias_wb1:
            T['Wb1'] = sb("Wb1", [128, 8, 1048], BF16)
            T['Wb1key'] = 'Wb1'
        T['QT'] = sb("QT", [64, 8, 128], BF16)
        T['qf'] = sb("qf", [128, 512])
        T['agf'] = sb("agf", [128, 512])
        T['gat'] = sb("gat", [128, 24])
        T['oa'] = sb("oa", [128, 512])
        T['otmp'] = sb("otmp", [128, 512])
        T['E'] = [sb("E%d" % i, [128, 512]) for i in range(NBUF)]
        T['Pb'] = [sb("Pb%d" % i, [128, 512], BF16) for i in range(NBUF)]
        T['PT'] = [sb("PT%d" % i, [128, 4, 128], BF16) for i in range(NBUF)]
        T['Pc'] = sb("Pc", [128, NC])
        T['acc'] = sb("acc", [128, 2, NC])
        T['Zt'] = sb("Zt", [128, 8, NCHMAX])
        T['Zs'] = sb("Zs", [128, 4, 8])
        T['imp'] = sb("imp", [128, 2, NB])
        T['imr'] = sb("imr", [128, NB])
        T['m8'] = sb("m8", [128, 16])
        T['selm'] = sb("selm", [128, 2, NB + 1])
        S.op('pool', [], ['selm'], lambda: nc.gpsimd.memset(T['selm'][:], 0.0))
        T['kcT'] = sb("kcT", [64, 2, NC], BF16)
        T['vc'] = sb("vc", [128, NC // 128, 2, 64], BF16)
        T['KsT'] = sb("KsT", [128, 2, NKT * 128], BF16)
        if alias_wb1:
            off = arena['ptr']
            T['Wb1'] = nc.alloc_sbuf_tensor_at("ph%d_Wb1" % arena['phase'], [128, 8, 1048], BF16, offset=off)
            T['Wb1key'] = 'Vs'
        T['Vs'] = sb("Vs", [128, NKT, 2, 64], BF16)
        T['KwT'] = sb("KwT", [64, 2, NWT * 128], BF16)
        T['Vw'] = sb("Vw", [128, NWT, 2, 64], BF16)
        T['kvt'] = [sb("kvt%d" % i, [128, 768]) for i in range(NKVT)]
        T['kvs'] = sb("kvs", [128, 2, 128])
        T['W1c'] = sb("W1c", [128, 32, 128], BF16)
        T['peT'] = sb("peT", [128, 32], BF16)
        T['w2'] = sb("w2", [128, 2, 64], BF16)
        T['b1c'] = sb("b1c", [128, 4])
        T['b2k'] = sb("b2k", [64, 2])
        T['b2v'] = sb("b2v", [128, 64])
        T['hid'] = sb("hid", [128, 512], BF16)
        T['ctxv'] = sb("ctxv", [128, 1])
        return T

    def load_cmp_weights(T):
        for kv in range(2):
            S.dma('pool', [], ['W1c'], T['W1c'][kv * 64:(kv + 1) * 64, :, :], cmp_w1[kv].rearrange("j d f -> d j f"))
            S.dma('pool', [], ['peT'], T['peT'][kv * 64:(kv + 1) * 64, :], cmp_pe[kv].rearrange("j d -> d j"),
                  allow_slow_non_contiguous=True)
            S.dma('pool', [], ['w2'], T['w2'][:, kv, :], cmp_w2[kv])
            S.dma('sp', [], ['b1c'], T['b1c'][:, kv:kv + 1], cmp_b1[kv:kv + 1, :].rearrange("o f -> f o"),
                  allow_slow_non_contiguous=True)
        S.dma('sp', [], ['b2k'], T['b2k'][:, 0:1], cmp_b2[0:1, :].rearrange("o f -> f o"), allow_slow_non_contiguous=True)
        S.dma('sp', [], ['b2v'], T['b2v'][:], cmp_b2[1:2, :].partition_broadcast(128))
        for kv in range(2):
            for jj in range(32):
                mm(7, PS[7][:, kv:kv + 1], T['W1c'][kv * 64:(kv + 1) * 64, jj, :], T['peT'][kv * 64:(kv + 1) * 64, jj:jj + 1],
                   ['W1c', 'peT'], start=(jj == 0), stop=(jj == 31))
        vtt(T['b1c'][:, 2:4], T['b1c'][:, 0:2], PS[7][:, 0:2], ALU.add, ['b1c', 'P7'], ['b1c'])
        for k in range(8):
            S.dma('pool', [], [T['Wb1key']], T['Wb1'][:, k, 0:512], w_in[k * 128:(k + 1) * 128, OFF_Q:OFF_Q + 512])
            S.dma('pool', [], [T['Wb1key']], T['Wb1'][:, k, 512:1048], w_in[k * 128:(k + 1) * 128, OFF_NG:OFF_NG + 536])

    def compress(T, ntok, NCv):
        S.op('dve', [], ['kcT'], lambda: nc.vector.memset(T['kcT'][:], 0.0))
        S.op('dve', [], ['vc'], lambda: nc.vector.memset(T['vc'][:], 0.0))
        raw = T['KsT']
        for kv in range(2):
            rows = slice(kv * 64, (kv + 1) * 64)
            for g in range(2):
                c0 = 0
                while c0 < NCv:
                    w = min(512, NCv - c0)
                    p = newpsB()
                    for jj in range(32):
                        rv = raw[rows, g, 0:ntok].rearrange("p (c j) -> p j c", j=16)
                        cc = c0 + jj // 16
                        mm(p, PS[p][:, 0:w], T['W1c'][rows, jj, :], rv[:, jj % 16, cc:cc + w],
                           ['W1c', 'KsT'], start=(jj == 0), stop=(jj == 31))
                    S.op('act', ['P%d' % p, 'b1c'], ['hid'],
                         lambda: nc.scalar.activation(out=T['hid'][:, 0:w], in_=PS[p][:, 0:w], func=AF.Silu,
                                                      bias=T['b1c'][:, 2 + kv:3 + kv]))
                    if kv == 0:
                        p2 = newpsB()
                        mm(p2, PS[p2][0:64, 0:w], T['w2'][:, 0, :], T['hid'][:, 0:w], ['w2', 'hid'])
                        S.op('act', ['P%d' % p2, 'b2k'], ['kcT'],
                             lambda: nc.scalar.activation(out=T['kcT'][:, g, c0:c0 + w], in_=PS[p2][0:64, 0:w],
                                                          func=AF.Identity, bias=T['b2k'][:, 0:1]))
                    else:
                        for ci in range((w + 127) // 128):
                            ww = min(128, w - ci * 128)
                            p2 = newpsB()
                            mm(p2, PS[p2][0:ww, 0:64], T['hid'][:, ci * 128:ci * 128 + ww], T['w2'][:, 1, :], ['w2', 'hid'])
                            vtt(T['vc'][0:ww, (c0 // 128) + ci, g, :], PS[p2][0:ww, 0:64], T['b2v'][0:ww, :], ALU.add,
                                ['P%d' % p2, 'b2v'], ['vc'])
                    c0 += w

    def kv_pipeline(T, n, loader, proc, depth):
        nk = len(T['kvt'])
        depth = min(depth, nk - 1)

        def issue(m):
            loader(m, T['kvt'][m % nk], 'kvt%d' % (m % nk))
        for m in range(min(depth, n)):
            issue(m)
        for m in range(n):
            if m + depth < n:
                issue(m + depth)
            proc(m)

    def kv_tile_to_arrays(T, src_ap, kt_sel, kt_win, do_cmp, cmp_kt, nbuf, srckeys=()):
        kvt = T['kvt'][nbuf % len(T['kvt'])]
        kk_ = 'kvt%d' % (nbuf % len(T['kvt']))
        if src_ap is None:
            pass
        elif callable(src_ap):
            src_ap(kvt, kk_)
        else:
            S.dma('sp', list(srckeys), [kk_], kvt[:], src_ap)
        if do_cmp:
            S.op('dve', [kk_], ['kvs'],
                 lambda: nc.vector.tensor_copy(out=T['kvs'][:].rearrange("p g (kv d) -> p g kv d", kv=2),
                                               in_=kvt[:, 0:256].rearrange("p (kv g d) -> p g kv d", kv=2, g=2)))
            p = newpsB()
            for g in range(2):
                S.op('pe', ['kvs', 'ident'], ['P%d' % p],
                     lambda: nc.tensor.transpose(out=PS[p][:, g * 128:(g + 1) * 128], in_=T['kvs'][:, g, :], identity=ident[:]))
            S.op('act', ['P%d' % p], ['KsT'],
                 lambda: nc.scalar.copy(out=T['KsT'][:, :, cmp_kt * 128:(cmp_kt + 1) * 128],
                                        in_=PS[p][:, 0:256].rearrange("p (g t) -> p g t", g=2)))
        if kt_sel is not None or kt_win is not None:
            p = newpsB()
            for bi, (kt, c0) in enumerate(((kt_sel, 256), (kt_win, 512))):
                if kt is None:
                    continue
                for g in range(2):
                    S.op('pe', [kk_, 'ident'], ['P%d' % p],
                         lambda: nc.tensor.transpose(out=PS[p][0:64, (bi * 2 + g) * 128:(bi * 2 + g + 1) * 128],
                                                     in_=kvt[:, c0 + g * 64:c0 + (g + 1) * 64], identity=ident[:]))
            if kt_sel is not None:
                S.op('act', ['P%d' % p], ['KsT'],
                     lambda: nc.scalar.copy(out=T['KsT'][0:64, :, kt_sel * 128:(kt_sel + 1) * 128],
                                            in_=PS[p][0:64, 0:256].rearrange("p (g t) -> p g t", g=2)))
                S.op('dve', [kk_], ['Vs'],
                     lambda: nc.vector.tensor_copy(out=T['Vs'][:, kt_sel, :, :],
                                                   in_=kvt[:, 384:512].rearrange("p (g d) -> p g d", g=2)))
            if kt_win is not None:
                S.op('act', ['P%d' % p], ['KwT'],
                     lambda: nc.scalar.copy(out=T['KwT'][:, :, kt_win * 128:(kt_win + 1) * 128],
                                            in_=PS[p][0:64, 256:512].rearrange("p (g t) -> p g t", g=2)))
                S.op('dve', [kk_], ['Vw'],
                     lambda: nc.vector.tensor_copy(out=T['Vw'][:, kt_win, :, :],
                                                   in_=kvt[:, 640:768].rearrange("p (g d) -> p g d", g=2)))

    def attn_branch(T, bri, KT, kkey, Vt, vkey, ktiles, tmasks, bmask, ctxs, keep_pc=False, key_off=0, mkey='cm', R=128):
        chunks = [ktiles[i:i + 4] for i in range(0, len(ktiles), 4)]
        Zt = T['Zt']
        NBUF = T['NBUF']
        S.op('pool', [], ['Zt'], lambda: nc.gpsimd.memset(Zt[:], 0.0))
        if keep_pc:
            S.op('pool', [], ['acc'], lambda: nc.gpsimd.memset(T['acc'][:], 0.0))
        nmm = len(ktiles)
        items = [(h, ci) for h in range(8) for ci in range(len(chunks))]
        kfirst = {}
        o = 0
        for ci, ch in enumerate(chunks):
            kfirst[ci] = o
            o += len(ch)

        def stageA(it, n):
            h, ci = it
            g = h // 4
            ch = chunks[ci]
            w = len(ch) * 128
            c0 = (ch[0] - key_off) * 128
            e = n % NBUF
            p = newpsB()
            mm(p, PS[p][0:R, 0:w], T['QT'][:, h, 0:R], KT[0:64, g, c0:c0 + w], ['QT', kkey])
            E, ek = T['E'][e], 'E%d' % e
            Pb, pk = (T['Pb'][e], 'Pb%d' % e)
            act(E[0:R, 0:w], PS[p][0:R, 0:w], AF.Exp, ['P%d' % p], [ek])
            for i, kt in enumerate(ch):
                if kt in tmasks:
                    vtt(E[0:R, i * 128:(i + 1) * 128], E[0:R, i * 128:(i + 1) * 128], tmasks[kt][0:R, :], ALU.mult, [ek, mkey], [ek])
                if kt in ctxs:
                    tsc(E[0:R, i * 128:(i + 1) * 128], E[0:R, i * 128:(i + 1) * 128], T['ctxv'][0:R, 0:1], None, ALU.mult, None,
                        [ek, 'ctxv'], [ek])
            if keep_pc:
                Pout, pok = T['Pc'][0:R, c0:c0 + w], 'Pc'
            else:
                Pout, pok = Pb[0:R, 0:w], pk
            if bmask is not None:
                bm, bkeys = bmask(g, ch[0], len(ch))
                bm = bm[0:R]
                S.op('dve', [ek] + bkeys, [pok, 'Zt'],
                     lambda: nc.vector.scalar_tensor_tensor(out=Pout.rearrange("p (n k) -> p n k", k=bm.shape[2]),
                                                            in0=E[0:R, 0:w].rearrange("p (n k) -> p n k", k=bm.shape[2]),
                                                            scalar=1.0, in1=bm, op0=ALU.mult, op1=ALU.mult,
                                                            accum_out=Zt[0:R, h, ci:ci + 1]))
            else:
                tsc(Pout, E[0:R, 0:w], 1.0, 0.0, ALU.mult, ALU.add, [ek], [pok, 'Zt'], accum=Zt[0:R, h, ci:ci + 1])

        def stageB(it, n):
            h, ci = it
            ch = chunks[ci]
            w = len(ch) * 128
            c0 = (ch[0] - key_off) * 128
            e = n % NBUF
            Pb, pk = (T['Pb'][e], 'Pb%d' % e)
            pok = 'Pc' if keep_pc else pk
            pt = newpsB()
            PT, ptk = T['PT'][e], 'PT%d' % e
            if keep_pc:
                for i, kt in enumerate(ch):
                    S.op('pe', [pok, 'ident'], ['P%d' % pt],
                         lambda: nc.tensor.transpose(out=PS[pt][:, i * 128:(i + 1) * 128],
                                                     in_=T['Pc'][:, c0 + i * 128:c0 + (i + 1) * 128], identity=ident[:]))
                S.op('act', ['P%d' % pt], [ptk],
                     lambda: nc.scalar.copy(out=PT[:, 0:len(ch), :].rearrange("p n t -> p (n t)"), in_=PS[pt][:, 0:w]))
            else:
                psb = PS[pt][:].bitcast(BF16)
                for i, kt in enumerate(ch):
                    S.op('pe', [pok, 'identb'], ['P%d' % pt],
                         lambda: nc.tensor.transpose(out=psb[:, i * 128:i * 128 + R], in_=Pb[0:R, i * 128:(i + 1) * 128],
                                                     identity=identb[0:R, 0:R]))
                S.op('act', ['P%d' % pt], [ptk],
                     lambda: nc.scalar.copy(out=PT[:, 0:len(ch), 0:R],
                                            in_=psb[:, 0:w].rearrange("p (n t) -> p n t", t=128)[:, :, 0:R]))

        def stageC(it, n):
            h, ci = it
            g = h // 4
            ch = chunks[ci]
            e = n % NBUF
            PT, ptk = T['PT'][e], 'PT%d' % e
            for i, kt in enumerate(ch):
                im = kfirst[ci] + i
                mm(6, PS[6][0:R, h * 64:(h + 1) * 64], PT[:, i, 0:R], Vt[:, kt - key_off, g, :], [ptk, vkey],
                   start=(im == 0), stop=(im == nmm - 1))

        def head_post(h):
            g = h // 4
            S.op('dve', ['Zt'], ['Zs'],
                 lambda: nc.vector.tensor_reduce(out=T['Zs'][:, 0, h:h + 1], in_=Zt[:, h, 0:len(chunks)], axis=AX.X, op=ALU.add))
            tsc(T['Zs'][:, 1, h:h + 1], T['Zs'][:, 0, h:h + 1], 1e-30, None, ALU.max, None, ['Zs'], ['Zs'])
            S.op('dve', ['Zs'], ['Zs'], lambda: nc.vector.reciprocal(out=T['Zs'][:, 2, h:h + 1], in_=T['Zs'][:, 1, h:h + 1]))
            vstt(T['acc'][:, g, :], T['Pc'][:], T['Zs'][:, 2, h:h + 1], T['acc'][:, g, :], ALU.mult, ALU.add,
                 ['Pc', 'Zs', 'acc'], ['acc'])

        if keep_pc or NBUF < 3:
            for n, it in enumerate(items):
                stageA(it, n)
                stageB(it, n)
                stageC(it, n)
                if keep_pc and it[1] == len(chunks) - 1:
                    head_post(it[0])
        else:
            N = len(items)
            skB = NBUF - 1
            skC = skB + NBUF - 1
            for step in range(N + skC):
                if step < N:
                    stageA(items[step], step)
                if 0 <= step - skB < N:
                    stageB(items[step - skB], step - skB)
                if 0 <= step - skC < N:
                    stageC(items[step - skC], step - skC)
        if not keep_pc:
            S.op('dve', ['Zt'], ['Zs'],
                 lambda: nc.vector.tensor_reduce(out=T['Zs'][:, 0, :], in_=Zt[:, :, 0:len(chunks)], axis=AX.X, op=ALU.add))
            tsc(T['Zs'][:, 1, :], T['Zs'][:, 0, :], 1e-30, None, ALU.max, None, ['Zs'], ['Zs'])
            S.op('dve', ['Zs'], ['Zs'], lambda: nc.vector.reciprocal(out=T['Zs'][:, 2, :], in_=T['Zs'][:, 1, :]))
        gv = T['gat'][:].rearrange("p (h c) -> p h c", c=3)[:, :, bri]
        vtt(T['Zs'][:, 3, :], T['Zs'][:, 2, :], gv, ALU.mult, ['Zs', 'gat'], ['Zs'])
        vtt(h3(T['otmp'][:]), h3(PS[6][:, :]), T['Zs'][:, 3, :].unsqueeze(2).to_broadcast([128, 8, 64]), ALU.mult,
            ['P6', 'Zs'], ['otmp'])
        vtt(T['oa'][:], T['oa'][:], T['otmp'][:], ALU.add, ['oa', 'otmp'], ['oa'])

    def topk_select(T, NC, NB, m1, m2, mkeys):
        NQ = NC // 4
        S.op('pool', [], ['imp'], lambda: nc.gpsimd.memset(T['imp'][:], 0.0))
        for g in range(2):
            av = T['acc'][:, g, :].rearrange("p (n r) -> p n r", r=4)
            S.op('dve', ['acc'], ['imp'],
                 lambda: nc.vector.tensor_reduce(out=T['imp'][:, g, 0:NQ], in_=av, axis=AX.X, op=ALU.add))
            vtt(T['imp'][:, g, 1:NQ], T['imp'][:, g, 1:NQ], av[:, 0:NQ - 1, 3], ALU.add, ['imp', 'acc'], ['imp'])
            if NB > NQ:
                S.op('dve', ['acc'], ['imp'],
                     lambda: nc.vector.tensor_copy(out=T['imp'][:, g, NQ:NQ + 1], in_=T['acc'][:, g, NC - 1:NC]))
            vtt(T['imp'][:, g, :], T['imp'][:, g, :], m1, ALU.mult, ['imp'] + mkeys, ['imp'])
            vtt(T['imp'][:, g, :], T['imp'][:, g, :], m2, ALU.add, ['imp'] + mkeys, ['imp'])
            S.op('dve', ['imp'], ['m8'], lambda: nc.vector.max(out=T['m8'][:, 0:8], in_=T['imp'][:, g, :]))
            S.op('dve', ['imp', 'm8'], ['imr'],
                 lambda: nc.vector.match_replace(out=T['imr'][:], in_to_replace=T['m8'][:, 0:8], in_values=T['imp'][:, g, :],
                                                 imm_value=-1e30))
            S.op('dve', ['imr'], ['m8'], lambda: nc.vector.max(out=T['m8'][:, 8:16], in_=T['imr'][:]))
            tsc(T['selm'][:, g, 0:NB], T['imp'][:, g, :], T['m8'][:, 15:16], None, ALU.is_ge, None, ['imp', 'm8'], ['selm'])

    def q_part(T, lt):
        i = lt % 2
        ck = 'cs%d' % i
        norm_transpose(lt, True)
        pq = newpsB()
        project(lt, T['Wb1'], T['Wb1key'], 0, 512, pq)
        act(T['qf'][:], PS[pq][:, :], AF.Copy, ['P%d' % pq], ['qf'])
        project(lt, T['Wb1'], T['Wb1key'], 512, 24, 7)
        act(T['gat'][:], PS[7][:, 0:24], AF.Sigmoid, ['P7'], ['gat'])
        pa_ = newpsB()
        project(lt, T['Wb1'], T['Wb1key'], 536, 512, pa_)
        act(T['agf'][:], PS[pa_][:, :], AF.Silu, ['P%d' % pa_], ['agf'])
        q3 = h3(T['qf'][:])
        x1, x2 = q3[:, :, 0:32], q3[:, :, 32:64]
        rt_ = hf[:].rearrange("p (a h c) -> p a h c", a=4, h=8)
        cosb = cs[i][:, 0:32].unsqueeze(1).to_broadcast([128, 8, 32])
        sinb = cs[i][:, 32:64].unsqueeze(1).to_broadcast([128, 8, 32])
        vtt(rt_[:, 0], x1, cosb, ALU.mult, ['qf', ck], ['hf'])
        vtt(rt_[:, 1], x2, sinb, ALU.mult, ['qf', ck], ['hf'])
        vtt(rt_[:, 2], x2, cosb, ALU.mult, ['qf', ck], ['hf'])
        vtt(rt_[:, 3], x1, sinb, ALU.mult, ['qf', ck], ['hf'])
        vtt(x1, rt_[:, 0], rt_[:, 1], ALU.subtract, ['hf'], ['qf'])
        vtt(x2, rt_[:, 2], rt_[:, 3], ALU.add, ['hf'], ['qf'])
        for half in range(2):
            p = newpsB()
            for hh in range(4):
                h = half * 4 + hh
                S.op('pe', ['qf', 'ident'], ['P%d' % p],
                     lambda: nc.tensor.transpose(out=PS[p][0:64, hh * 128:(hh + 1) * 128], in_=T['qf'][:, h * 64:(h + 1) * 64],
                                                 identity=ident[:]))
            act(T['QT'][:, half * 4:(half + 1) * 4, :].rearrange("p h t -> p (h t)"), PS[p][0:64, :], AF.Copy,
                ['P%d' % p], ['QT'], scale=0.125)
        S.op('pool', [], ['oa'], lambda: nc.gpsimd.memset(T['oa'][:], 0.0))

    def finish_tile(T, row0):
        vtt(T['otmp'][:], T['oa'][:], T['agf'][:], ALU.mult, ['oa', 'agf'], ['otmp'])
        S.dma('sp', ['otmp'], [], oascr[row0:row0 + 128, :], T['otmp'][:])

    Mup_, Caus_ = cm[:, 3 * 128:4 * 128], cm[:, 5 * 128 + 3:6 * 128 + 3]

    def phase_b1_prompt():
        S.barrier()
        arena_reset()
        T = nsa_common_alloc(256, 64, 32, 32, 8, NBUF=4, NKVT=4)
        cmk = sb("cmk", [128, NT_MAIN, 256])
        m12 = sb("m12", [128, 2, NT_MAIN, 64])
        load_cmp_weights(T)
        S.dma('sp', [], ['cmk'], cmk[:], cmaskP[:, :, :])
        S.dma('sp', [], ['m12'], m12[:, 0], tkm1[:, :, :])
        S.dma('sp', [], ['m12'], m12[:, 1], tkm2[:, :, :])
        S.dma('sp', [], ['ctxv'], T['ctxv'][:], ctxv_d[:, :])
        ldp = lambda m, kvt, key: S.dma('sp', [], [key], kvt[:], kvscr[m * 128:(m + 1) * 128, :])
        kv_pipeline(T, 32, ldp, lambda m: kv_tile_to_arrays(T, None, None, None, True, m, m), 3)
        compress(T, 4096, 255)
        kv_pipeline(T, 32, ldp, lambda m: kv_tile_to_arrays(T, None, m, m, False, 0, m), 3)
        tiles = list(range(NT_MAIN)) if not _KT else [0, 15]
        load_x(NT_CTX + tiles[0])
        for ti, j in enumerate(tiles):
            lt = NT_CTX + j
            if ti + 1 < len(tiles):
                load_x(NT_CTX + tiles[ti + 1])
            q_part(T, lt)
            cmv = cmk[:, j, :]
            attn_branch(T, 0, T['kcT'], 'kcT', T['vc'], 'vc', [0, 1], {}, lambda g, k0, n: (cmv[:, k0 * 128:(k0 + n) * 128].unsqueeze(2), ['cmk']),
                        set(), keep_pc=True)
            topk_select(T, 256, 64, m12[:, 0, j, :], m12[:, 1, j, :], ['m12'])
            kts = list(range(0, lt + 1))
            attn_branch(T, 1, T['KsT'], 'KsT', T['Vs'], 'Vs', kts, {lt: Caus_},
                        lambda g, k0, n: (T['selm'][:, g, 2 * k0:2 * (k0 + n)].unsqueeze(2).to_broadcast([128, 2 * n, 64]), ['selm']),
                        set())
            wts = list(range(lt - 4, lt + 1))
            attn_branch(T, 2, T['KwT'], 'KwT', T['Vw'], 'Vw', wts, {lt - 4: Mup_, lt: Caus_}, None,
                        set(k for k in wts if k < NT_CTX))
            finish_tile(T, j * 128)

    def phase_b1_sample(s_i):
        S.barrier()
        arena_reset()
        T = nsa_common_alloc(1024, 257, 129, 5, 33, alias_wb1=True, NBUF=3, NKVT=4)
        m7 = sb("m7", [128, 128])
        m12 = sb("m12s", [128, 2, 257])
        pti = sb("pti", [128, 128], I32)
        ptf = sb("ptf", [128, 128])
        idx = sb("idx", [128, 128], I32)
        iop = sb("iop", [128, 1])
        load_cmp_weights(T)
        S.op('dve', [], ['m7'], lambda: nc.vector.memset(m7[:], 1.0))
        S.op('dve', [], ['m7'], lambda: nc.vector.memset(m7[:, 127:128], 0.0))
        S.dma('sp', [], ['m12s'], m12[:], tkmS[:, :, :])
        S.dma('sp', [], ['iop'], iop[:], iotap[:, :])
        S.dma('sp', [], ['pti'], pti[:], ptab[s_i:s_i + 1, :].partition_broadcast(128))
        S.op('dve', ['pti'], ['ptf'], lambda: nc.vector.tensor_copy(out=ptf[:], in_=pti[:]))
        tsc(ptf[:], ptf[:], 128.0, iop[:, 0:1], ALU.mult, ALU.add, ['ptf', 'iop'], ['ptf'])
        S.op('dve', ['ptf'], ['idx'], lambda: nc.vector.tensor_copy(out=idx[:], in_=ptf[:]))
        lt = NT_CTX + NT_MAIN + s_i
        load_x(lt)
        q_part(T, lt)
        kv_pipeline(T, 128, lambda m, kvt, key: S.idma(['idx'], [key], kvt[:, 0:256], ccmp[:, :], idx[:, m:m + 1]),
                    lambda m: kv_tile_to_arrays(T, None, None, None, True, m, m), 3)
        compress(T, 16384, 1023)
        kv_pipeline(T, 128, lambda m, kvt, key: S.idma(['idx'], [key], kvt[:, 256:512], csel[:, :], idx[:, m:m + 1]),
                    lambda m: kv_tile_to_arrays(T, None, m, None, False, 0, m), 3)
        for r_ in range(4):
            kv_tile_to_arrays(T, lambda kvt, kk_: S.dma('sp', [], [kk_], kvt[:, 512:768], cwin[s_i, r_ * 128:(r_ + 1) * 128, :]),
                              None, r_, False, 0, r_)
        kv_tile_to_arrays(T, kvscr[lt * 128:(lt + 1) * 128, :], 128, 4, False, 0, 0)
        attn_branch(T, 0, T['kcT'], 'kcT', T['vc'], 'vc', list(range(8)), {7: m7[:]}, None, set(), keep_pc=True, mkey='m7')
        topk_select(T, 1024, 257, m12[:, 0, :], m12[:, 1, :], ['m12s'])
        attn_branch(T, 1, T['KsT'], 'KsT', T['Vs'], 'Vs', list(range(129)), {128: Caus_},
                    lambda g, k0, n: (T['selm'][:, g, 2 * k0:2 * (k0 + n)].unsqueeze(2).to_broadcast([128, 2 * n, 64]), ['selm']),
                    set(), R=32)
        attn_branch(T, 2, T['KwT'], 'KwT', T['Vw'], 'Vw', list(range(5)), {0: Mup_, 4: Caus_}, None, set(), R=32)
        finish_tile(T, (NT_MAIN + s_i) * 128)

    def phase_b2():
        S.barrier()
        arena_reset()
        Wb2 = sb("Wb2", [128, 8, 2560], BF16)
        Wpa = sb("Wpa", [128, 4, 1024], BF16)
        Wpb = sb("Wpb", [128, 4, 1024], BF16)
        Wo = sb("Wo", [128, 8, 1024], BF16)
        Wg = sb("Wg", [128, 8, 1024], BF16)
        Wpp = sb("Wpp", [128, 2, 1024], BF16)
        g2bc = sb("g2bc", [128, D])
        g3bc = sb("g3bc", [128, D])
        oat = sb("oat", [128, 512])
        obt = sb("obt", [128, 512])
        rgs = sb("rgs", [128, 512])
        aT = sb("aT", [128, 4, 128], BF16)
        bT = sb("bT", [128, 4, 128], BF16)
        mg = sb("mg", [128, 2048])
        mm_ = sb("mm_", [128, D])
        mT = sb("mT", [128, 8, 128], BF16)
        x1 = sb("x1", [128, D])
        pt_ = sb("pt_", [128, 256])
        pT = sb("pT", [128, 2, 128], BF16)
        gt = sb("gt", [128, D])
        yo = sb("yo", [128, D])
        for k in range(8):
            S.dma('pool', [], ['Wb2'], Wb2[:, k, :], w_in[k * 128:(k + 1) * 128, OFF_RG:OFF_RG + 2560])
            S.dma('pool', [], ['Wo'], Wo[:, k, :], w_out[k * 128:(k + 1) * 128, :])
            S.dma('pool', [], ['Wg'], Wg[:, k, :], w_pg[k * 128:(k + 1) * 128, :])
        for k in range(4):
            S.dma('pool', [], ['Wpa'], Wpa[:, k, :], w_pa[k * 128:(k + 1) * 128, :])
            S.dma('pool', [], ['Wpb'], Wpb[:, k, :], w_pb[k * 128:(k + 1) * 128, :])
        for k in range(2):
            S.dma('pool', [], ['Wpp'], Wpp[:, k, :], w_pp[k * 128:(k + 1) * 128, :])
        S.dma('sp', [], ['g2bc'], g2bc[:], ple_g.partition_broadcast(128))
        S.dma('sp', [], ['g3bc'], g3bc[:], fin_g.partition_broadcast(128))

        def transp(src, skey, nk, dstT, dkey):
            for half in range((nk + 3) // 4):
                n = min(4, nk - half * 4)
                p = newpsB()
                for i in range(n):
                    kx = half * 4 + i
                    S.op('pe', [skey, 'ident'], ['P%d' % p],
                         lambda: nc.tensor.transpose(out=PS[p][:, i * 128:(i + 1) * 128], in_=src[:, kx * 128:(kx + 1) * 128],
                                                     identity=ident[:]))
                S.op('act', ['P%d' % p], [dkey],
                     lambda: nc.scalar.copy(out=dstT[:, half * 4:half * 4 + n, :].rearrange("p n t -> p (n t)"), in_=PS[p][:, 0:n * 128]))

        def rms(src, skey, gb, gkey, dst, dkey):
            S.op('act', [skey], ['junk', 'ss'],
                 lambda: nc.scalar.activation(out=junk[:], in_=src[:], func=AF.Square, accum_out=ss[:, 0:1]))
            tsc(ss[:, 1:2], ss[:, 0:1], 1.0 / D, 1e-6, ALU.mult, ALU.add, ['ss'], ['ss'])
            act(ss[:, 2:3], ss[:, 1:2], AF.Sqrt, ['ss'], ['ss'])
            S.op('dve', ['ss'], ['ss'], lambda: nc.vector.reciprocal(out=ss[:, 3:4], in_=ss[:, 2:3]))
            vstt(dst[:], src[:], ss[:, 3:4], gb[:], ALU.mult, ALU.mult, [skey, 'ss', gkey], [dkey])

        tiles = list(range(NT_MAIN + NT_S)) if not _KT else [0, 15, 16]
        load_x(NT_CTX + tiles[0])
        for ti, j in enumerate(tiles):
            lt = NT_CTX + j
            i = lt % 2
            xk = 'xt%d' % i
            if ti + 1 < len(tiles):
                load_x(NT_CTX + tiles[ti + 1])
            S.dma('sp', [], ['oat'], oat[:], oascr[j * 128:(j + 1) * 128, :])
            S.dma('sp', [], ['obt'], obt[:], obscr[j * 128:(j + 1) * 128, :])
            S.dma('sp', [], ['pt_'], pt_[:], p_loc[j * 128:(j + 1) * 128, :])
            norm_transpose(lt, True)
            pr_ = newpsB()
            project(lt, Wb2, 'Wb2', 0, 512, pr_)
            act(rgs[:], PS[pr_][:, :], AF.Silu, ['P%d' % pr_], ['rgs'])
            for q4 in range(4):
                pm = newpsB()
                project(lt, Wb2, 'Wb2', 512 + q4 * 512, 512, pm)
                act(mg[:, q4 * 512:(q4 + 1) * 512], PS[pm][:, :], AF.Sigmoid, ['P%d' % pm], ['mg'])
            vtt(obt[:], obt[:], rgs[:], ALU.mult, ['obt', 'rgs'], ['obt'])
            transp(oat, 'oat', 4, aT, 'aT')
            transp(obt, 'obt', 4, bT, 'bT')
            for cg in range(2):
                pa2 = newpsB()
                for k in range(4):
                    mm(pa2, PS[pa2][:, :], aT[:, k, :], Wpa[:, k, cg * 512:(cg + 1) * 512], ['aT', 'Wpa'], start=(k == 0), stop=(k == 3))
                vtt(mm_[:, cg * 512:(cg + 1) * 512], PS[pa2][:, :], mg[:, cg * 512:(cg + 1) * 512], ALU.mult, ['P%d' % pa2, 'mg'], ['mm_'])
                pb2 = newpsB()
                for k in range(4):
                    mm(pb2, PS[pb2][:, :], bT[:, k, :], Wpb[:, k, cg * 512:(cg + 1) * 512], ['bT', 'Wpb'], start=(k == 0), stop=(k == 3))
                vtt(yo[:, cg * 512:(cg + 1) * 512], PS[pb2][:, :], mg[:, 1024 + cg * 512:1024 + (cg + 1) * 512], ALU.mult,
                    ['P%d' % pb2, 'mg'], ['yo'])
            vtt(mm_[:], mm_[:], yo[:], ALU.add, ['mm_', 'yo'], ['mm_'])
            transp(mm_, 'mm_', 8, mT, 'mT')
            for cg in range(2):
                po = newpsB()
                for k in range(8):
                    mm(po, PS[po][:, :], mT[:, k, :], Wo[:, k, cg * 512:(cg + 1) * 512], ['mT', 'Wo'], start=(k == 0), stop=(k == 7))
                vtt(x1[:, cg * 512:(cg + 1) * 512], PS[po][:, :], xt[i][:, cg * 512:(cg + 1) * 512], ALU.add, ['P%d' % po, xk], ['x1'])
            rms(x1, 'x1', g2bc, 'g2bc', mm_, 'mm_')
            transp(mm_, 'mm_', 8, mT, 'mT')
            transp(pt_, 'pt_', 2, pT, 'pT')
            for cg in range(2):
                pg_ = newpsB()
                for k in range(8):
                    mm(pg_, PS[pg_][:, :], mT[:, k, :], Wg[:, k, cg * 512:(cg + 1) * 512], ['mT', 'Wg'], start=(k == 0), stop=(k == 7))
                act(gt[:, cg * 512:(cg + 1) * 512], PS[pg_][:, :], AF.Sigmoid, ['P%d' % pg_], ['gt'])
                pp_ = newpsB()
                for k in range(2):
                    mm(pp_, PS[pp_][:, :], pT[:, k, :], Wpp[:, k, cg * 512:(cg + 1) * 512], ['pT', 'Wpp'], start=(k == 0), stop=(k == 1))
                vtt(gt[:, cg * 512:(cg + 1) * 512], gt[:, cg * 512:(cg + 1) * 512], PS[pp_][:, :], ALU.mult, ['gt', 'P%d' % pp_], ['gt'])
            vtt(x1[:], x1[:], gt[:], ALU.add, ['x1', 'gt'], ['x1'])
            rms(x1, 'x1', g3bc, 'g3bc', yo, 'yo')
            if j < NT_MAIN:
                S.dma('sp', ['yo'], [], o_y[j * 128:(j + 1) * 128, :], yo[:])
            else:
                s_ = j - NT_MAIN
                S.dma('sp', ['yo'], [], o_y_s[s_ * 8:(s_ + 1) * 8, :], yo[0:8, :])

    if _KCUT >= 50:
        _sk = os.environ.get('KSKIP', '')
        if 'b1p' not in _sk:
            phase_b1_prompt()
        if 'b1s' not in _sk:
            for s_i in range(NT_S if not _KT else 1):
                phase_b1_sample(s_i)
        if 'b2' not in _sk:
            phase_b2()
    if os.environ.get('KVERB'):
        print("last phase used", arena['ptr'] - arena['lo'], "free", nc.sbuf_top - arena['ptr'], "counts", S.ecnt, flush=True)
    S.finish('sp')
    return nc


_PROG = None


def _rope_table(pos):
    half = 32
    inv = (np.float32(10000.0) ** (-np.arange(half, dtype=np.float32) / np.float32(half))).astype(np.float32)
    ang = pos.astype(np.float32)[:, None] * inv[None, :]
    return np.concatenate([np.cos(ang), np.sin(ang)], axis=1).astype(np.float32)


def kernel(**inp):
    global _PROG
    f = lambda a: np.ascontiguousarray(np.asarray(a))
    x_prompt = f(inp['x_prompt'])
    x_sample = f(inp['x_sample'])
    B, SEQ = x_prompt.shape[:2]
    HALF = SEQ // 2
    PAST = inp['page_table'].shape[1] * 128
    ccmp_full = f(inp['cache_cmp_kv'])[0].reshape(-1, 256)
    csel_full = f(inp['cache_sel_kv'])[0].reshape(-1, 256)
    page_table = f(inp['page_table']).astype(np.int32)
    trim = bool(os.environ.get('KPOOLTRIM'))
    NPOOL = 512 if trim else ccmp_full.shape[0] // 128
    if _PROG is None or _PROG[0] != NPOOL:
        _PROG = (NPOOL, build_program(NPOOL))
    nc = _PROG[1]
    in_maps = []
    ident = np.eye(128, dtype=np.float32)
    ii = np.arange(128)
    cdec = np.float32(-np.exp(-0.5))
    triu = (ii[:, None] <= ii[None, :]).astype(np.float32)
    trius = (ii[:, None] < ii[None, :]).astype(np.float32)
    lastsel = np.zeros((128, 3), np.float32)
    lastsel[:8, 2] = 1.0
    lastsel[:, 0] = cdec
    lastsel[:8, 1] = cdec
    cmat = np.concatenate([triu * cdec, trius * cdec, (ii[:, None] > ii[None, :]).astype(np.float32),
                           trius, triu, lastsel, (ii[:, None] >= ii[None, :]).astype(np.float32)], axis=1).astype(np.float32)
    e0row = np.zeros((1, 128), np.float32)
    e0row[0, 0] = 1.0
    rw_vec = np.stack([f(inp[k]).reshape(512) for k in
                       ['rwkv_w0', 'rwkv_a0', 'rwkv_k_k', 'rwkv_k_a', 'rwkv_r_k', 'rwkv_gn_g', 'rwkv_gn_b']]).astype(np.float32)
    tt = np.arange(128)[:, None, None]
    jj_ = np.arange(NT_MAIN)[None, :, None]
    L = 2048 + 128 * jj_ + tt
    cmaskP, tkm1, tkm2 = [], [], []
    for half in range(2):
        cl = np.arange(256)[None, None, :]
        ok = (16 * cl + 31 <= L) & ((half == 1) | (cl >= 128)) & (cl < 255)
        cmaskP.append(ok.astype(np.float32))
        n = np.arange(64)[None, None, :]
        cur = L // 64
        first = 32 * (1 - half)
        forced = (n == first) | (n == cur) | ((n == cur - 1) & (cur - 1 >= first))
        future = n * 64 > L
        invalid = n < first
        m1 = np.where(forced | future | invalid, 0.0, 1.0)
        m2 = np.where(forced, 1e4, 0.0)
        m2 = np.where(future, -1.0, m2)
        m2 = np.where(invalid, -2.0, m2)
        tkm1.append(m1.astype(np.float32))
        tkm2.append(m2.astype(np.float32))
    cl = (np.arange(2)[None, :, None] * 128 + np.arange(128)[:, None, None])
    nn = np.arange(64)[None, None, :]
    ovP = ((16 * cl <= 64 * nn + 63) & (16 * cl + 31 >= 64 * nn)).astype(np.float32)
    ns = np.arange(257)[None, :]
    forced_s = (ns == 0) | (ns == 256) | (ns == 255)
    tkmS = np.stack([np.broadcast_to(np.where(forced_s, 0.0, 1.0), (128, 257)),
                     np.broadcast_to(np.where(forced_s, 1e4, 0.0), (128, 257))], axis=1).astype(np.float32)
    cls = (np.arange(8)[None, :, None] * 128 + np.arange(128)[:, None, None])
    nns = np.arange(257)[None, None, :]
    ovS = ((16 * cls <= 64 * nns + 63) & (16 * cls + 31 >= 64 * nns) & (cls < 1023)).astype(np.float32)
    rw_up = np.stack([f(inp['rwkv_w_up'])[0], f(inp['rwkv_a_up'])[0]]).astype(np.float32)
    for c in range(NCORES):
        b, half = c // 2, c % 2
        xl = np.zeros((NT * 128, D), np.float32)
        if half == 1:
            xl[0:HALF] = x_prompt[b, 0:HALF]
        xl[HALF:2 * HALF] = x_prompt[b, half * HALF:(half + 1) * HALF]
        pos = np.zeros((NT * 128,), np.float32)
        pos[0:2 * HALF] = np.arange(2 * HALF) + (half - 1) * HALF
        for s in range(NT_S):
            r0 = (NT_CTX + NT_MAIN + s) * 128
            xl[r0:r0 + 8] = x_sample[4 * c + s]
            pos[r0:r0 + 8] = PAST + np.arange(8)
        p_loc = np.zeros(((NT_MAIN + NT_S) * 128, 256), np.float32)
        p_loc[0:HALF] = f(inp['p_prompt'])[0, b, half * HALF:(half + 1) * HALF]
        for s in range(NT_S):
            p_loc[(NT_MAIN + s) * 128:(NT_MAIN + s) * 128 + 8] = f(inp['p_sample'])[0, 4 * c + s]
        ptab_c = page_table[4 * c:4 * c + 4]
        if trim:
            pages = ptab_c.reshape(-1)
            ccmp_c = ccmp_full.reshape(-1, 128, 256)[pages].reshape(-1, 256)
            csel_c = csel_full.reshape(-1, 128, 256)[pages].reshape(-1, 256)
            ptab_c = np.arange(512, dtype=np.int32).reshape(4, 128)
        else:
            ccmp_c, csel_c = ccmp_full, csel_full
        m = {
            'x_loc': xl,
            'rope_cs': _rope_table(pos),
            'ident': ident,
            'norm_g': f(inp['norm_g']).reshape(1, D),
            'w_in': f(inp['w_in'])[0],
            'rwkv_mu': f(inp['rwkv_mu']).reshape(1, C_R),
            'sshift': f(inp['state_shift'])[0, 4 * c:4 * c + 4],
            'cmat': cmat, 'e0row': e0row,
            'rw_vec': rw_vec, 'rw_up': rw_up,
            'swkv': f(inp['state_wkv'])[0, 4 * c:4 * c + 4],
            'cmp_w1': f(inp['cmp_w1'])[0], 'cmp_pe': f(inp['cmp_pe'])[0], 'cmp_w2': f(inp['cmp_w2'])[0],
            'cmp_b1': f(inp['cmp_b1'])[0], 'cmp_b2': f(inp['cmp_b2'])[0],
            'cmaskP': cmaskP[half], 'tkm1': tkm1[half], 'tkm2': tkm2[half], 'ovP': ovP,
            'ctxv': np.full((128, 1), float(half), np.float32),
            'tkmS': tkmS, 'ovS': ovS, 'iotap': np.arange(128, dtype=np.float32).reshape(128, 1),
            'ptab': ptab_c, 'ccmp': ccmp_c, 'csel': csel_c,
            'w_pa': f(inp['w_pa'])[0], 'w_pb': f(inp['w_pb'])[0], 'w_out': f(inp['w_out'])[0],
            'w_pg': f(inp['w_ple_gate'])[0], 'w_pp': f(inp['w_ple_proj'])[0],
            'ple_g': f(inp['ple_norm_g']).reshape(1, D), 'fin_g': f(inp['final_norm_g']).reshape(1, D),
            'p_loc': p_loc,
            'cwin': f(inp['cache_win_kv'])[0, 4 * c:4 * c + 4].reshape(NT_S, 512, 256),
        }
        in_maps.append(m)
    if os.environ.get('KTRACE'):
        import time as _t
        _t0 = _t.time()
        _r = run_bass_kernel_spmd(nc, in_maps, core_ids=list(range(NCORES)), trace=True)
        print("KTRACE exec_time_ns", _r.exec_time_ns, "wall", _t.time() - _t0, flush=True)
        res = _r.results
    else:
        res = run_bass_kernel_spmd(nc, in_maps, core_ids=list(range(NCORES))).results
    if _KDBG:
        DBG['ob'] = [r['obscr'] for r in res]
        DBG['oa'] = [r['oascr'] for r in res]
    DB, DS = x_sample.shape[:2]
    y_p = np.zeros((B, SEQ, D), np.float32)
    y_s = np.zeros((DB, DS, D), np.float32)
    cmp_p = np.zeros((1, B, SEQ, 2, 2, 64), np.float32)
    sel_p = np.zeros((1, B, SEQ, 2, 2, 64), np.float32)
    win_p = np.zeros((1, B, 512, 2, 2, 64), np.float32)
    cmp_s = np.zeros((1, DB, DS, 2, 2, 64), np.float32)
    sel_s = np.zeros((1, DB, DS, 2, 2, 64), np.float32)
    win_s = np.zeros((1, DB, 512, 2, 2, 64), np.float32)
    wkv_p = np.zeros((1, B, 8, 64, 64), np.float32)
    wkv_s = np.zeros((1, DB, 8, 64, 64), np.float32)
    sh_p = np.zeros((1, B, C_R), np.float32)
    sh_s = np.zeros((1, DB, C_R), np.float32)
    for c in range(NCORES):
        b, half = c // 2, c % 2
        r = res[c]
        sl = slice(half * HALF, (half + 1) * HALF)
        y_p[b, sl] = r['o_y']
        y_s[4 * c:4 * c + 4] = r['o_y_s'].reshape(4, 8, D)
        cmp_p[0, b, sl] = r['o_cmp'].reshape(HALF, 2, 2, 64)
        sel_p[0, b, sl] = r['o_sel'].reshape(HALF, 2, 2, 64)
        if half == 1:
            win_p[0, b] = r['o_win'].reshape(512, 2, 2, 64)
            sh_p[0, b] = r['o_shift'][0]
        if half == 1:
            wkv_p[0, b] = r['o_wkv']
        wkv_s[0, 4 * c:4 * c + 4] = r['o_wkv_s']
        cmp_s[0, 4 * c:4 * c + 4] = r['o_cmp_s'].reshape(4, 8, 2, 2, 64)
        sel_s[0, 4 * c:4 * c + 4] = r['o_sel_s'].reshape(4, 8, 2, 2, 64)
        win_s[0, 4 * c:4 * c + 4] = r['o_win_s'].reshape(4, 512, 2, 2, 64)
        sh_s[0, 4 * c:4 * c + 4] = r['o_shift_s']
    return (y_p, y_s, cmp_p, cmp_s, sel_p, sel_s, win_p, win_s, wkv_p, wkv_s, sh_p, sh_s)
```

```python
import numpy as np
import concourse.bass as bass
import concourse.mybir as mybir
from concourse.bass_utils import run_bass_kernel_spmd

F32 = mybir.dt.float32
BF16 = mybir.dt.bfloat16
I32 = mybir.dt.int32
AF = mybir.ActivationFunctionType
ALU = mybir.AluOpType
AX = mybir.AxisListType

NCORES = 8
D = 1024
NT_CTX = 16
NT_MAIN = 16
NT_S = 4
NT = NT_CTX + NT_MAIN + NT_S
C_R = 1664
OFF_Q, OFF_KV, OFF_NG, OFF_AG, OFF_RZ, OFF_RG, OFF_MG = 0, 512, 1280, 1304, 1816, 3480, 3992
SYNC_SAME = ('dve', 'act', 'pool')


class Buf:
    def __init__(self, name):
        self.name = name
        self.lw = None
        self.rd = []
        self.sem = None
        self.semcnt = 0


class Sched:
    def __init__(self, nc):
        self.nc = nc
        self.eng = {'pe': nc.tensor, 'dve': nc.vector, 'act': nc.scalar, 'pool': nc.gpsimd, 'sp': nc.sync}
        self.esem = {k: nc.alloc_semaphore(name='es_' + k) for k in self.eng}
        self.ecnt = {k: 0 for k in self.eng}
        self.waited = {k: {} for k in self.eng}
        self.bufs = {}
        self.store_bufs = []
        self.nsem = 0

    def buf(self, name):
        b = self.bufs.get(name)
        if b is None:
            b = self.bufs[name] = Buf(name)
        return b

    def _wait(self, e, ev):
        sem, cnt, src = ev
        if src == e and e not in SYNC_SAME:
            return
        key = id(sem)
        if self.waited[e].get(key, 0) >= cnt:
            return
        self.eng[e].wait_ge(sem, cnt)
        self.waited[e][key] = cnt

    def _deps(self, e, reads, writes):
        for b in reads:
            b = self.buf(b)
            if b.lw is not None:
                self._wait(e, b.lw)
        for b in writes:
            b = self.buf(b)
            if b.lw is not None:
                self._wait(e, b.lw)
            for ev in b.rd:
                self._wait(e, ev)

    def op(self, e, reads, writes, fn):
        self._deps(e, reads, writes)
        ins = fn()
        self.ecnt[e] += 1
        ins.then_inc(self.esem[e], 1)
        ev = (self.esem[e], self.ecnt[e], e)
        for b in writes:
            b = self.buf(b)
            b.lw = ev
            b.rd = []
        for b in reads:
            if b not in writes:
                self.buf(b).rd.append(ev)
        return ins

    def dma(self, q, reads, writes, out, in_, **kw):
        self._deps(q, reads, writes)
        ob = self.buf(writes[0]) if writes else self.buf(reads[0])
        if ob.sem is None:
            ob.sem = self.nc.alloc_semaphore(name='ds_%d' % self.nsem)
            self.nsem += 1
        ins = self.eng[q].dma_start(out=out, in_=in_, **kw)
        ob.semcnt += 16
        ins.then_inc(ob.sem, 16)
        ev = (ob.sem, ob.semcnt, None)
        for b in writes:
            b = self.buf(b)
            b.lw = ev
            b.rd = []
        for b in reads:
            self.buf(b).rd.append(ev)
        if not writes:
            self.store_bufs.append(ob)
        return ins

    def idma(self, reads, writes, out, in_, idx_ap):
        q = 'pool'
        self._deps(q, reads, writes)
        ob = self.buf(writes[0])
        if ob.sem is None:
            ob.sem = self.nc.alloc_semaphore(name='ds_%d' % self.nsem)
            self.nsem += 1
        ins = self.nc.gpsimd.indirect_dma_start(out=out, out_offset=None, in_=in_,
                                                in_offset=bass.IndirectOffsetOnAxis(ap=idx_ap, axis=0))
        ob.semcnt += 16
        ins.then_inc(ob.sem, 16)
        ev = (ob.sem, ob.semcnt, None)
        for b in writes:
            b = self.buf(b)
            b.lw = ev
            b.rd = []
        for b in reads:
            self.buf(b).rd.append(ev)
        return ins

    def barrier(self):
        dsems = [(b.sem, b.semcnt) for b in self.bufs.values() if b.sem is not None]
        for e in self.eng:
            for o in self.eng:
                if o != e and self.ecnt[o] > self.waited[e].get(id(self.esem[o]), 0):
                    self.eng[e].wait_ge(self.esem[o], self.ecnt[o])
                    self.waited[e][id(self.esem[o])] = self.ecnt[o]
            for sem, cnt in dsems:
                if cnt > self.waited[e].get(id(sem), 0):
                    self.eng[e].wait_ge(sem, cnt)
                    self.waited[e][id(sem)] = cnt

    def finish(self, e='sp'):
        seen = set()
        for ob in self.store_bufs:
            if id(ob) in seen:
                continue
            seen.add(id(ob))
            self.eng[e].wait_ge(ob.sem, ob.semcnt)


import os
_KT = os.environ.get('KTILES')
_KCUT = int(os.environ.get('KCUT', '99'))
_KDBG = bool(os.environ.get('KDBG'))
DBG = {}


def build_program(NPOOL):
    nc = bass.Bass("TRN2", target_bir_lowering=False)
    S = Sched(nc)

    def din(name, shape, dt=F32):
        return nc.dram_tensor(name, list(shape), dt, kind="ExternalInput").ap()

    def dout(name, shape, dt=F32):
        return nc.dram_tensor(name, list(shape), dt, kind="ExternalOutput").ap()

    x_loc = din("x_loc", [NT * 128, D])
    rope_cs = din("rope_cs", [NT * 128, 64])
    ident_d = din("ident", [128, 128])
    norm_g = din("norm_g", [1, D])
    w_in = din("w_in", [D, 6040])
    rwkv_mu = din("rwkv_mu", [1, C_R])
    sshift = din("sshift", [NT_S, C_R])
    cwin = din("cwin", [NT_S, 512, 256])

    cmat = din("cmat", [128, 5 * 128 + 3 + 128])
    e0row_d = din("e0row", [1, 128])
    rw_vec = din("rw_vec", [7, 512])
    rw_up = din("rw_up", [2, 64, 512])
    swkv = din("swkv", [NT_S, 8, 64, 64])
    cmp_w1 = din("cmp_w1", [2, 32, 64, 128])
    cmp_pe = din("cmp_pe", [2, 32, 64])
    cmp_w2 = din("cmp_w2", [2, 128, 64])
    cmp_b1 = din("cmp_b1", [2, 128])
    cmp_b2 = din("cmp_b2", [2, 64])
    cmaskP = din("cmaskP", [128, NT_MAIN, 256])
    tkm1 = din("tkm1", [128, NT_MAIN, 64])
    tkm2 = din("tkm2", [128, NT_MAIN, 64])
    ovP = din("ovP", [128, 2, 64])
    ctxv_d = din("ctxv", [128, 1])
    tkmS = din("tkmS", [128, 2, 257])
    ovS = din("ovS", [128, 8, 257])
    iotap = din("iotap", [128, 1])
    ptab = din("ptab", [NT_S, 128], I32)
    ccmp = din("ccmp", [NPOOL * 128, 256])
    csel = din("csel", [NPOOL * 128, 256])
    w_pa = din("w_pa", [512, D])
    w_pb = din("w_pb", [512, D])
    w_out = din("w_out", [D, D])
    w_pg = din("w_pg", [D, D])
    w_pp = din("w_pp", [256, D])
    ple_g = din("ple_g", [1, D])
    fin_g = din("fin_g", [1, D])
    p_loc = din("p_loc", [(NT_MAIN + NT_S) * 128, 256])
    o_y = dout("o_y", [2048, D])
    o_y_s = dout("o_y_s", [NT_S * 8, D])

    o_wkv = dout("o_wkv", [8, 64, 64])
    o_wkv_s = dout("o_wkv_s", [NT_S, 8, 64, 64])
    kvscr = nc.dram_tensor("kvscr", [NT * 128, 768], F32, kind="Internal").ap()
    oascr = nc.dram_tensor("oascr", [(NT_MAIN + NT_S) * 128, 512], F32, kind="ExternalOutput" if _KDBG else "Internal").ap()
    obscr = nc.dram_tensor("obscr", [(NT_MAIN + NT_S) * 128, 512], F32, kind="ExternalOutput" if _KDBG else "Internal").ap()
    o_cmp = dout("o_cmp", [2048, 256])
    o_sel = dout("o_sel", [2048, 256])
    o_win = dout("o_win", [512, 256])
    o_shift = dout("o_shift", [1, C_R])
    o_cmp_s = dout("o_cmp_s", [NT_S * 8, 256])
    o_sel_s = dout("o_sel_s", [NT_S * 8, 256])
    o_win_s = dout("o_win_s", [NT_S, 512, 256])
    o_shift_s = dout("o_shift_s", [NT_S, C_R])

    psb = lambda name, shape, dt=F32: nc.alloc_sbuf_tensor(name, list(shape), dt)
    arena = {'ptr': None, 'lo': None, 'phase': 0}
    DTB = {F32: 4, BF16: 2, I32: 4}

    def sb(name, shape, dt=F32):
        if arena['lo'] is None:
            arena['lo'] = arena['ptr'] = (nc.sbuf_base + 63) // 64 * 64
        nbytes = int(np.prod(shape[1:])) * DTB[dt]
        off = arena['ptr']
        arena['ptr'] = (off + nbytes + 31) // 32 * 32
        assert arena['ptr'] <= nc.sbuf_top, ("SBUF overflow", name, arena['ptr'], nc.sbuf_top)
        return nc.alloc_sbuf_tensor_at("ph%d_%s" % (arena['phase'], name), list(shape), dt, offset=off)

    def arena_reset():
        if os.environ.get('KVERB'):
            print("phase", arena['phase'], "used", arena['ptr'] - arena['lo'], "free", nc.sbuf_top - arena['ptr'], flush=True)
        arena['ptr'] = arena['lo']
        arena['phase'] += 1
    ident = psb("identt", [128, 128])
    gbc = psb("gbc", [128, D])
    identb = psb("identb", [128, 128], BF16)
    xt = [psb("xt%d" % i, [128, D]) for i in range(2)]
    cs = [psb("cs%d" % i, [128, 64]) for i in range(2)]
    hf = psb("hf", [128, D])
    ss = psb("ss", [128, 4])
    hT = [psb("hT%d" % i, [128, 8, 129], BF16) for i in range(2)]
    cm = psb("cm", [128, 5 * 128 + 3 + 128])
    junk = psb("junk", [128, D], BF16)
    PS = [nc.alloc_psum_tensor("P%d" % i, [128, 512], F32) for i in range(8)]
    mubc = sb("mubc", [128, C_R])
    Wkv = sb("Wkv", [128, 8, 768], BF16)
    W1 = sb("W1", [128, 8, C_R], BF16)
    W2 = sb("W2", [128, 8, C_R], BF16)
    wstage = [sb("wstage0", [128, C_R])] * 2
    TriU, TriUs, Mlow, Mup, Mupi = [cm[:, j * 128:(j + 1) * 128] for j in range(5)]
    e0row = sb("e0rowb", [1, 128], BF16)
    e0f = sb("e0f", [1, 128])
    rwv = sb("rwv", [128, 7, 512])
    w0bc, a0bc, kkbc, kabc, rkbc, gngbc, gnbbc = [rwv[:, j, :] for j in range(7)]
    wup = sb("wup", [64, 512])
    aup = sb("aup", [64, 512])
    ssf = wstage[0][0:1, :]
    zl = ssf
    ssm = sb("ssm", [1, C_R], BF16)
    twd = sb("twd", [64, 128])
    adT = sb("adT", [64, 128])
    RW = {n: sb("rw_" + n, [128, 512]) for n in
          ['t0', 'sig', 'a', 'P', 'Pinv', 'Pm1', 'kk', 'k2', 'b', 'kkt', 'kt', 'bt', 'rt', 'V', 't1']}
    rsm = sb("rsm", [128, 8, 8])
    TT = {n: sb("tt_" + n, [64, 8, 128]) for n in ['kkt', 'kt', 'bt', 'rt']}
    MX = {n: sb("mx_" + n, [128, 8, 128]) for n in ['X0', 'X1', 'XT0', 'XT1', 'Z', 'A1T']}
    Hs = sb("Hs", [64, 8, 64])
    Sio = sb("Sio", [64, 8, 64])
    PCt = sb("PCt", [64, 8])
    kvf = [sb("kvf0", [128, 768])] * 2
    rtmp = hf[:, 0:768].rearrange("p (a b c d) -> p a b c d", a=4, b=3, c=2)

    S.dma('sp', [], ['ident'], ident[:], ident_d[:, :])
    S.op('dve', ['ident'], ['identb'], lambda: nc.vector.tensor_copy(out=identb[:], in_=ident[:]))
    S.dma('sp', [], ['gbc'], gbc[:], norm_g.partition_broadcast(128))
    S.dma('sp', [], ['mubc'], mubc[:], rwkv_mu.partition_broadcast(128))
    for k in range(8):
        S.dma('pool', [], ['Wkv'], Wkv[:, k, :], w_in[k * 128:(k + 1) * 128, OFF_KV:OFF_KV + 768])
    for k in range(8):
        ws = wstage[k % 2]
        wk = 'wstage0'
        S.dma('sp', [], [wk], ws[:], w_in[k * 128:(k + 1) * 128, OFF_RZ:OFF_RZ + C_R])
        S.op('dve', [wk, 'mubc'], ['W2'],
             lambda: nc.vector.tensor_tensor(out=W2[:, k, :], in0=ws[:], in1=mubc[:], op=ALU.mult))
        S.op('dve', [wk, 'W2'], ['W1'],
             lambda: nc.vector.tensor_tensor(out=W1[:, k, :], in0=ws[:], in1=W2[:, k, :], op=ALU.subtract))
    S.dma('sp', [], ['cm'], cm[:], cmat[:, :])
    S.dma('sp', [], ['e0f'], e0f[:], e0row_d[:, :])
    S.op('dve', ['e0f'], ['e0row'], lambda: nc.vector.tensor_copy(out=e0row[:], in_=e0f[:]))
    for j in range(7):
        S.dma('sp', [], ['rwv'], rwv[:, j, :], rw_vec[j:j + 1, :].partition_broadcast(128))
    S.dma('sp', [], ['wup'], wup[:], rw_up[0])
    S.dma('sp', [], ['aup'], aup[:], rw_up[1])
    S.op('dve', [], ['hT0'], lambda: nc.vector.memset(hT[0][:], 0.0))
    S.op('dve', [], ['hT1'], lambda: nc.vector.memset(hT[1][:], 0.0))

    def load_x(lt):
        i = lt % 2
        S.dma('sp', [], ['xt%d' % i], xt[i][:], x_loc[lt * 128:(lt + 1) * 128, :])
        S.dma('sp', [], ['cs%d' % i], cs[i][:], rope_cs[lt * 128:(lt + 1) * 128, :])

    def norm_transpose(lt, first_of_seq):
        i = lt % 2
        xk, hk = 'xt%d' % i, 'hT%d' % i
        x_ = xt[i]
        S.op('act', [xk], ['junk', 'ss'],
             lambda: nc.scalar.activation(out=junk[:], in_=x_[:], func=AF.Square, accum_out=ss[:, 0:1]))
        S.op('dve', ['ss'], ['ss'],
             lambda: nc.vector.tensor_scalar(out=ss[:, 1:2], in0=ss[:, 0:1], scalar1=1.0 / D, scalar2=1e-6,
                                             op0=ALU.mult, op1=ALU.add))
        S.op('act', ['ss'], ['ss'], lambda: nc.scalar.activation(out=ss[:, 2:3], in_=ss[:, 1:2], func=AF.Sqrt))
        S.op('dve', ['ss'], ['ss'], lambda: nc.vector.reciprocal(out=ss[:, 3:4], in_=ss[:, 2:3]))
        S.op('dve', [xk, 'ss', 'gbc'], ['hf'],
             lambda: nc.vector.scalar_tensor_tensor(out=hf[:], in0=x_[:], scalar=ss[:, 3:4], in1=gbc[:],
                                                    op0=ALU.mult, op1=ALU.mult))
        for k in range(8):
            pk = k // 4
            S.op('pe', ['hf', 'ident'], ['P%d' % pk],
                 lambda: nc.tensor.transpose(out=PS[pk][:, (k % 4) * 128:(k % 4 + 1) * 128],
                                             in_=hf[:, k * 128:(k + 1) * 128], identity=ident[:]))
        if first_of_seq:
            S.op('pool', [], [hk], lambda: nc.gpsimd.memset(hT[i][:, :, 0:1], 0.0))
        else:
            S.op('pool', ['hT%d' % (1 - i)], [hk],
                 lambda: nc.gpsimd.tensor_copy(out=hT[i][:, :, 0:1], in_=hT[1 - i][:, :, 128:129]))
        for pk in range(2):
            S.op('act', ['P%d' % pk], [hk],
                 lambda: nc.scalar.copy(out=hT[i][:, pk * 4:(pk + 1) * 4, 1:129],
                                        in_=PS[pk][:].rearrange("p (k t) -> p k t", k=4)))

    def project(lt, W, wkey, c0, ncols, pidx, shifted=False):
        i = lt % 2
        hk = 'hT%d' % i
        lo = 0 if shifted else 1
        for k in range(8):
            S.op('pe', [hk, wkey], ['P%d' % pidx],
                 lambda: nc.tensor.matmul(PS[pidx][:, 0:ncols], lhsT=hT[i][:, k, lo:lo + 128],
                                          rhs=W[:, k, c0:c0 + ncols], start=(k == 0), stop=(k == 7)))

    def kv_part(lt):
        i = lt % 2
        kk = 'kvf0'
        kv_ = kvf[i]
        for cg in range(2):
            project(lt, Wkv, 'Wkv', cg * 384, 384, 2 + cg)
            S.op('act', ['P%d' % (2 + cg)], [kk],
                 lambda: nc.scalar.copy(out=kv_[:, cg * 384:(cg + 1) * 384], in_=PS[2 + cg][:, 0:384]))
        kview = kv_[:].rearrange("p (br kv g d) -> p br kv g d", br=3, kv=2, g=2, d=64)
        x1 = kview[:, :, 0, :, 0:32]
        x2 = kview[:, :, 0, :, 32:64]
        ck = 'cs%d' % i
        cosb = cs[i][:, 0:32].unsqueeze(1).unsqueeze(1).to_broadcast([128, 3, 2, 32])
        sinb = cs[i][:, 32:64].unsqueeze(1).unsqueeze(1).to_broadcast([128, 3, 2, 32])
        S.op('dve', [kk, ck], ['hf'], lambda: nc.vector.tensor_tensor(out=rtmp[:, 0], in0=x1, in1=cosb, op=ALU.mult))
        S.op('dve', [kk, ck], ['hf'], lambda: nc.vector.tensor_tensor(out=rtmp[:, 1], in0=x2, in1=sinb, op=ALU.mult))
        S.op('dve', [kk, ck], ['hf'], lambda: nc.vector.tensor_tensor(out=rtmp[:, 2], in0=x2, in1=cosb, op=ALU.mult))
        S.op('dve', [kk, ck], ['hf'], lambda: nc.vector.tensor_tensor(out=rtmp[:, 3], in0=x1, in1=sinb, op=ALU.mult))
        S.op('dve', ['hf'], [kk], lambda: nc.vector.tensor_tensor(out=x1, in0=rtmp[:, 0], in1=rtmp[:, 1], op=ALU.subtract))
        S.op('dve', ['hf'], [kk], lambda: nc.vector.tensor_tensor(out=x2, in0=rtmp[:, 2], in1=rtmp[:, 3], op=ALU.add))
        S.dma('sp', [kk], [], kvscr[lt * 128:(lt + 1) * 128, :], kv_[:, :])
        if NT_CTX <= lt < NT_CTX + NT_MAIN:
            r0 = (lt - NT_CTX) * 128
            S.dma('sp', [kk], [], o_cmp[r0:r0 + 128, :], kv_[:, 0:256])
            S.dma('sp', [kk], [], o_sel[r0:r0 + 128, :], kv_[:, 256:512])
            if lt >= NT_CTX + NT_MAIN - 4:
                w0 = (lt - (NT_CTX + NT_MAIN - 4)) * 128
                S.dma('sp', [kk], [], o_win[w0:w0 + 128, :], kv_[:, 512:768])
        elif lt >= NT_CTX + NT_MAIN:
            s = lt - NT_CTX - NT_MAIN
            S.dma('sp', [kk], [], o_cmp_s[s * 8:(s + 1) * 8, :], kv_[0:8, 0:256])
            S.dma('sp', [kk], [], o_sel_s[s * 8:(s + 1) * 8, :], kv_[0:8, 256:512])
            S.dma('sp', [kk], [], o_win_s[s, 504:512, :], kv_[0:8, 512:768])

    def z_last(lt, col, out_ap):
        i = lt % 2
        hk = 'hT%d' % i
        for cg in range(4):
            n = 0
            for W, wk in ((W1, 'W1'), (W2, 'W2')):
                for k in range(8):
                    S.op('pe', [hk, wk], ['P4'],
                         lambda: nc.tensor.matmul(PS[4][0:1, 0:416], lhsT=hT[i][:, k, col:col + 1],
                                                  rhs=W[:, k, cg * 416:(cg + 1) * 416],
                                                  start=(n == 0), stop=(n == 15)))
                    n += 1
            S.op('act', ['P4'], ['wstage0'], lambda: nc.scalar.copy(out=zl[0:1, cg * 416:(cg + 1) * 416], in_=PS[4][0:1, 0:416]))
        S.dma('sp', ['wstage0'], [], out_ap, zl)

    psn = [0]

    def newps():
        psn[0] = 3 + (psn[0] - 3 + 1) % 5 if psn[0] >= 3 else 3
        return psn[0]

    def mm(pidx, out, lhsT, rhs, rkeys, start=True, stop=True):
        S.op('pe', rkeys, ['P%d' % pidx],
             lambda: nc.tensor.matmul(out, lhsT=lhsT, rhs=rhs, start=start, stop=stop))

    def vtt(out, in0, in1, op, r, w):
        S.op('dve', r, w, lambda: nc.vector.tensor_tensor(out=out, in0=in0, in1=in1, op=op))

    def vstt(out, in0, scalar, in1, op0, op1, r, w):
        S.op('dve', r, w, lambda: nc.vector.scalar_tensor_tensor(out=out, in0=in0, scalar=scalar, in1=in1,
                                                                 op0=op0, op1=op1))

    def act(out, in_, func, r, w, scale=1.0):
        S.op('act', r, w, lambda: nc.scalar.activation(out=out, in_=in_, func=func, scale=scale))

    def h3(ap):
        return ap.rearrange("p (h d) -> p h d", h=8)

    def bank4(pidx):
        return PS[pidx][:].rearrange("p (h t) -> p h t", h=4)

    def rwkv_tile(lt):
        i = lt % 2
        hk = 'hT%d' % i
        is_s = lt >= NT_CTX + NT_MAIN
        s_i = lt - NT_CTX - NT_MAIN
        R = dict(RW)
        R['RHS'], R['nU'], R['Y'] = RW['kk'], RW['b'], RW['P']
        kal = {'RHS': 'kk', 'nU': 'b', 'Y': 'P'}
        k_ = lambda n: 'rw_' + kal.get(n, n)
        if lt == 0:
            S.op('dve', [], ['Hs'], lambda: nc.vector.memset(Hs[:], 0.0))
        if is_s:
            S.dma('sp', [], ['wstage0'], ssf, sshift[s_i:s_i + 1, :])
            vtt(ssm[:], ssf, mubc[0:1, :], ALU.mult, ['wstage0', 'mubc'], ['ssm'])
            S.dma('sp', [], ['Sio'], Sio[:], swkv[s_i].rearrange("h v k -> v h k"))
            p = newps()
            for h in range(8):
                S.op('pe', ['Sio', 'ident'], ['P%d' % p],
                     lambda: nc.tensor.transpose(out=PS[p][0:64, h * 64:(h + 1) * 64], in_=Sio[:, h, :],
                                                 identity=ident[0:64, 0:64]))
            S.op('act', ['P%d' % p], ['Hs'],
                 lambda: nc.scalar.copy(out=Hs[:].rearrange("p h v -> p (h v)"), in_=PS[p][0:64, :]))
        zp = []
        for g3 in range(3):
            p = g3
            n = 0
            tot = 16 + (1 if is_s else 0)
            for (W, wk, lo) in ((W1, 'W1', 1), (W2, 'W2', 0)):
                for k in range(8):
                    mm(p, PS[p][:, :], hT[i][:, k, lo:lo + 128], W[:, k, g3 * 512:(g3 + 1) * 512], [hk, wk],
                       start=(n == 0), stop=(n == tot - 1))
                    n += 1
            if is_s:
                mm(p, PS[p][:, :], e0row[0:1, :], ssm[0:1, g3 * 512:(g3 + 1) * 512], ['e0row', 'ssm'],
                   start=False, stop=True)
            zp.append(p)
        pr, pk, pv = zp
        rP, kP, vP = 'P%d' % pr, 'P%d' % pk, 'P%d' % pv
        if _KCUT < 1:
            return
        p = newps()
        for part in range(2):
            c0 = 1536 + 64 * part
            n = 0
            tot = 16 + (1 if is_s else 0)
            for (W, wk, lo) in ((W1, 'W1', 1), (W2, 'W2', 0)):
                for k in range(8):
                    mm(p, PS[p][0:64, part * 128:(part + 1) * 128], W[:, k, c0:c0 + 64], hT[i][:, k, lo:lo + 128],
                       [hk, wk], start=(n == 0), stop=(n == tot - 1))
                    n += 1
            if is_s:
                mm(p, PS[p][0:64, part * 128:(part + 1) * 128], ssm[0:1, c0:c0 + 64], e0row[0:1, :],
                   ['e0row', 'ssm'], start=False, stop=True)
        act(twd[:], PS[p][0:64, 0:128], AF.Tanh, ['P%d' % p], ['twd'])
        act(adT[:], PS[p][0:64, 128:256], AF.Copy, ['P%d' % p], ['adT'])
        if _KCUT < 2:
            return
        pu = newps()
        mm(pu, PS[pu][:, :], twd[:], wup[:], ['twd', 'wup'])
        pa = newps()
        mm(pa, PS[pa][:, :], adT[:], aup[:], ['adT', 'aup'])
        vtt(R['t0'][:], PS[pu][:, :], w0bc, ALU.add, ['P%d' % pu, 'rwv'], [k_('t0')])
        act(R['sig'][:], R['t0'][:], AF.Sigmoid, [k_('t0')], [k_('sig')])
        vtt(R['t1'][:], PS[pa][:, :], a0bc, ALU.add, ['P%d' % pa, 'rwv'], [k_('t1')])
        act(R['a'][:], R['t1'][:], AF.Sigmoid, [k_('t1')], [k_('a')])
        if _KCUT < 3:
            return
        pc = newps()
        mm(pc, PS[pc][:, :], TriU, R['sig'][:], ['cm', k_('sig')])
        pce = newps()
        mm(pce, PS[pce][:, :], TriUs, R['sig'][:], ['cm', k_('sig')])
        act(R['P'][:], PS[pc][:, :], AF.Exp, ['P%d' % pc], [k_('P')])
        act(R['Pinv'][:], PS[pc][:, :], AF.Exp, ['P%d' % pc], [k_('Pinv')], scale=-1.0)
        act(R['Pm1'][:], PS[pce][:, :], AF.Exp, ['P%d' % pce], [k_('Pm1')])
        ppc = newps()
        lcol = 5 * 128 + (1 if is_s else 0)
        for h in range(8):
            mm(ppc, PS[ppc][0:64, h:h + 1], R['sig'][:, h * 64:(h + 1) * 64], cm[:, lcol:lcol + 1], ['cm', k_('sig')])
        act(PCt[:], PS[ppc][0:64, 0:8], AF.Exp, ['P%d' % ppc], ['PCt'])
        if _KCUT < 4:
            return
        vtt(R['t0'][:], PS[pk][:, :], kkbc, ALU.mult, [kP, 'rwv'], [k_('t0')])
        vtt(R['t1'][:], R['t0'][:], R['t0'][:], ALU.mult, [k_('t0')], [k_('t1')])
        S.op('dve', [k_('t1')], ['rsm'],
             lambda: nc.vector.tensor_reduce(out=rsm[:, 0, :], in_=h3(R['t1'][:]), axis=AX.X, op=ALU.add))
        act(rsm[:, 1, :], rsm[:, 0, :], AF.Sqrt, ['rsm'], ['rsm'])
        S.op('dve', ['rsm'], ['rsm'],
             lambda: nc.vector.tensor_scalar(out=rsm[:, 2, :], in0=rsm[:, 1, :], scalar1=1e-12, scalar2=None, op0=ALU.max))
        S.op('dve', ['rsm'], ['rsm'], lambda: nc.vector.reciprocal(out=rsm[:, 3, :], in_=rsm[:, 2, :]))
        vtt(h3(R['kk'][:]), h3(R['t0'][:]), rsm[:, 3, :].unsqueeze(2).to_broadcast([128, 8, 64]), ALU.mult,
            [k_('t0'), 'rsm'], [k_('kk')])
        vstt(R['t1'][:], R['a'][:], -1.0, kabc, ALU.add, ALU.mult, [k_('a'), 'rwv'], [k_('t1')])
        vstt(R['k2'][:], R['t1'][:], 1.0, PS[pk][:, :], ALU.add, ALU.mult, [k_('t1'), kP], [k_('k2')])
        if is_s:
            rm = cm[:, 5 * 128 + 2:5 * 128 + 3]
            for nm in ('kk', 'k2'):
                S.op('dve', [k_(nm), 'cm'], [k_(nm)],
                     lambda: nc.vector.tensor_scalar(out=R[nm][:], in0=R[nm][:], scalar1=rm, scalar2=None, op0=ALU.mult))
        vtt(R['b'][:], R['kk'][:], R['a'][:], ALU.mult, [k_('kk'), k_('a')], [k_('b')])
        vtt(R['kkt'][:], R['kk'][:], R['Pm1'][:], ALU.mult, [k_('kk'), k_('Pm1')], [k_('kkt')])
        vtt(R['kt'][:], R['k2'][:], R['Pinv'][:], ALU.mult, [k_('k2'), k_('Pinv')], [k_('kt')])
        vtt(R['bt'][:], R['b'][:], R['Pinv'][:], ALU.mult, [k_('b'), k_('Pinv')], [k_('bt')])
        vtt(R['rt'][:], PS[pr][:, :], R['P'][:], ALU.mult, [rP, k_('P')], [k_('rt')])
        act(R['V'][:], PS[pv][:, :], AF.Copy, [vP], [k_('V')])
        if is_s:
            S.op('dve', [k_('V'), 'cm'], [k_('V')],
                 lambda: nc.vector.tensor_scalar(out=R['V'][:], in0=R['V'][:], scalar1=rm, scalar2=None, op0=ALU.mult))
        vtt(R['t0'][:], PS[pr][:, :], R['k2'][:], ALU.mult, [rP, k_('k2')], [k_('t0')])
        vtt(R['t0'][:], R['t0'][:], rkbc, ALU.mult, [k_('t0'), 'rwv'], [k_('t0')])
        S.op('dve', [k_('t0')], ['rsm'],
             lambda: nc.vector.tensor_reduce(out=rsm[:, 4, :], in_=h3(R['t0'][:]), axis=AX.X, op=ALU.add))
        if _KCUT < 5:
            return
        for n in ['kkt', 'kt', 'bt', 'rt']:
            for half in range(2):
                p = newps()
                for hh in range(4):
                    h = half * 4 + hh
                    S.op('pe', [k_(n), 'ident'], ['P%d' % p],
                         lambda: nc.tensor.transpose(out=PS[p][0:64, hh * 128:(hh + 1) * 128],
                                                     in_=R[n][:, h * 64:(h + 1) * 64], identity=ident[:]))
                S.op('act', ['P%d' % p], ['tt_' + n],
                     lambda: nc.scalar.copy(out=TT[n][:, half * 4:(half + 1) * 4, :].rearrange("p h t -> p (h t)"),
                                            in_=PS[p][0:64, :]))
        kktT, ktT, btT, rtT = TT['kkt'], TT['kt'], TT['bt'], TT['rt']
        if _KCUT < 6:
            return
        def pair_mat(dst, lhs, lkey, rhs_, rkey, mask, neg, extra=None):
            for half in range(2):
                p = newps()
                for hh in range(4):
                    h = half * 4 + hh
                    mm(p, PS[p][:, hh * 128:(hh + 1) * 128], lhs[:, h, :], rhs_[:, h, :], [lkey, rkey])
                mb = mask.unsqueeze(1).to_broadcast([128, 4, 128])
                vstt(MX[dst][:, half * 4:(half + 1) * 4, :], bank4(p), -1.0 if neg else 1.0, mb, ALU.mult, ALU.mult,
                     ['P%d' % p, 'cm'], ['mx_' + {'A3T': 'XT0', 'A4T': 'XT1'}.get(dst, dst)])
        pair_mat('X0', kktT, 'tt_kkt', btT, 'tt_bt', Mlow, True)
        pair_mat('XT0', btT, 'tt_bt', kktT, 'tt_kkt', Mup, True)
        pair_mat('A1T', ktT, 'tt_kt', kktT, 'tt_kkt', Mup, False)
        for half in range(2):
            vtt(MX['Z'][:, half * 4:(half + 1) * 4, :], MX['XT0'][:, half * 4:(half + 1) * 4, :],
                ident[:].unsqueeze(1).to_broadcast([128, 4, 128]), ALU.add, ['mx_XT0', 'ident'], ['mx_Z'])
        if _KCUT < 7:
            return
        cur = 0
        for j in range(1, 7):
            Xo, XTo = MX['X%d' % cur], MX['XT%d' % cur]
            Xn, XTn = MX['X%d' % (1 - cur)], MX['XT%d' % (1 - cur)]
            ko, kto, kn, ktn = 'mx_X%d' % cur, 'mx_XT%d' % cur, 'mx_X%d' % (1 - cur), 'mx_XT%d' % (1 - cur)
            for half in range(2):
                p = newps()
                for hh in range(4):
                    h = half * 4 + hh
                    mm(p, PS[p][:, hh * 128:(hh + 1) * 128], XTo[:, h, :], Xo[:, h, :], [ko, kto])
                S.op('act', ['P%d' % p], [kn],
                     lambda: nc.scalar.copy(out=Xn[:, half * 4:(half + 1) * 4, :], in_=bank4(p)))
            if j < 6:
                for half in range(2):
                    p = newps()
                    for hh in range(4):
                        h = half * 4 + hh
                        mm(p, PS[p][:, hh * 128:(hh + 1) * 128], Xo[:, h, :], XTo[:, h, :], [ko, kto])
                    S.op('act', ['P%d' % p], [ktn],
                         lambda: nc.scalar.copy(out=XTn[:, half * 4:(half + 1) * 4, :], in_=bank4(p)))
            for half in range(2):
                p = newps()
                for hh in range(4):
                    h = half * 4 + hh
                    mm(p, PS[p][:, hh * 128:(hh + 1) * 128], Xn[:, h, :], MX['Z'][:, h, :], [kn, 'mx_Z'])
                vtt(MX['Z'][:, half * 4:(half + 1) * 4, :], MX['Z'][:, half * 4:(half + 1) * 4, :], bank4(p), ALU.add,
                    ['mx_Z', 'P%d' % p], ['mx_Z'])
            cur = 1 - cur
        if _KCUT < 8:
            return
        MX['A3T'], MX['A4T'] = MX['XT0'], MX['XT1']
        pair_mat('A3T', ktT, 'tt_kt', rtT, 'tt_rt', Mupi, False)
        pair_mat('A4T', btT, 'tt_bt', rtT, 'tt_rt', Mupi, False)
        V3 = h3(R['V'][:])
        p = newps()
        for h in range(8):
            mm(p, PS[p][:, h * 64:(h + 1) * 64], kktT[:, h, :], Hs[:, h, :], ['tt_kkt', 'Hs'], start=True, stop=False)
            mm(p, PS[p][:, h * 64:(h + 1) * 64], MX['A1T'][:, h, :], V3[:, h, :], ['mx_A1T', k_('V')], start=False, stop=True)
        act(R['RHS'][:], PS[p][:, :], AF.Copy, ['P%d' % p], [k_('RHS')])
        RHS3 = h3(R['RHS'][:])
        p = newps()
        for h in range(8):
            mm(p, PS[p][:, h * 64:(h + 1) * 64], MX['Z'][:, h, :], RHS3[:, h, :], ['mx_Z', k_('RHS')])
        act(R['nU'][:], PS[p][:, :], AF.Copy, ['P%d' % p], [k_('nU')], scale=-1.0)
        nU3 = h3(R['nU'][:])
        p = newps()
        for h in range(8):
            mm(p, PS[p][:, h * 64:(h + 1) * 64], rtT[:, h, :], Hs[:, h, :], ['tt_rt', 'Hs'], start=True, stop=False)
            mm(p, PS[p][:, h * 64:(h + 1) * 64], MX['A3T'][:, h, :], V3[:, h, :], ['mx_XT0', k_('V')], start=False, stop=False)
            mm(p, PS[p][:, h * 64:(h + 1) * 64], MX['A4T'][:, h, :], nU3[:, h, :], ['mx_XT1', k_('nU')], start=False, stop=True)
        act(R['Y'][:], PS[p][:, :], AF.Copy, ['P%d' % p], [k_('Y')])
        if _KCUT < 9:
            return
        p = newps()
        kt3, bt3 = h3(R['kt'][:]), h3(R['bt'][:])
        for h in range(8):
            mm(p, PS[p][0:64, h * 64:(h + 1) * 64], kt3[:, h, :], V3[:, h, :], [k_('kt'), k_('V')], start=True, stop=False)
            mm(p, PS[p][0:64, h * 64:(h + 1) * 64], bt3[:, h, :], nU3[:, h, :], [k_('bt'), k_('nU')], start=False, stop=True)
        Hf = Hs[:].rearrange("p h v -> p (h v)")
        vtt(Hf, Hf, PS[p][0:64, :], ALU.add, ['Hs', 'P%d' % p], ['Hs'])
        vtt(Hs[:], Hs[:], PCt[:].unsqueeze(2).to_broadcast([64, 8, 64]), ALU.mult, ['Hs', 'PCt'], ['Hs'])
        if _KCUT < 10:
            return
        if is_s or lt == NT_CTX + NT_MAIN - 1:
            p = newps()
            for h in range(8):
                S.op('pe', ['Hs', 'ident'], ['P%d' % p],
                     lambda: nc.tensor.transpose(out=PS[p][0:64, h * 64:(h + 1) * 64], in_=Hs[:, h, :],
                                                 identity=ident[0:64, 0:64]))
            S.op('act', ['P%d' % p], ['Sio'],
                 lambda: nc.scalar.copy(out=Sio[:].rearrange("p h v -> p (h v)"), in_=PS[p][0:64, :]))
            dst = o_wkv_s[s_i] if is_s else o_wkv
            S.dma('sp', ['Sio'], [], dst.rearrange("h v k -> v h k"), Sio[:])
        if _KCUT < 11:
            return
        if lt >= NT_CTX:
            Y3 = h3(R['Y'][:])
            S.op('dve', [k_('Y')], ['rsm'],
                 lambda: nc.vector.tensor_reduce(out=rsm[:, 5, :], in_=Y3, axis=AX.X, op=ALU.add))
            S.op('dve', ['rsm'], ['rsm'],
                 lambda: nc.vector.tensor_scalar(out=rsm[:, 5, :], in0=rsm[:, 5, :], scalar1=-1.0 / 64, scalar2=None, op0=ALU.mult))
            vtt(Y3, Y3, rsm[:, 5, :].unsqueeze(2).to_broadcast([128, 8, 64]), ALU.add, [k_('Y'), 'rsm'], [k_('Y')])
            vtt(R['t0'][:], R['Y'][:], R['Y'][:], ALU.mult, [k_('Y')], [k_('t0')])
            S.op('dve', [k_('t0')], ['rsm'],
                 lambda: nc.vector.tensor_reduce(out=rsm[:, 6, :], in_=h3(R['t0'][:]), axis=AX.X, op=ALU.add))
            S.op('dve', ['rsm'], ['rsm'],
                 lambda: nc.vector.tensor_scalar(out=rsm[:, 6, :], in0=rsm[:, 6, :], scalar1=1.0 / 64, scalar2=64e-5,
                                                 op0=ALU.mult, op1=ALU.add))
            act(rsm[:, 7, :], rsm[:, 6, :], AF.Sqrt, ['rsm'], ['rsm'])
            S.op('dve', ['rsm'], ['rsm'], lambda: nc.vector.reciprocal(out=rsm[:, 6, :], in_=rsm[:, 7, :]))
            vtt(Y3, Y3, rsm[:, 6, :].unsqueeze(2).to_broadcast([128, 8, 64]), ALU.mult, [k_('Y'), 'rsm'], [k_('Y')])
            vtt(R['Y'][:], R['Y'][:], gngbc, ALU.mult, [k_('Y'), 'rwv'], [k_('Y')])
            vtt(R['Y'][:], R['Y'][:], gnbbc, ALU.add, [k_('Y'), 'rwv'], [k_('Y')])
            vtt(h3(R['t0'][:]), V3, rsm[:, 4, :].unsqueeze(2).to_broadcast([128, 8, 64]), ALU.mult,
                [k_('V'), 'rsm'], [k_('t0')])
            vtt(R['Y'][:], R['Y'][:], R['t0'][:], ALU.add, [k_('Y'), k_('t0')], [k_('Y')])
            r0 = (lt - NT_CTX) * 128
            S.dma('sp', [k_('Y')], [], obscr[r0:r0 + 128, :], R['Y'][:])

    for s in range(NT_S):
        S.dma('pool', [], ['cwin_copy'], o_win_s[s, 0:504, :], cwin[s, 8:512, :])
    S.store_bufs.append(S.buf('cwin_copy'))
    tiles = list(range(NT)) if not _KT else [int(v) for v in _KT.split(',')]
    load_x(tiles[0])
    for ti, lt in enumerate(tiles):
        if ti + 1 < len(tiles):
            load_x(tiles[ti + 1])
        first = (lt == 0) or (lt >= NT_CTX + NT_MAIN)
        norm_transpose(lt, first)
        kv_part(lt)
        rwkv_tile(lt)
        if lt == NT_CTX + NT_MAIN - 1:
            z_last(lt, 128, o_shift[0:1, :])
        if lt >= NT_CTX + NT_MAIN:
            s = lt - NT_CTX - NT_MAIN
            z_last(lt, 8, o_shift_s[s:s + 1, :])

    psb_n = [0]

    def newpsB():
        psb_n[0] = (psb_n[0] + 1) % 6
        return psb_n[0]

    def tsc(out, in0, s1, s2, op0, op1, r, w, accum=None):
        if op1 is None:
            S.op('dve', r, w, lambda: nc.vector.tensor_scalar(out=out, in0=in0, scalar1=s1, scalar2=s2, op0=op0))
        else:
            S.op('dve', r, w, lambda: nc.vector.tensor_scalar(out=out, in0=in0, scalar1=s1, scalar2=s2, op0=op0,
                                                              op1=op1, accum_out=accum))

    def nsa_common_alloc(NC, NB, NKT, NWT, NCHMAX, alias_wb1=False, NBUF=2, NKVT=2):
        T = {'NBUF': NBUF}
        if not alias_wb1:
            T['Wb1'] = sb("Wb1", [128, 8, 1048], BF16)
            T['Wb1key'] = 'Wb1'
        T['QT'] = sb("QT", [64, 8, 128], BF16)
        T['qf'] = sb("qf", [128, 512])
        T['agf'] = sb("agf", [128, 512])
        T['gat'] = sb("gat", [128, 24])
        T['oa'] = sb("oa", [128, 512])
        T['otmp'] = sb("otmp", [128, 512])
        T['E'] = [sb("E%d" % i, [128, 512]) for i in range(NBUF)]
        T['Pb'] = [sb("Pb%d" % i, [128, 512], BF16) for i in range(NBUF)]
        T['PT'] = [sb("PT%d" % i, [128, 4, 128], BF16) for i in range(NBUF)]
        T['Pc'] = sb("Pc", [128, NC])
        T['acc'] = sb("acc", [128, 2, NC])
        T['Zt'] = sb("Zt", [128, 8, NCHMAX])
        T['Zs'] = sb("Zs", [128, 4, 8])
        T['imp'] = sb("imp", [128, 2, NB])
        T['imr'] = sb("imr", [128, NB])
        T['m8'] = sb("m8", [128, 16])
        T['selm'] = sb("selm", [128, 2, NB + 1])
        S.op('pool', [], ['selm'], lambda: nc.gpsimd.memset(T['selm'][:], 0.0))
        T['kcT'] = sb("kcT", [64, 2, NC], BF16)
        T['vc'] = sb("vc", [128, NC // 128, 2, 64], BF16)
        T['KsT'] = sb("KsT", [128, 2, NKT * 128], BF16)
        if alias_wb1:
            off = arena['ptr']
            T['Wb1'] = nc.alloc_sbuf_tensor_at("ph%d_Wb1" % arena['phase'], [128, 8, 1048], BF16, offset=off)
            T['Wb1key'] = 'Vs'
        T['Vs'] = sb("Vs", [128, NKT, 2, 64], BF16)
        T['KwT'] = sb("KwT", [64, 2, NWT * 128], BF16)
        T['Vw'] = sb("Vw", [128, NWT, 2, 64], BF16)
        T['kvt'] = [sb("kvt%d" % i, [128, 768]) for i in range(NKVT)]
        T['kvs'] = sb("kvs", [128, 2, 128])
        T['W1c'] = sb("W1c", [128, 32, 128], BF16)
        T['peT'] = sb("peT", [128, 32], BF16)
        T['w2'] = sb("w2", [128, 2, 64], BF16)
        T['b1c'] = sb("b1c", [128, 4])
        T['b2k'] = sb("b2k", [64, 2])
        T['b2v'] = sb("b2v", [128, 64])
        T['hid'] = sb("hid", [128, 512], BF16)
        T['ctxv'] = sb("ctxv", [128, 1])
        return T

    def load_cmp_weights(T):
        for kv in range(2):
            S.dma('pool', [], ['W1c'], T['W1c'][kv * 64:(kv + 1) * 64, :, :], cmp_w1[kv].rearrange("j d f -> d j f"))
            S.dma('pool', [], ['peT'], T['peT'][kv * 64:(kv + 1) * 64, :], cmp_pe[kv].rearrange("j d -> d j"),
                  allow_slow_non_contiguous=True)
            S.dma('pool', [], ['w2'], T['w2'][:, kv, :], cmp_w2[kv])
            S.dma('sp', [], ['b1c'], T['b1c'][:, kv:kv + 1], cmp_b1[kv:kv + 1, :].rearrange("o f -> f o"),
                  allow_slow_non_contiguous=True)
        S.dma('sp', [], ['b2k'], T['b2k'][:, 0:1], cmp_b2[0:1, :].rearrange("o f -> f o"), allow_slow_non_contiguous=True)
        S.dma('sp', [], ['b2v'], T['b2v'][:], cmp_b2[1:2, :].partition_broadcast(128))
        for kv in range(2):
            for jj in range(32):
                mm(7, PS[7][:, kv:kv + 1], T['W1c'][kv * 64:(kv + 1) * 64, jj, :], T['peT'][kv * 64:(kv + 1) * 64, jj:jj + 1],
                   ['W1c', 'peT'], start=(jj == 0), stop=(jj == 31))
        vtt(T['b1c'][:, 2:4], T['b1c'][:, 0:2], PS[7][:, 0:2], ALU.add, ['b1c', 'P7'], ['b1c'])
        for k in range(8):
            S.dma('pool', [], [T['Wb1key']], T['Wb1'][:, k, 0:512], w_in[k * 128:(k + 1) * 128, OFF_Q:OFF_Q + 512])
            S.dma('pool', [], [T['Wb1key']], T['Wb1'][:, k, 512:1048], w_in[k * 128:(k + 1) * 128, OFF_NG:OFF_NG + 536])

    def compress(T, ntok, NCv):
        S.op('dve', [], ['kcT'], lambda: nc.vector.memset(T['kcT'][:], 0.0))
        S.op('dve', [], ['vc'], lambda: nc.vector.memset(T['vc'][:], 0.0))
        raw = T['KsT']
        for kv in range(2):
            rows = slice(kv * 64, (kv + 1) * 64)
            for g in range(2):
                c0 = 0
                while c0 < NCv:
                    w = min(512, NCv - c0)
                    p = newpsB()
                    for jj in range(32):
                        rv = raw[rows, g, 0:ntok].rearrange("p (c j) -> p j c", j=16)
                        cc = c0 + jj // 16
                        mm(p, PS[p][:, 0:w], T['W1c'][rows, jj, :], rv[:, jj % 16, cc:cc + w],
                           ['W1c', 'KsT'], start=(jj == 0), stop=(jj == 31))
                    S.op('act', ['P%d' % p, 'b1c'], ['hid'],
                         lambda: nc.scalar.activation(out=T['hid'][:, 0:w], in_=PS[p][:, 0:w], func=AF.Silu,
                                                      bias=T['b1c'][:, 2 + kv:3 + kv]))
                    if kv == 0:
                        p2 = newpsB()
                        mm(p2, PS[p2][0:64, 0:w], T['w2'][:, 0, :], T['hid'][:, 0:w], ['w2', 'hid'])
                        S.op('act', ['P%d' % p2, 'b2k'], ['kcT'],
                             lambda: nc.scalar.activation(out=T['kcT'][:, g, c0:c0 + w], in_=PS[p2][0:64, 0:w],
                                                          func=AF.Identity, bias=T['b2k'][:, 0:1]))
                    else:
                        for ci in range((w + 127) // 128):
                            ww = min(128, w - ci * 128)
                            p2 = newpsB()
                            mm(p2, PS[p2][0:ww, 0:64], T['hid'][:, ci * 128:ci * 128 + ww], T['w2'][:, 1, :], ['w2', 'hid'])
                            vtt(T['vc'][0:ww, (c0 // 128) + ci, g, :], PS[p2][0:ww, 0:64], T['b2v'][0:ww, :], ALU.add,
                                ['P%d' % p2, 'b2v'], ['vc'])
                    c0 += w

    def kv_pipeline(T, n, loader, proc, depth):
        nk = len(T['kvt'])
        depth = min(depth, nk - 1)

        def issue(m):
            loader(m, T['kvt'][m % nk], 'kvt%d' % (m % nk))
        for m in range(min(depth, n)):
            issue(m)
        for m in range(n):
            if m + depth < n:
                issue(m + depth)
            proc(m)

    def kv_tile_to_arrays(T, src_ap, kt_sel, kt_win, do_cmp, cmp_kt, nbuf, srckeys=()):
        kvt = T['kvt'][nbuf % len(T['kvt'])]
        kk_ = 'kvt%d' % (nbuf % len(T['kvt']))
        if src_ap is None:
            pass
        elif callable(src_ap):
            src_ap(kvt, kk_)
        else:
            S.dma('sp', list(srckeys), [kk_], kvt[:], src_ap)
        if do_cmp:
            S.op('dve', [kk_], ['kvs'],
                 lambda: nc.vector.tensor_copy(out=T['kvs'][:].rearrange("p g (kv d) -> p g kv d", kv=2),
                                               in_=kvt[:, 0:256].rearrange("p (kv g d) -> p g kv d", kv=2, g=2)))
            p = newpsB()
            for g in range(2):
                S.op('pe', ['kvs', 'ident'], ['P%d' % p],
                     lambda: nc.tensor.transpose(out=PS[p][:, g * 128:(g + 1) * 128], in_=T['kvs'][:, g, :], identity=ident[:]))
            S.op('act', ['P%d' % p], ['KsT'],
                 lambda: nc.scalar.copy(out=T['KsT'][:, :, cmp_kt * 128:(cmp_kt + 1) * 128],
                                        in_=PS[p][:, 0:256].rearrange("p (g t) -> p g t", g=2)))
        if kt_sel is not None or kt_win is not None:
            p = newpsB()
            for bi, (kt, c0) in enumerate(((kt_sel, 256), (kt_win, 512))):
                if kt is None:
                    continue
                for g in range(2):
                    S.op('pe', [kk_, 'ident'], ['P%d' % p],
                         lambda: nc.tensor.transpose(out=PS[p][0:64, (bi * 2 + g) * 128:(bi * 2 + g + 1) * 128],
                                                     in_=kvt[:, c0 + g * 64:c0 + (g + 1) * 64], identity=ident[:]))
            if kt_sel is not None:
                S.op('act', ['P%d' % p], ['KsT'],
                     lambda: nc.scalar.copy(out=T['KsT'][0:64, :, kt_sel * 128:(kt_sel + 1) * 128],
                                            in_=PS[p][0:64, 0:256].rearrange("p (g t) -> p g t", g=2)))
                S.op('dve', [kk_], ['Vs'],
                     lambda: nc.vector.tensor_copy(out=T['Vs'][:, kt_sel, :, :],
                                                   in_=kvt[:, 384:512].rearrange("p (g d) -> p g d", g=2)))
            if kt_win is not None:
                S.op('act', ['P%d' % p], ['KwT'],
                     lambda: nc.scalar.copy(out=T['KwT'][:, :, kt_win * 128:(kt_win + 1) * 128],
                                            in_=PS[p][0:64, 256:512].rearrange("p (g t) -> p g t", g=2)))
                S.op('dve', [kk_], ['Vw'],
                     lambda: nc.vector.tensor_copy(out=T['Vw'][:, kt_win, :, :],
                                                   in_=kvt[:, 640:768].rearrange("p (g d) -> p g d", g=2)))

    def attn_branch(T, bri, KT, kkey, Vt, vkey, ktiles, tmasks, bmask, ctxs, keep_pc=False, key_off=0, mkey='cm', R=128):
        chunks = [ktiles[i:i + 4] for i in range(0, len(ktiles), 4)]
        Zt = T['Zt']
        NBUF = T['NBUF']
        S.op('pool', [], ['Zt'], lambda: nc.gpsimd.memset(Zt[:], 0.0))
        if keep_pc:
            S.op('pool', [], ['acc'], lambda: nc.gpsimd.memset(T['acc'][:], 0.0))
        nmm = len(ktiles)
        items = [(h, ci) for h in range(8) for ci in range(len(chunks))]
        kfirst = {}
        o = 0
        for ci, ch in enumerate(chunks):
            kfirst[ci] = o
            o += len(ch)

        def stageA(it, n):
            h, ci = it
            g = h // 4
            ch = chunks[ci]
            w = len(ch) * 128
            c0 = (ch[0] - key_off) * 128
            e = n % NBUF
            p = newpsB()
            mm(p, PS[p][0:R, 0:w], T['QT'][:, h, 0:R], KT[0:64, g, c0:c0 + w], ['QT', kkey])
            E, ek = T['E'][e], 'E%d' % e
            Pb, pk = (T['Pb'][e], 'Pb%d' % e)
            act(E[0:R, 0:w], PS[p][0:R, 0:w], AF.Exp, ['P%d' % p], [ek])
            for i, kt in enumerate(ch):
                if kt in tmasks:
                    vtt(E[0:R, i * 128:(i + 1) * 128], E[0:R, i * 128:(i + 1) * 128], tmasks[kt][0:R, :], ALU.mult, [ek, mkey], [ek])
                if kt in ctxs:
                    tsc(E[0:R, i * 128:(i + 1) * 128], E[0:R, i * 128:(i + 1) * 128], T['ctxv'][0:R, 0:1], None, ALU.mult, None,
                        [ek, 'ctxv'], [ek])
            if keep_pc:
                Pout, pok = T['Pc'][0:R, c0:c0 + w], 'Pc'
            else:
                Pout, pok = Pb[0:R, 0:w], pk
            if bmask is not None:
                bm, bkeys = bmask(g, ch[0], len(ch))
                bm = bm[0:R]
                S.op('dve', [ek] + bkeys, [pok, 'Zt'],
                     lambda: nc.vector.scalar_tensor_tensor(out=Pout.rearrange("p (n k) -> p n k", k=bm.shape[2]),
                                                            in0=E[0:R, 0:w].rearrange("p (n k) -> p n k", k=bm.shape[2]),
                                                            scalar=1.0, in1=bm, op0=ALU.mult, op1=ALU.mult,
                                                            accum_out=Zt[0:R, h, ci:ci + 1]))
            else:
                tsc(Pout, E[0:R, 0:w], 1.0, 0.0, ALU.mult, ALU.add, [ek], [pok, 'Zt'], accum=Zt[0:R, h, ci:ci + 1])

        def stageB(it, n):
            h, ci = it
            ch = chunks[ci]
            w = len(ch) * 128
            c0 = (ch[0] - key_off) * 128
            e = n % NBUF
            Pb, pk = (T['Pb'][e], 'Pb%d' % e)
            pok = 'Pc' if keep_pc else pk
            pt = newpsB()
            PT, ptk = T['PT'][e], 'PT%d' % e
            if keep_pc:
                for i, kt in enumerate(ch):
                    S.op('pe', [pok, 'ident'], ['P%d' % pt],
                         lambda: nc.tensor.transpose(out=PS[pt][:, i * 128:(i + 1) * 128],
                                                     in_=T['Pc'][:, c0 + i * 128:c0 + (i + 1) * 128], identity=ident[:]))
                S.op('act', ['P%d' % pt], [ptk],
                     lambda: nc.scalar.copy(out=PT[:, 0:len(ch), :].rearrange("p n t -> p (n t)"), in_=PS[pt][:, 0:w]))
            else:
                psb = PS[pt][:].bitcast(BF16)
                for i, kt in enumerate(ch):
                    S.op('pe', [pok, 'identb'], ['P%d' % pt],
                         lambda: nc.tensor.transpose(out=psb[:, i * 128:i * 128 + R], in_=Pb[0:R, i * 128:(i + 1) * 128],
                                                     identity=identb[0:R, 0:R]))
                S.op('act', ['P%d' % pt], [ptk],
                     lambda: nc.scalar.copy(out=PT[:, 0:len(ch), 0:R],
                                            in_=psb[:, 0:w].rearrange("p (n t) -> p n t", t=128)[:, :, 0:R]))

        def stageC(it, n):
            h, ci = it
            g = h // 4
            ch = chunks[ci]
            e = n % NBUF
            PT, ptk = T['PT'][e], 'PT%d' % e
            for i, kt in enumerate(ch):
                im = kfirst[ci] + i
                mm(6, PS[6][0:R, h * 64:(h + 1) * 64], PT[:, i, 0:R], Vt[:, kt - key_off, g, :], [ptk, vkey],
                   start=(im == 0), stop=(im == nmm - 1))

        def head_post(h):
            g = h // 4
            S.op('dve', ['Zt'], ['Zs'],
                 lambda: nc.vector.tensor_reduce(out=T['Zs'][:, 0, h:h + 1], in_=Zt[:, h, 0:len(chunks)], axis=AX.X, op=ALU.add))
            tsc(T['Zs'][:, 1, h:h + 1], T['Zs'][:, 0, h:h + 1], 1e-30, None, ALU.max, None, ['Zs'], ['Zs'])
            S.op('dve', ['Zs'], ['Zs'], lambda: nc.vector.reciprocal(out=T['Zs'][:, 2, h:h + 1], in_=T['Zs'][:, 1, h:h + 1]))
            vstt(T['acc'][:, g, :], T['Pc'][:], T['Zs'][:, 2, h:h + 1], T['acc'][:, g, :], ALU.mult, ALU.add,
                 ['Pc', 'Zs', 'acc'], ['acc'])

        if keep_pc or NBUF < 3:
            for n, it in enumerate(items):
                stageA(it, n)
                stageB(it, n)
                stageC(it, n)
                if keep_pc and it[1] == len(chunks) - 1:
                    head_post(it[0])
        else:
            N = len(items)
            skB = NBUF - 1
            skC = skB + NBUF - 1
            for step in range(N + skC):
                if step < N:
                    stageA(items[step], step)
                if 0 <= step - skB < N:
                    stageB(items[step - skB], step - skB)
                if 0 <= step - skC < N:
                    stageC(items[step - skC], step - skC)
        if not keep_pc:
            S.op('dve', ['Zt'], ['Zs'],
                 lambda: nc.vector.tensor_reduce(out=T['Zs'][:, 0, :], in_=Zt[:, :, 0:len(chunks)], axis=AX.X, op=ALU.add))
            tsc(T['Zs'][:, 1, :], T['Zs'][:, 0, :], 1e-30, None, ALU.max, None, ['Zs'], ['Zs'])
            S.op('dve', ['Zs'], ['Zs'], lambda: nc.vector.reciprocal(out=T['Zs'][:, 2, :], in_=T['Zs'][:, 1, :]))
        gv = T['gat'][:].rearrange("p (h c) -> p h c", c=3)[:, :, bri]
        vtt(T['Zs'][:, 3, :], T['Zs'][:, 2, :], gv, ALU.mult, ['Zs', 'gat'], ['Zs'])
        vtt(h3(T['otmp'][:]), h3(PS[6][:, :]), T['Zs'][:, 3, :].unsqueeze(2).to_broadcast([128, 8, 64]), ALU.mult,
            ['P6', 'Zs'], ['otmp'])
        vtt(T['oa'][:], T['oa'][:], T['otmp'][:], ALU.add, ['oa', 'otmp'], ['oa'])

    def topk_select(T, NC, NB, m1, m2, mkeys):
        NQ = NC // 4
        S.op('pool', [], ['imp'], lambda: nc.gpsimd.memset(T['imp'][:], 0.0))
        for g in range(2):
            av = T['acc'][:, g, :].rearrange("p (n r) -> p n r", r=4)
            S.op('dve', ['acc'], ['imp'],
                 lambda: nc.vector.tensor_reduce(out=T['imp'][:, g, 0:NQ], in_=av, axis=AX.X, op=ALU.add))
            vtt(T['imp'][:, g, 1:NQ], T['imp'][:, g, 1:NQ], av[:, 0:NQ - 1, 3], ALU.add, ['imp', 'acc'], ['imp'])
            if NB > NQ:
                S.op('dve', ['acc'], ['imp'],
                     lambda: nc.vector.tensor_copy(out=T['imp'][:, g, NQ:NQ + 1], in_=T['acc'][:, g, NC - 1:NC]))
            vtt(T['imp'][:, g, :], T['imp'][:, g, :], m1, ALU.mult, ['imp'] + mkeys, ['imp'])
            vtt(T['imp'][:, g, :], T['imp'][:, g, :], m2, ALU.add, ['imp'] + mkeys, ['imp'])
            S.op('dve', ['imp'], ['m8'], lambda: nc.vector.max(out=T['m8'][:, 0:8], in_=T['imp'][:, g, :]))
            S.op('dve', ['imp', 'm8'], ['imr'],
                 lambda: nc.vector.match_replace(out=T['imr'][:], in_to_replace=T['m8'][:, 0:8], in_values=T['imp'][:, g, :],
                                                 imm_value=-1e30))
            S.op('dve', ['imr'], ['m8'], lambda: nc.vector.max(out=T['m8'][:, 8:16], in_=T['imr'][:]))
            tsc(T['selm'][:, g, 0:NB], T['imp'][:, g, :], T['m8'][:, 15:16], None, ALU.is_ge, None, ['imp', 'm8'], ['selm'])

    def q_part(T, lt):
        i = lt % 2
        ck = 'cs%d' % i
        norm_transpose(lt, True)
        pq = newpsB()
        project(lt, T['Wb1'], T['Wb1key'], 0, 512, pq)
        act(T['qf'][:], PS[pq][:, :], AF.Copy, ['P%d' % pq], ['qf'])
        project(lt, T['Wb1'], T['Wb1key'], 512, 24, 7)
        act(T['gat'][:], PS[7][:, 0:24], AF.Sigmoid, ['P7'], ['gat'])
        pa_ = newpsB()
        project(lt, T['Wb1'], T['Wb1key'], 536, 512, pa_)
        act(T['agf'][:], PS[pa_][:, :], AF.Silu, ['P%d' % pa_], ['agf'])
        q3 = h3(T['qf'][:])
        x1, x2 = q3[:, :, 0:32], q3[:, :, 32:64]
        rt_ = hf[:].rearrange("p (a h c) -> p a h c", a=4, h=8)
        cosb = cs[i][:, 0:32].unsqueeze(1).to_broadcast([128, 8, 32])
        sinb = cs[i][:, 32:64].unsqueeze(1).to_broadcast([128, 8, 32])
        vtt(rt_[:, 0], x1, cosb, ALU.mult, ['qf', ck], ['hf'])
        vtt(rt_[:, 1], x2, sinb, ALU.mult, ['qf', ck], ['hf'])
        vtt(rt_[:, 2], x2, cosb, ALU.mult, ['qf', ck], ['hf'])
        vtt(rt_[:, 3], x1, sinb, ALU.mult, ['qf', ck], ['hf'])
        vtt(x1, rt_[:, 0], rt_[:, 1], ALU.subtract, ['hf'], ['qf'])
        vtt(x2, rt_[:, 2], rt_[:, 3], ALU.add, ['hf'], ['qf'])
        for half in range(2):
            p = newpsB()
            for hh in range(4):
                h = half * 4 + hh
                S.op('pe', ['qf', 'ident'], ['P%d' % p],
                     lambda: nc.tensor.transpose(out=PS[p][0:64, hh * 128:(hh + 1) * 128], in_=T['qf'][:, h * 64:(h + 1) * 64],
                                                 identity=ident[:]))
            act(T['QT'][:, half * 4:(half + 1) * 4, :].rearrange("p h t -> p (h t)"), PS[p][0:64, :], AF.Copy,
                ['P%d' % p], ['QT'], scale=0.125)
        S.op('pool', [], ['oa'], lambda: nc.gpsimd.memset(T['oa'][:], 0.0))

    def finish_tile(T, row0):
        vtt(T['otmp'][:], T['oa'][:], T['agf'][:], ALU.mult, ['oa', 'agf'], ['otmp'])
        S.dma('sp', ['otmp'], [], oascr[row0:row0 + 128, :], T['otmp'][:])

    Mup_, Caus_ = cm[:, 3 * 128:4 * 128], cm[:, 5 * 128 + 3:6 * 128 + 3]

    def phase_b1_prompt():
        S.barrier()
        arena_reset()
        T = nsa_common_alloc(256, 64, 32, 32, 8, NBUF=4, NKVT=4)
        cmk = sb("cmk", [128, NT_MAIN, 256])
        m12 = sb("m12", [128, 2, NT_MAIN, 64])
        load_cmp_weights(T)
        S.dma('sp', [], ['cmk'], cmk[:], cmaskP[:, :, :])
        S.dma('sp', [], ['m12'], m12[:, 0], tkm1[:, :, :])
        S.dma('sp', [], ['m12'], m12[:, 1], tkm2[:, :, :])
        S.dma('sp', [], ['ctxv'], T['ctxv'][:], ctxv_d[:, :])
        ldp = lambda m, kvt, key: S.dma('sp', [], [key], kvt[:], kvscr[m * 128:(m + 1) * 128, :])
        kv_pipeline(T, 32, ldp, lambda m: kv_tile_to_arrays(T, None, None, None, True, m, m), 3)
        compress(T, 4096, 255)
        kv_pipeline(T, 32, ldp, lambda m: kv_tile_to_arrays(T, None, m, m, False, 0, m), 3)
        tiles = list(range(NT_MAIN)) if not _KT else [0, 15]
        load_x(NT_CTX + tiles[0])
        for ti, j in enumerate(tiles):
            lt = NT_CTX + j
            if ti + 1 < len(tiles):
                load_x(NT_CTX + tiles[ti + 1])
            q_part(T, lt)
            cmv = cmk[:, j, :]
            attn_branch(T, 0, T['kcT'], 'kcT', T['vc'], 'vc', [0, 1], {}, lambda g, k0, n: (cmv[:, k0 * 128:(k0 + n) * 128].unsqueeze(2), ['cmk']),
                        set(), keep_pc=True)
            topk_select(T, 256, 64, m12[:, 0, j, :], m12[:, 1, j, :], ['m12'])
            kts = list(range(0, lt + 1))
            attn_branch(T, 1, T['KsT'], 'KsT', T['Vs'], 'Vs', kts, {lt: Caus_},
                        lambda g, k0, n: (T['selm'][:, g, 2 * k0:2 * (k0 + n)].unsqueeze(2).to_broadcast([128, 2 * n, 64]), ['selm']),
                        set())
            wts = list(range(lt - 4, lt + 1))
            attn_branch(T, 2, T['KwT'], 'KwT', T['Vw'], 'Vw', wts, {lt - 4: Mup_, lt: Caus_}, None,
                        set(k for k in wts if k < NT_CTX))
            finish_tile(T, j * 128)

    def phase_b1_sample(s_i):
        S.barrier()
        arena_reset()
        T = nsa_common_alloc(1024, 257, 129, 5, 33, alias_wb1=True, NBUF=3, NKVT=4)
        m7 = sb("m7", [128, 128])
        m12 = sb("m12s", [128, 2, 257])
        pti = sb("pti", [128, 128], I32)
        ptf = sb("ptf", [128, 128])
        idx = sb("idx", [128, 128], I32)
        iop = sb("iop", [128, 1])
        load_cmp_weights(T)
        S.op('dve', [], ['m7'], lambda: nc.vector.memset(m7[:], 1.0))
        S.op('dve', [], ['m7'], lambda: nc.vector.memset(m7[:, 127:128], 0.0))
        S.dma('sp', [], ['m12s'], m12[:], tkmS[:, :, :])
        S.dma('sp', [], ['iop'], iop[:], iotap[:, :])
        S.dma('sp', [], ['pti'], pti[:], ptab[s_i:s_i + 1, :].partition_broadcast(128))
        S.op('dve', ['pti'], ['ptf'], lambda: nc.vector.tensor_copy(out=ptf[:], in_=pti[:]))
        tsc(ptf[:], ptf[:], 128.0, iop[:, 0:1], ALU.mult, ALU.add, ['ptf', 'iop'], ['ptf'])
        S.op('dve', ['ptf'], ['idx'], lambda: nc.vector.tensor_copy(out=idx[:], in_=ptf[:]))
        lt = NT_CTX + NT_MAIN + s_i
        load_x(lt)
        q_part(T, lt)
        kv_pipeline(T, 128, lambda m, kvt, key: S.idma(['idx'], [key], kvt[:, 0:256], ccmp[:, :], idx[:, m:m + 1]),
                    lambda m: kv_tile_to_arrays(T, None, None, None, True, m, m), 3)
        compress(T, 16384, 1023)
        kv_pipeline(T, 128, lambda m, kvt, key: S.idma(['idx'], [key], kvt[:, 256:512], csel[:, :], idx[:, m:m + 1]),
                    lambda m: kv_tile_to_arrays(T, None, m, None, False, 0, m), 3)
        for r_ in range(4):
            kv_tile_to_arrays(T, lambda kvt, kk_: S.dma('sp', [], [kk_], kvt[:, 512:768], cwin[s_i, r_ * 128:(r_ + 1) * 128, :]),
                              None, r_, False, 0, r_)
        kv_tile_to_arrays(T, kvscr[lt * 128:(lt + 1) * 128, :], 128, 4, False, 0, 0)
        attn_branch(T, 0, T['kcT'], 'kcT', T['vc'], 'vc', list(range(8)), {7: m7[:]}, None, set(), keep_pc=True, mkey='m7')
        topk_select(T, 1024, 257, m12[:, 0, :], m12[:, 1, :], ['m12s'])
        attn_branch(T, 1, T['KsT'], 'KsT', T['Vs'], 'Vs', list(range(129)), {128: Caus_},
                    lambda g, k0, n: (T['selm'][:, g, 2 * k0:2 * (k0 + n)].unsqueeze(2).to_broadcast([128, 2 * n, 64]), ['selm']),
                    set(), R=32)
        attn_branch(T, 2, T['KwT'], 'KwT', T['Vw'], 'Vw', list(range(5)), {0: Mup_, 4: Caus_}, None, set(), R=32)
        finish_tile(T, (NT_MAIN + s_i) * 128)

    def phase_b2():
        S.barrier()
        arena_reset()
        Wb2 = sb("Wb2", [128, 8, 2560], BF16)
        Wpa = sb("Wpa", [128, 4, 1024], BF16)
        Wpb = sb("Wpb", [128, 4, 1024], BF16)
        Wo = sb("Wo", [128, 8, 1024], BF16)
        Wg = sb("Wg", [128, 8, 1024], BF16)
        Wpp = sb("Wpp", [128, 2, 1024], BF16)
        g2bc = sb("g2bc", [128, D])
        g3bc = sb("g3bc", [128, D])
        oat = sb("oat", [128, 512])
        obt = sb("obt", [128, 512])
        rgs = sb("rgs", [128, 512])
        aT = sb("aT", [128, 4, 128], BF16)
        bT = sb("bT", [128, 4, 128], BF16)
        mg = sb("mg", [128, 2048])
        mm_ = sb("mm_", [128, D])
        mT = sb("mT", [128, 8, 128], BF16)
        x1 = sb("x1", [128, D])
        pt_ = sb("pt_", [128, 256])
        pT = sb("pT", [128, 2, 128], BF16)
        gt = sb("gt", [128, D])
        yo = sb("yo", [128, D])
        for k in range(8):
            S.dma('pool', [], ['Wb2'], Wb2[:, k, :], w_in[k * 128:(k + 1) * 128, OFF_RG:OFF_RG + 2560])
            S.dma('pool', [], ['Wo'], Wo[:, k, :], w_out[k * 128:(k + 1) * 128, :])
            S.dma('pool', [], ['Wg'], Wg[:, k, :], w_pg[k * 128:(k + 1) * 128, :])
        for k in range(4):
            S.dma('pool', [], ['Wpa'], Wpa[:, k, :], w_pa[k * 128:(k + 1) * 128, :])
            S.dma('pool', [], ['Wpb'], Wpb[:, k, :], w_pb[k * 128:(k + 1) * 128, :])
        for k in range(2):
            S.dma('pool', [], ['Wpp'], Wpp[:, k, :], w_pp[k * 128:(k + 1) * 128, :])
        S.dma('sp', [], ['g2bc'], g2bc[:], ple_g.partition_broadcast(128))
        S.dma('sp', [], ['g3bc'], g3bc[:], fin_g.partition_broadcast(128))

        def transp(src, skey, nk, dstT, dkey):
            for half in range((nk + 3) // 4):
                n = min(4, nk - half * 4)
                p = newpsB()
                for i in range(n):
                    kx = half * 4 + i
                    S.op('pe', [skey, 'ident'], ['P%d' % p],
                         lambda: nc.tensor.transpose(out=PS[p][:, i * 128:(i + 1) * 128], in_=src[:, kx * 128:(kx + 1) * 128],
                                                     identity=ident[:]))
                S.op('act', ['P%d' % p], [dkey],
                     lambda: nc.scalar.copy(out=dstT[:, half * 4:half * 4 + n, :].rearrange("p n t -> p (n t)"), in_=PS[p][:, 0:n * 128]))

        def rms(src, skey, gb, gkey, dst, dkey):
            S.op('act', [skey], ['junk', 'ss'],
                 lambda: nc.scalar.activation(out=junk[:], in_=src[:], func=AF.Square, accum_out=ss[:, 0:1]))
            tsc(ss[:, 1:2], ss[:, 0:1], 1.0 / D, 1e-6, ALU.mult, ALU.add, ['ss'], ['ss'])
            act(ss[:, 2:3], ss[:, 1:2], AF.Sqrt, ['ss'], ['ss'])
            S.op('dve', ['ss'], ['ss'], lambda: nc.vector.reciprocal(out=ss[:, 3:4], in_=ss[:, 2:3]))
            vstt(dst[:], src[:], ss[:, 3:4], gb[:], ALU.mult, ALU.mult, [skey, 'ss', gkey], [dkey])

        tiles = list(range(NT_MAIN + NT_S)) if not _KT else [0, 15, 16]
        load_x(NT_CTX + tiles[0])
        for ti, j in enumerate(tiles):
            lt = NT_CTX + j
            i = lt % 2
            xk = 'xt%d' % i
            if ti + 1 < len(tiles):
                load_x(NT_CTX + tiles[ti + 1])
            S.dma('sp', [], ['oat'], oat[:], oascr[j * 128:(j + 1) * 128, :])
            S.dma('sp', [], ['obt'], obt[:], obscr[j * 128:(j + 1) * 128, :])
            S.dma('sp', [], ['pt_'], pt_[:], p_loc[j * 128:(j + 1) * 128, :])
            norm_transpose(lt, True)
            pr_ = newpsB()
            project(lt, Wb2, 'Wb2', 0, 512, pr_)
            act(rgs[:], PS[pr_][:, :], AF.Silu, ['P%d' % pr_], ['rgs'])
            for q4 in range(4):
                pm = newpsB()
                project(lt, Wb2, 'Wb2', 512 + q4 * 512, 512, pm)
                act(mg[:, q4 * 512:(q4 + 1) * 512], PS[pm][:, :], AF.Sigmoid, ['P%d' % pm], ['mg'])
            vtt(obt[:], obt[:], rgs[:], ALU.mult, ['obt', 'rgs'], ['obt'])
            transp(oat, 'oat', 4, aT, 'aT')
            transp(obt, 'obt', 4, bT, 'bT')
            for cg in range(2):
                pa2 = newpsB()
                for k in range(4):
                    mm(pa2, PS[pa2][:, :], aT[:, k, :], Wpa[:, k, cg * 512:(cg + 1) * 512], ['aT', 'Wpa'], start=(k == 0), stop=(k == 3))
                vtt(mm_[:, cg * 512:(cg + 1) * 512], PS[pa2][:, :], mg[:, cg * 512:(cg + 1) * 512], ALU.mult, ['P%d' % pa2, 'mg'], ['mm_'])
                pb2 = newpsB()
                for k in range(4):
                    mm(pb2, PS[pb2][:, :], bT[:, k, :], Wpb[:, k, cg * 512:(cg + 1) * 512], ['bT', 'Wpb'], start=(k == 0), stop=(k == 3))
                vtt(yo[:, cg * 512:(cg + 1) * 512], PS[pb2][:, :], mg[:, 1024 + cg * 512:1024 + (cg + 1) * 512], ALU.mult,
                    ['P%d' % pb2, 'mg'], ['yo'])
            vtt(mm_[:], mm_[:], yo[:], ALU.add, ['mm_', 'yo'], ['mm_'])
            transp(mm_, 'mm_', 8, mT, 'mT')
            for cg in range(2):
                po = newpsB()
                for k in range(8):
                    mm(po, PS[po][:, :], mT[:, k, :], Wo[:, k, cg * 512:(cg + 1) * 512], ['mT', 'Wo'], start=(k == 0), stop=(k == 7))
                vtt(x1[:, cg * 512:(cg + 1) * 512], PS[po][:, :], xt[i][:, cg * 512:(cg + 1) * 512], ALU.add, ['P%d' % po, xk], ['x1'])
            rms(x1, 'x1', g2bc, 'g2bc', mm_, 'mm_')
            transp(mm_, 'mm_', 8, mT, 'mT')
            transp(pt_, 'pt_', 2, pT, 'pT')
            for cg in range(2):
                pg_ = newpsB()
                for k in range(8):
                    mm(pg_, PS[pg_][:, :], mT[:, k, :], Wg[:, k, cg * 512:(cg + 1) * 512], ['mT', 'Wg'], start=(k == 0), stop=(k == 7))
                act(gt[:, cg * 512:(cg + 1) * 512], PS[pg_][:, :], AF.Sigmoid, ['P%d' % pg_], ['gt'])
                pp_ = newpsB()
                for k in range(2):
                    mm(pp_, PS[pp_][:, :], pT[:, k, :], Wpp[:, k, cg * 512:(cg + 1) * 512], ['pT', 'Wpp'], start=(k == 0), stop=(k == 1))
                vtt(gt[:, cg * 512:(cg + 1) * 512], gt[:, cg * 512:(cg + 1) * 512], PS[pp_][:, :], ALU.mult, ['gt', 'P%d' % pp_], ['gt'])
            vtt(x1[:], x1[:], gt[:], ALU.add, ['x1', 'gt'], ['x1'])
            rms(x1, 'x1', g3bc, 'g3bc', yo, 'yo')
            if j < NT_MAIN:
                S.dma('sp', ['yo'], [], o_y[j * 128:(j + 1) * 128, :], yo[:])
            else:
                s_ = j - NT_MAIN
                S.dma('sp', ['yo'], [], o_y_s[s_ * 8:(s_ + 1) * 8, :], yo[0:8, :])

    if _KCUT >= 50:
        _sk = os.environ.get('KSKIP', '')
        if 'b1p' not in _sk:
            phase_b1_prompt()
        if 'b1s' not in _sk:
            for s_i in range(NT_S if not _KT else 1):
                phase_b1_sample(s_i)
        if 'b2' not in _sk:
            phase_b2()
    if os.environ.get('KVERB'):
        print("last phase used", arena['ptr'] - arena['lo'], "free", nc.sbuf_top - arena['ptr'], "counts", S.ecnt, flush=True)
    S.finish('sp')
    return nc


_PROG = None


def _rope_table(pos):
    half = 32
    inv = (np.float32(10000.0) ** (-np.arange(half, dtype=np.float32) / np.float32(half))).astype(np.float32)
    ang = pos.astype(np.float32)[:, None] * inv[None, :]
    return np.concatenate([np.cos(ang), np.sin(ang)], axis=1).astype(np.float32)


def kernel(**inp):
    global _PROG
    f = lambda a: np.ascontiguousarray(np.asarray(a))
    x_prompt = f(inp['x_prompt'])
    x_sample = f(inp['x_sample'])
    B, SEQ = x_prompt.shape[:2]
    HALF = SEQ // 2
    PAST = inp['page_table'].shape[1] * 128
    ccmp_full = f(inp['cache_cmp_kv'])[0].reshape(-1, 256)
    csel_full = f(inp['cache_sel_kv'])[0].reshape(-1, 256)
    page_table = f(inp['page_table']).astype(np.int32)
    NPOOL = ccmp_full.shape[0] // 128
    if _PROG is None or _PROG[0] != NPOOL:
        _PROG = (NPOOL, build_program(NPOOL))
    nc = _PROG[1]
    in_maps = []
    ident = np.eye(128, dtype=np.float32)
    ii = np.arange(128)
    cdec = np.float32(-np.exp(-0.5))
    triu = (ii[:, None] <= ii[None, :]).astype(np.float32)
    trius = (ii[:, None] < ii[None, :]).astype(np.float32)
    lastsel = np.zeros((128, 3), np.float32)
    lastsel[:8, 2] = 1.0
    lastsel[:, 0] = cdec
    lastsel[:8, 1] = cdec
    cmat = np.concatenate([triu * cdec, trius * cdec, (ii[:, None] > ii[None, :]).astype(np.float32),
                           trius, triu, lastsel, (ii[:, None] >= ii[None, :]).astype(np.float32)], axis=1).astype(np.float32)
    e0row = np.zeros((1, 128), np.float32)
    e0row[0, 0] = 1.0
    rw_vec = np.stack([f(inp[k]).reshape(512) for k in
                       ['rwkv_w0', 'rwkv_a0', 'rwkv_k_k', 'rwkv_k_a', 'rwkv_r_k', 'rwkv_gn_g', 'rwkv_gn_b']]).astype(np.float32)
    tt = np.arange(128)[:, None, None]
    jj_ = np.arange(NT_MAIN)[None, :, None]
    L = 2048 + 128 * jj_ + tt
    cmaskP, tkm1, tkm2 = [], [], []
    for half in range(2):
        cl = np.arange(256)[None, None, :]
        ok = (16 * cl + 31 <= L) & ((half == 1) | (cl >= 128)) & (cl < 255)
        cmaskP.append(ok.astype(np.float32))
        n = np.arange(64)[None, None, :]
        cur = L // 64
        first = 32 * (1 - half)
        forced = (n == first) | (n == cur) | ((n == cur - 1) & (cur - 1 >= first))
        future = n * 64 > L
        invalid = n < first
        m1 = np.where(forced | future | invalid, 0.0, 1.0)
        m2 = np.where(forced, 1e4, 0.0)
        m2 = np.where(future, -1.0, m2)
        m2 = np.where(invalid, -2.0, m2)
        tkm1.append(m1.astype(np.float32))
        tkm2.append(m2.astype(np.float32))
    cl = (np.arange(2)[None, :, None] * 128 + np.arange(128)[:, None, None])
    nn = np.arange(64)[None, None, :]
    ovP = ((16 * cl <= 64 * nn + 63) & (16 * cl + 31 >= 64 * nn)).astype(np.float32)
    ns = np.arange(257)[None, :]
    forced_s = (ns == 0) | (ns == 256) | (ns == 255)
    tkmS = np.stack([np.broadcast_to(np.where(forced_s, 0.0, 1.0), (128, 257)),
                     np.broadcast_to(np.where(forced_s, 1e4, 0.0), (128, 257))], axis=1).astype(np.float32)
    cls = (np.arange(8)[None, :, None] * 128 + np.arange(128)[:, None, None])
    nns = np.arange(257)[None, None, :]
    ovS = ((16 * cls <= 64 * nns + 63) & (16 * cls + 31 >= 64 * nns) & (cls < 1023)).astype(np.float32)
    rw_up = np.stack([f(inp['rwkv_w_up'])[0], f(inp['rwkv_a_up'])[0]]).astype(np.float32)
    for c in range(NCORES):
        b, half = c // 2, c % 2
        xl = np.zeros((NT * 128, D), np.float32)
        if half == 1:
            xl[0:HALF] = x_prompt[b, 0:HALF]
        xl[HALF:2 * HALF] = x_prompt[b, half * HALF:(half + 1) * HALF]
        pos = np.zeros((NT * 128,), np.float32)
        pos[0:2 * HALF] = np.arange(2 * HALF) + (half - 1) * HALF
        for s in range(NT_S):
            r0 = (NT_CTX + NT_MAIN + s) * 128
            xl[r0:r0 + 8] = x_sample[4 * c + s]
            pos[r0:r0 + 8] = PAST + np.arange(8)
        p_loc = np.zeros(((NT_MAIN + NT_S) * 128, 256), np.float32)
        p_loc[0:HALF] = f(inp['p_prompt'])[0, b, half * HALF:(half + 1) * HALF]
        for s in range(NT_S):
            p_loc[(NT_MAIN + s) * 128:(NT_MAIN + s) * 128 + 8] = f(inp['p_sample'])[0, 4 * c + s]
        ptab_c = page_table[4 * c:4 * c + 4]
        ccmp_c, csel_c = ccmp_full, csel_full
        m = {
            'x_loc': xl,
            'rope_cs': _rope_table(pos),
            'ident': ident,
            'norm_g': f(inp['norm_g']).reshape(1, D),
            'w_in': f(inp['w_in'])[0],
            'rwkv_mu': f(inp['rwkv_mu']).reshape(1, C_R),
            'sshift': f(inp['state_shift'])[0, 4 * c:4 * c + 4],
            'cmat': cmat, 'e0row': e0row,
            'rw_vec': rw_vec, 'rw_up': rw_up,
            'swkv': f(inp['state_wkv'])[0, 4 * c:4 * c + 4],
            'cmp_w1': f(inp['cmp_w1'])[0], 'cmp_pe': f(inp['cmp_pe'])[0], 'cmp_w2': f(inp['cmp_w2'])[0],
            'cmp_b1': f(inp['cmp_b1'])[0], 'cmp_b2': f(inp['cmp_b2'])[0],
            'cmaskP': cmaskP[half], 'tkm1': tkm1[half], 'tkm2': tkm2[half], 'ovP': ovP,
            'ctxv': np.full((128, 1), float(half), np.float32),
            'tkmS': tkmS, 'ovS': ovS, 'iotap': np.arange(128, dtype=np.float32).reshape(128, 1),
            'ptab': ptab_c, 'ccmp': ccmp_c, 'csel': csel_c,
            'w_pa': f(inp['w_pa'])[0], 'w_pb': f(inp['w_pb'])[0], 'w_out': f(inp['w_out'])[0],
            'w_pg': f(inp['w_ple_gate'])[0], 'w_pp': f(inp['w_ple_proj'])[0],
            'ple_g': f(inp['ple_norm_g']).reshape(1, D), 'fin_g': f(inp['final_norm_g']).reshape(1, D),
            'p_loc': p_loc,
            'cwin': f(inp['cache_win_kv'])[0, 4 * c:4 * c + 4].reshape(NT_S, 512, 256),
        }
        in_maps.append(m)
    if os.environ.get('KTRACE'):
        import time as _t
        _t0 = _t.time()
        _r = run_bass_kernel_spmd(nc, in_maps, core_ids=list(range(NCORES)), trace=True)
        print("KTRACE exec_time_ns", _r.exec_time_ns, "wall", _t.time() - _t0, flush=True)
        res = _r.results
    else:
        res = run_bass_kernel_spmd(nc, in_maps, core_ids=list(range(NCORES))).results
    if _KDBG:
        DBG['ob'] = [r['obscr'] for r in res]
        DBG['oa'] = [r['oascr'] for r in res]
    DB, DS = x_sample.shape[:2]
    y_p = np.zeros((B, SEQ, D), np.float32)
    y_s = np.zeros((DB, DS, D), np.float32)
    cmp_p = np.zeros((1, B, SEQ, 2, 2, 64), np.float32)
    sel_p = np.zeros((1, B, SEQ, 2, 2, 64), np.float32)
    win_p = np.zeros((1, B, 512, 2, 2, 64), np.float32)
    cmp_s = np.zeros((1, DB, DS, 2, 2, 64), np.float32)
    sel_s = np.zeros((1, DB, DS, 2, 2, 64), np.float32)
    win_s = np.zeros((1, DB, 512, 2, 2, 64), np.float32)
    wkv_p = np.zeros((1, B, 8, 64, 64), np.float32)
    wkv_s = np.zeros((1, DB, 8, 64, 64), np.float32)
    sh_p = np.zeros((1, B, C_R), np.float32)
    sh_s = np.zeros((1, DB, C_R), np.float32)
    for c in range(NCORES):
        b, half = c // 2, c % 2
        r = res[c]
        sl = slice(half * HALF, (half + 1) * HALF)
        y_p[b, sl] = r['o_y']
        y_s[4 * c:4 * c + 4] = r['o_y_s'].reshape(4, 8, D)
        cmp_p[0, b, sl] = r['o_cmp'].reshape(HALF, 2, 2, 64)
        sel_p[0, b, sl] = r['o_sel'].reshape(HALF, 2, 2, 64)
        if half == 1:
            win_p[0, b] = r['o_win'].reshape(512, 2, 2, 64)
            sh_p[0, b] = r['o_shift'][0]
        if half == 1:
            wkv_p[0, b] = r['o_wkv']
        wkv_s[0, 4 * c:4 * c + 4] = r['o_wkv_s']
        cmp_s[0, 4 * c:4 * c + 4] = r['o_cmp_s'].reshape(4, 8, 2, 2, 64)
        sel_s[0, 4 * c:4 * c + 4] = r['o_sel_s'].reshape(4, 8, 2, 2, 64)
        win_s[0, 4 * c:4 * c + 4] = r['o_win_s'].reshape(4, 512, 2, 2, 64)
        sh_s[0, 4 * c:4 * c + 4] = r['o_shift_s']
    return (y_p, y_s, cmp_p, cmp_s, sel_p, sel_s, win_p, win_s, wkv_p, wkv_s, sh_p, sh_s)
```
